# Optimizing a Trainium2 kernel written in Bass

```python
import jax, jax.numpy as jnp
from jax import lax
import numpy as np

D_MODEL = 1024
BATCH = 8
SEQ = 2048
DEPTH = 1
DEC_BATCH = 128
DEC_SEQ = 4
PAST_LEN = 16384
PAGE_SIZE = 128

GM_WIDTH = D_MODEL
GM_GROUPS = 4
GM_CHUNK = 128
ML_HEADS = 4
ML_HEAD_DIM = D_MODEL // ML_HEADS
ML_WIDTH = ML_HEADS * ML_HEAD_DIM
ML_CHUNK = 64
CONV_W = 4
D_FF = -(-8 * D_MODEL // (3 * 256)) * 256
EPS = 1e-6
IN_SPLITS = (GM_WIDTH, GM_WIDTH, ML_WIDTH, ML_WIDTH, ML_WIDTH, ML_WIDTH, ML_HEADS, ML_HEADS, D_MODEL, D_MODEL)
IN_COLS = sum(IN_SPLITS)

kernel_name = "gated_gmlp_mlstm_hybrid_step"


def rmsnorm(x, g):
    xf = x.astype(jnp.float32)
    y = xf * lax.rsqrt(jnp.mean(xf * xf, axis=-1, keepdims=True) + EPS)
    return y.astype(x.dtype) * g


def layernorm(x, g, b):
    xf = x.astype(jnp.float32)
    mu = jnp.mean(xf, axis=-1, keepdims=True)
    var = jnp.mean(jnp.square(xf - mu), axis=-1, keepdims=True)
    return ((xf - mu) * lax.rsqrt(var + EPS)).astype(x.dtype) * g + b


def head_norm(h, g):
    mu = jnp.mean(h, axis=-1, keepdims=True)
    var = jnp.mean(jnp.square(h - mu), axis=-1, keepdims=True)
    return (h - mu) * lax.rsqrt(var + EPS) * g.reshape(ML_HEADS, ML_HEAD_DIM).astype(jnp.float32)


def gmlp_spatial(u, v, w_s, b_s, L):
    B, T, W = v.shape
    n_chunks = T // L
    ws = jnp.tril(w_s[:, :L, :L])
    vb = v.reshape(B, n_chunks, L, GM_GROUPS, W // GM_GROUPS)
    s = jnp.einsum('gts,bnsgc->bntgc', ws, vb) + b_s[:, :L].T[None, None, :, :, None]
    return u * s.reshape(B, T, W)


def causal_conv(xp, w, b, T):
    out = b
    for j in range(CONV_W):
        out = out + w[j] * xp[:, j:j + T]
    return out


def mlstm_chunkwise(q, k, v, i_pre, logf, C0, n0, m0, L):
    B, T, H, DH = q.shape
    N = T // L
    to_chunks = lambda a: a.reshape(B, N, L, H, DH).transpose(1, 0, 3, 2, 4)
    gate_chunks = lambda a: a.reshape(B, N, L, H).transpose(1, 0, 3, 2)
    mask = jnp.tril(jnp.ones((L, L), dtype=bool))

    def step(carry, inp):
        C, n, m = carry
        qc, kc, vc, ic, fc = inp
        bcum = jnp.cumsum(fc, axis=-1)
        dmat = bcum[..., :, None] - bcum[..., None, :] + ic[..., None, :]
        dmat = jnp.where(mask, dmat, -jnp.inf)
        inter = bcum + m[..., None]
        m_t = jnp.maximum(inter, jnp.max(dmat, axis=-1))
        w_inter = jnp.exp(inter - m_t)
        s = jnp.exp(dmat - m_t[..., None]) * jnp.einsum('bhtd,bhsd->bhts', qc, kc)
        num = w_inter[..., None] * jnp.einsum('bhvd,bhtd->bhtv', C, qc) + jnp.einsum('bhts,bhsv->bhtv', s, vc)
        den = w_inter * jnp.einsum('bhd,bhtd->bht', n, qc) + jnp.sum(s, axis=-1)
        h = num / jnp.maximum(jnp.abs(den), jnp.exp(-m_t))[..., None]
        b_last = bcum[..., -1]
        tail = b_last[..., None] - bcum + ic
        m_new = jnp.maximum(b_last + m, jnp.max(tail, axis=-1))
        decay = jnp.exp(b_last + m - m_new)
        w_in = jnp.exp(tail - m_new[..., None])
        C_new = decay[..., None, None] * C + jnp.einsum('bhs,bhsv,bhsd->bhvd', w_in, vc, kc)
        n_new = decay[..., None] * n + jnp.einsum('bhs,bhsd->bhd', w_in, kc)
        return (C_new, n_new, m_new), h

    (C, n, m), h = lax.scan(step, (C0, n0, m0),
                            (to_chunks(q), to_chunks(k), to_chunks(v), gate_chunks(i_pre), gate_chunks(logf)))
    h = h.transpose(1, 0, 3, 2, 4).reshape(B, T, H, DH)
    return h, C, n, m


def hybrid_layer(x, conv_state, C0, n0, m0, gm_chunk, ml_chunk,
                 g_norm1, w_in, b_i, b_f, ln_g, ln_b, w_s, b_s, conv_w, conv_b, hn_g,
                 b_gate, w_proj_a, w_proj_b, w_out, g_norm2, w_ffn_in, w_ffn_out):
    B, T, _ = x.shape
    h = rmsnorm(x, g_norm1)
    z = h @ w_in
    bounds = np.cumsum(IN_SPLITS)[:-1].tolist()
    u_gm, v_gm, q_raw, k_raw, v_ml, o_ml, i_raw, f_raw, ga, gb = jnp.split(z, bounds, axis=-1)

    u = jax.nn.gelu(u_gm, approximate=False)
    v_n = layernorm(jax.nn.gelu(v_gm, approximate=False), ln_g, ln_b)
    a_out = gmlp_spatial(u, v_n, w_s, b_s, gm_chunk)

    qk_raw = jnp.concatenate([q_raw, k_raw], axis=-1)
    qk_pad = jnp.concatenate([conv_state.astype(qk_raw.dtype), qk_raw], axis=1)
    conv_new = qk_pad[:, -(CONV_W - 1):]
    qk = jax.nn.silu(causal_conv(qk_pad, conv_w, conv_b, T))
    q = qk[..., :ML_WIDTH].reshape(B, T, ML_HEADS, ML_HEAD_DIM).astype(jnp.float32)
    k = (qk[..., ML_WIDTH:].reshape(B, T, ML_HEADS, ML_HEAD_DIM).astype(jnp.float32)
         * (ML_HEAD_DIM ** -0.5))
    vm = v_ml.reshape(B, T, ML_HEADS, ML_HEAD_DIM).astype(jnp.float32)
    i_pre = (i_raw + b_i).astype(jnp.float32)
    logf = jax.nn.log_sigmoid((f_raw + b_f).astype(jnp.float32))
    hm, C, n, m = mlstm_chunkwise(q, k, vm, i_pre, logf,
                                  C0.astype(jnp.float32), n0.astype(jnp.float32), m0.astype(jnp.float32),
                                  ml_chunk)
    hm = head_norm(hm, hn_g).reshape(B, T, ML_WIDTH).astype(x.dtype)
    b_out = jax.nn.sigmoid(o_ml) * hm

    merged = (jax.nn.sigmoid(ga + b_gate[0]) * (a_out @ w_proj_a)
              + jax.nn.sigmoid(gb + b_gate[1]) * (b_out @ w_proj_b))
    x = x + merged @ w_out

    h2 = rmsnorm(x, g_norm2)
    gt, up = jnp.split(h2 @ w_ffn_in, 2, axis=-1)
    x = x + (jax.nn.silu(gt) * up) @ w_ffn_out
    return x, v_n, conv_new, C, n, m


def setup_inputs(seed: int = 0) -> dict:
    key = jax.random.key(seed)
    ks = jax.random.split(key, 32)
    nrm = lambda k, shape, s: jax.random.normal(k, shape, jnp.float32) * s
    f32 = jnp.float32
    inputs = {
        "x_prompt": nrm(ks[0], (BATCH, SEQ, D_MODEL), 1.0),
        "x_sample": nrm(ks[1], (DEC_BATCH, DEC_SEQ, D_MODEL), 1.0),
        "state_conv": nrm(ks[2], (DEPTH, DEC_BATCH, CONV_W - 1, 2 * ML_WIDTH), 1.0),
        "state_C": nrm(ks[3], (DEPTH, DEC_BATCH, ML_HEADS, ML_HEAD_DIM, ML_HEAD_DIM), ML_HEAD_DIM ** -0.5),
        "state_n": nrm(ks[4], (DEPTH, DEC_BATCH, ML_HEADS, ML_HEAD_DIM), 0.5),
        "state_m": nrm(ks[5], (DEPTH, DEC_BATCH, ML_HEADS), 1.0),
        "g_norm1": 1.0 + nrm(ks[6], (DEPTH, D_MODEL), 0.02),
        "w_in": nrm(ks[7], (DEPTH, D_MODEL, IN_COLS), D_MODEL ** -0.5),
        "b_i": nrm(ks[8], (DEPTH, ML_HEADS), 0.1),
        "b_f": jnp.linspace(3.0, 6.0, ML_HEADS, dtype=f32)[None, :] + nrm(ks[9], (DEPTH, ML_HEADS), 0.1),
        "ln_g": 1.0 + nrm(ks[10], (DEPTH, GM_WIDTH), 0.02),
        "ln_b": nrm(ks[11], (DEPTH, GM_WIDTH), 0.02),
        "w_s": nrm(ks[12], (DEPTH, GM_GROUPS, GM_CHUNK, GM_CHUNK), GM_CHUNK ** -0.5),
        "b_s": 1.0 + nrm(ks[13], (DEPTH, GM_GROUPS, GM_CHUNK), 0.02),
        "conv_w": nrm(ks[14], (DEPTH, CONV_W, 2 * ML_WIDTH), CONV_W ** -0.5),
        "conv_b": nrm(ks[15], (DEPTH, 2 * ML_WIDTH), 0.02),
        "hn_g": 1.0 + nrm(ks[16], (DEPTH, ML_WIDTH), 0.02),
        "b_gate": nrm(ks[17], (DEPTH, 2, D_MODEL), 0.02),
        "w_proj_a": nrm(ks[18], (DEPTH, GM_WIDTH, D_MODEL), GM_WIDTH ** -0.5),
        "w_proj_b": nrm(ks[19], (DEPTH, ML_WIDTH, D_MODEL), ML_WIDTH ** -0.5),
        "w_out": nrm(ks[20], (DEPTH, D_MODEL, D_MODEL), D_MODEL ** -0.5),
        "g_norm2": 1.0 + nrm(ks[21], (DEPTH, D_MODEL), 0.02),
        "w_ffn_in": nrm(ks[22], (DEPTH, D_MODEL, 2 * D_FF), D_MODEL ** -0.5),
        "w_ffn_out": nrm(ks[23], (DEPTH, D_FF, D_MODEL), D_FF ** -0.5),
        "g_final": 1.0 + nrm(ks[24], (D_MODEL,), 0.02),
    }
    return inputs


def reference(x_prompt, x_sample, state_conv, state_C, state_n, state_m,
              g_norm1, w_in, b_i, b_f, ln_g, ln_b, w_s, b_s, conv_w, conv_b, hn_g,
              b_gate, w_proj_a, w_proj_b, w_out, g_norm2, w_ffn_in, w_ffn_out, g_final):
    Bp, Tp, _ = x_prompt.shape
    Ts = x_sample.shape[1]
    xp, xs = x_prompt, x_sample
    conv_p, C_p, n_p, m_p = [], [], [], []
    conv_s, C_s, n_s, m_s, v_s = [], [], [], [], []
    for l in range(DEPTH):
        w = (g_norm1[l], w_in[l], b_i[l], b_f[l], ln_g[l], ln_b[l], w_s[l], b_s[l], conv_w[l], conv_b[l],
             hn_g[l], b_gate[l], w_proj_a[l], w_proj_b[l], w_out[l], g_norm2[l], w_ffn_in[l], w_ffn_out[l])
        zc = jnp.zeros((Bp, CONV_W - 1, 2 * ML_WIDTH), xp.dtype)
        zC = jnp.zeros((Bp, ML_HEADS, ML_HEAD_DIM, ML_HEAD_DIM), jnp.float32)
        zn = jnp.zeros((Bp, ML_HEADS, ML_HEAD_DIM), jnp.float32)
        zm = jnp.zeros((Bp, ML_HEADS), jnp.float32)
        xp, _, cp, Cp, np_, mp = hybrid_layer(xp, zc, zC, zn, zm, GM_CHUNK, min(ML_CHUNK, Tp), *w)
        xs, vgs, cs, Cs, ns, ms = hybrid_layer(xs, state_conv[l], state_C[l], state_n[l], state_m[l],
                                               Ts, Ts, *w)
        conv_p.append(cp); C_p.append(Cp.astype(state_C.dtype)); n_p.append(np_.astype(state_n.dtype))
        m_p.append(mp.astype(state_m.dtype))
        conv_s.append(cs); C_s.append(Cs.astype(state_C.dtype)); n_s.append(ns.astype(state_n.dtype))
        m_s.append(ms.astype(state_m.dtype)); v_s.append(vgs)
    y_prompt = rmsnorm(xp, g_final)
    y_sample = rmsnorm(xs, g_final)
    return (y_prompt, y_sample,
            jnp.stack(conv_p), jnp.stack(C_p), jnp.stack(n_p), jnp.stack(m_p),
            jnp.stack(conv_s), jnp.stack(C_s), jnp.stack(n_s), jnp.stack(m_s), jnp.stack(v_s))
```

```python
import math
import numpy as np
from contextlib import ExitStack
import concourse.bass as bass
import concourse.mybir as mybir
from concourse.bass_utils import run_bass_kernel_spmd

F32 = mybir.dt.float32
BF16 = mybir.dt.bfloat16
AF = mybir.ActivationFunctionType
ALU = mybir.AluOpType
AX = mybir.AxisListType

SAME_ENG_SYNC = True
EPS = 1e-6
LN16 = math.log(16.0)
NEG = -1.0e30
NCORES = 8


class Buf:
    __slots__ = ("name", "w", "r", "excl", "dsem")

    def __init__(self, name, excl=False):
        self.name = name
        self.w = None
        self.r = {}
        self.excl = excl
        self.dsem = None


class KB:
    ENGS = ("pe", "act", "dve", "pool", "sp")

    def __init__(self, nc):
        self.nc = nc
        self.prog = {e: [] for e in self.ENGS}
        self.cnt = {e: 0 for e in self.ENGS}
        self.dcnt = {}
        self.waited = {e: {} for e in self.ENGS}
        self.stack = ExitStack()
        self.nbuf = 0
        self.sb_bytes = 0

    def sb(self, name, shape, dt):
        n = 1
        for s in shape[1:]:
            n *= s
        self.sb_bytes += n * (2 if dt == BF16 else 4)
        return self.stack.enter_context(self.nc.sbuf_tensor("sb_" + name, list(shape), dt))

    def ps(self, name, shape, dt):
        return self.stack.enter_context(self.nc.psum_tensor("ps_" + name, list(shape), dt))

    def buf(self, name, excl=False):
        self.nbuf += 1
        return Buf(f"{name}_{self.nbuf}", excl)

    def bufs(self, name, n):
        return [self.buf(f"{name}{i}") for i in range(n)]

    def _deps(self, eng, reads, writes, dma_sem=None):
        deps = {}
        own = "E:" + eng

        def add(d, excl):
            if d is None:
                return
            kk, v = d
            if kk == own and (excl or eng == "pe" or not SAME_ENG_SYNC):
                return
            if deps.get(kk, 0) < v:
                deps[kk] = v

        for b in reads:
            add(b.w, b.excl)
            if b.excl:
                for kv in b.r.items():
                    add(kv, True)
        for b in writes:
            if not (dma_sem is not None and b.w is not None and b.w[0] == dma_sem):
                add(b.w, b.excl)
            for kv in b.r.items():
                add(kv, b.excl)
        out = []
        for kk, v in deps.items():
            if self.waited[eng].get(kk, 0) >= v:
                continue
            self.waited[eng][kk] = v
            out.append((kk, v))
        return out

    def _mark(self, key, val, reads, writes):
        for b in reads:
            if b.excl:
                b.w = (key, val)
                b.r = {}
            elif b.r.get(key, 0) < val:
                b.r[key] = val
        for b in writes:
            b.w = (key, val)
            b.r = {}

    def op(self, eng, fn, reads=(), writes=()):
        reads = [b for b in reads if b is not None]
        writes = [b for b in writes if b is not None]
        waits = self._deps(eng, reads, writes)
        self.cnt[eng] += 1
        key = "E:" + eng
        self.prog[eng].append((waits, fn, key, 1))
        self._mark(key, self.cnt[eng], reads, writes)

    def dma(self, eng, out, in_, reads=(), writes=(), sem=None):
        reads = [b for b in reads if b is not None]
        writes = [b for b in writes if b is not None]
        if sem.dsem is None:
            sem.dsem = "D:" + sem.name
            self.dcnt[sem.dsem] = 0
        key = sem.dsem
        waits = self._deps(eng, reads, writes, dma_sem=key)
        self.dcnt[key] += 16
        fn = (lambda e, out=out, in_=in_: e.dma_start(out=out, in_=in_))
        self.prog[eng].append((waits, fn, key, 16))
        self._mark(key, self.dcnt[key], reads, writes)

    def final_wait(self, eng, bufs):
        waits = self._deps(eng, [], bufs)
        self.prog[eng].append((waits, None, None, 0))

    def emit(self):
        nc = self.nc
        keys = ["E:" + e for e in self.ENGS if e != "sp"] + list(self.dcnt.keys())
        sems = {}
        for i, kk in enumerate(keys):
            sems[kk] = self.stack.enter_context(nc.semaphore(f"s{i}"))
        prog = self.prog

        def run(e, lst):
            for waits, fn, key, inc in lst:
                for kk, v in waits:
                    e.wait_ge(sems[kk], v)
                if fn is None:
                    continue
                fn(e).then_inc(sems[key], inc)

        with nc.Block() as block:
            @block.tensor
            def _(e):
                run(e, prog["pe"])

            @block.scalar
            def _(e):
                run(e, prog["act"])

            @block.vector
            def _(e):
                run(e, prog["dve"])

            @block.gpsimd
            def _(e):
                run(e, prog["pool"])

            @block.sync
            def _(e):
                run(e, prog["sp"])
        self.stack.close()


def build_program(n_prompt_blocks=4, do_sample=True, debug=False):
    nc = bass.Bass("TRN2", target_bir_lowering=False)

    def din(name, shape, dt=F32):
        return nc.dram_tensor(name, list(shape), dt, kind="ExternalInput").ap()

    def dout(name, shape, dt=F32):
        return nc.dram_tensor(name, list(shape), dt, kind="ExternalOutput").ap()

    D = {}
    D["xp"] = din("xp", [2048, 1024]); D["xs"] = din("xs", [64, 1024])
    D["sconv"] = din("sconv", [48, 2048]); D["sC"] = din("sC", [16, 4, 256, 256])
    D["sn"] = din("sn", [16, 1024]); D["smr"] = din("smr", [64, 4])
    for nm in ("g1", "g2", "gf", "lng", "lnb"):
        D[nm] = din(nm, [1024])
    D["w_in"] = din("w_in", [16, 128, 8, 512]); D["wif"] = din("wif", [128, 8, 8]); D["bif"] = din("bif", [8])
    D["wsTP"] = din("wsTP", [128, 4, 128]); D["wsTS"] = din("wsTS", [64, 4, 64])
    D["bsrP"] = din("bsrP", [512]); D["bsrS"] = din("bsrS", [256])
    D["convw"] = din("convw", [128, 16, 4]); D["convb"] = din("convb", [128, 16])
    D["hng"] = din("hng", [128, 8]); D["bgate"] = din("bgate", [128, 16])
    D["wpa"] = din("wpa", [2, 128, 8, 512]); D["wpb"] = din("wpb", [2, 128, 8, 512]); D["wo"] = din("wo", [2, 128, 8, 512])
    D["wfi"] = din("wfi", [11, 128, 8, 512]); D["wfo"] = din("wfo", [2, 128, 22, 512])
    O = {}
    O["yp"] = dout("yp", [2048, 1024]); O["ys"] = dout("ys", [64, 1024]); O["convp"] = dout("convp", [3, 2048])
    O["Cp"] = dout("Cp", [4, 256, 256]); O["np"] = dout("np", [1, 1024]); O["mp"] = dout("mp", [1, 4])
    O["convs"] = dout("convs", [16, 3, 2048]); O["Cs"] = dout("Cs", [16, 4, 256, 256]); O["ns"] = dout("ns", [16, 1024])
    O["ms"] = dout("ms", [16, 4]); O["vs"] = dout("vs", [64, 1024])
    scr = nc.dram_tensor("qkscr", [64, 2048], F32, kind="Internal").ap()
    wscr = nc.dram_tensor("wscr", [39, 128, 4096], BF16, kind="Internal").ap()
    if debug:
        for nm in ("dbg_ao", "dbg_bo", "dbg_mg"):
            O[nm] = dout(nm, [128, 8, 512], BF16)
        O["dbg_qk"] = dout("dbg_qk", [128, 16, 512], BF16)
        O["dbg_vn"] = dout("dbg_vn", [4, 128, 1024], BF16)
        O["dbg_gv"] = dout("dbg_gv", [4, 128, 1024], F32)
        O["dbg_hst"] = dout("dbg_hst", [4, 128, 16], F32)

    k = KB(nc)
    op = k.op

    def mm(out, lhsT, rhs, start, stop, R, W):
        op("pe", lambda e: e.matmul(out=out, lhsT=lhsT, rhs=rhs, start=start, stop=stop), R, W)

    def tr(out, in_, ident, R, W):
        op("pe", lambda e: e.transpose(out=out, in_=in_, identity=ident), R, W)

    def act(out, in_, func, R, W, bias=0.0, scale=1.0, accum=None, eng="act"):
        if accum is None:
            op(eng, lambda e: e.activation(out=out, in_=in_, func=func, bias=bias, scale=scale), R, W)
        else:
            op(eng, lambda e: e.activation(out=out, in_=in_, func=func, bias=bias, scale=scale, accum_out=accum), R, W)

    def tt(eng, out, in0, in1, o, R, W):
        op(eng, lambda e: e.tensor_tensor(out=out, in0=in0, in1=in1, op=o), R, W)

    def ts(eng, out, in0, s1, s2, op0, op1, R, W):
        if s2 is None:
            op(eng, lambda e: e.tensor_scalar(out=out, in0=in0, scalar1=s1, scalar2=None, op0=op0), R, W)
        else:
            op(eng, lambda e: e.tensor_scalar(out=out, in0=in0, scalar1=s1, scalar2=s2, op0=op0, op1=op1), R, W)

    def stt(out, in0, scalar, in1, op0, op1, R, W):
        op("dve", lambda e: e.scalar_tensor_tensor(out=out, in0=in0, scalar=scalar, in1=in1, op0=op0, op1=op1), R, W)

    def cp(eng, out, in_, R, W):
        if eng == "act":
            op("act", lambda e: e.activation(out=out, in_=in_, func=AF.Copy), R, W)
        else:
            op(eng, lambda e: e.tensor_copy(out=out, in_=in_), R, W)

    def memset(eng, ap, val, W):
        op(eng, lambda e: e.memset(ap, val), [], W)

    def asel(out, in_, pattern, cmp, fill, base, chm, R, W):
        op("pool", lambda e: e.affine_select(out=out, in_=in_, pattern=pattern, compare_op=cmp, fill=fill,
                                             base=base, channel_multiplier=chm), R, W)

    pj = [k.ps(f"pj{i}", [128, 512], F32) for i in range(4)]
    B_pj = [k.buf(f"pj{i}", excl=True) for i in range(4)]
    ptb = k.ps("ptb", [128, 8, 128], BF16); B_ptb = k.buf("ptb", excl=True)
    pg = k.ps("pg", [128, 512], F32); B_pg = k.buf("pg", excl=True)
    pnum = k.ps("pnum", [128, 512], F32); B_pnum = k.buf("pnum", excl=True)
    pc = k.ps("pc", [128, 512], F32); B_pc = k.buf("pc", excl=True)

    identf = k.sb("identf", [128, 128], F32); B_idf = k.buf("identf")
    identb = k.sb("identb", [128, 128], BF16); B_idb = k.buf("identb")
    onesf = k.sb("onesf", [128, 128], F32); B_ones = k.buf("onesf")
    bc_t = {}
    B_bc = {}
    for nm in ("g1", "g2", "gf", "lng", "lnb"):
        bc_t[nm] = k.sb("bc_" + nm, [128, 1024], F32)
        B_bc[nm] = k.buf("bc_" + nm)
        k.dma("sp", bc_t[nm][:], D[nm].partition_broadcast(128), writes=[B_bc[nm]], sem=B_bc[nm])
    B_sm = k.buf("smallconst")
    convw = k.sb("convw", [128, 16, 4], F32); convb = k.sb("convb", [128, 16], F32)
    hng = k.sb("hng", [128, 8], F32); bgate = k.sb("bgate", [128, 16], F32)
    bif = k.sb("bif", [128, 8], F32); wif = k.sb("wif", [128, 8, 8], BF16)
    bsf = {"P": k.sb("bsfP", [2, 512], F32), "S": k.sb("bsfS", [2, 256], F32)}
    bs2 = {"P": k.sb("bs2P", [2, 512], BF16), "S": k.sb("bs2S", [2, 256], BF16)}
    ones2 = k.sb("ones2", [2, 128], BF16)
    k.dma("sp", convw[:], D["convw"], writes=[B_sm], sem=B_sm)
    k.dma("sp", convb[:], D["convb"], writes=[B_sm], sem=B_sm)
    k.dma("sp", hng[:], D["hng"], writes=[B_sm], sem=B_sm)
    k.dma("sp", bgate[:], D["bgate"], writes=[B_sm], sem=B_sm)
    k.dma("sp", bif[:], D["bif"].partition_broadcast(128), writes=[B_sm], sem=B_sm)
    k.dma("sp", bsf["P"][:], D["bsrP"].partition_broadcast(2), writes=[B_sm], sem=B_sm)
    k.dma("sp", bsf["S"][:], D["bsrS"].partition_broadcast(2), writes=[B_sm], sem=B_sm)
    B_wif = k.buf("wif")
    k.dma("pool", wif[:], D["wif"], writes=[B_wif], sem=B_wif)

    memset("pool", identf[:], 1.0, [B_idf])
    asel(identf[:], identf[:], [[-1, 128]], ALU.is_equal, 0.0, 0, 1, [B_idf], [B_idf])
    cp("dve", identb[:], identf[:], [B_idf], [B_idb])
    memset("pool", onesf[:], 1.0, [B_ones])

    xt = k.sb("xt", [128, 4, 1024], F32); B_xt = k.bufs("xt", 4)
    hT = k.sb("hT", [128, 8, 512], BF16); B_hT = k.bufs("hT", 4)
    xn = k.sb("xn", [128, 1024], BF16); B_xn = k.buf("xn")
    junk = xn; B_junk = B_xn
    st4 = k.sb("st4", [128, 8], F32); B_st4 = k.buf("st4")
    rst = k.sb("rst", [128, 12], F32)
    U8 = k.sb("U8", [128, 8, 512], BF16); B_U8 = k.bufs("U8", 8)
    aoT = k.sb("aoT", [128, 8, 512], BF16); B_ao = k.bufs("aoT", 4)
    boT = k.sb("boT", [128, 8, 512], BF16); B_bo = k.bufs("boT", 4)
    sgab = k.sb("sgab", [128, 16, 512], BF16)
    sga = sgab[:, 0:8, :]; B_sga = k.bufs("sga", 8)
    sgb = sgab[:, 8:16, :]; B_sgb = k.bufs("sgb", 8)
    xpre = sgab[:, :, :].rearrange("p a b -> p (a b)").bitcast(F32).rearrange("p (i c) -> p i c", i=4)
    B_sgab = B_sga + B_sgb

    def Bxpre(i):
        return B_sgab[4 * i:4 * i + 4]
    arena = k.sb("arena", [128, 11264], BF16); B_ar = k.bufs("arena", 22)
    gv = arena[:, 0:8192].bitcast(F32).rearrange("p (i c) -> p i c", i=4)
    qkT = arena[:, 0:8192].rearrange("p (c t) -> p c t", c=16)
    gT = arena[:, :].rearrange("p (c t) -> p c t", c=22)
    memset("pool", ones2[:], 1.0, [B_ones])
    B_bs = k.buf("bs2")
    bsh = arena[0:2, 0:512]
    bsg = arena[0:2, 1024:2048].bitcast(F32)
    bsr_ = arena[0:2, 2048:3072].bitcast(F32)
    for G_ in ("P", "S"):
        n_ = 512 if G_ == "P" else 256
        cp("dve", bsh[:, 0:n_], bsf[G_][:, :], [B_sm], [B_bs] + B_ar[0:6])
        cp("dve", bsg[:, 0:n_], bsh[:, 0:n_], [B_bs] + B_ar[0:6], [B_bs] + B_ar[0:6])
        tt("dve", bsr_[:, 0:n_], bsf[G_][:, :], bsg[:, 0:n_], ALU.subtract, [B_sm, B_bs], [B_bs] + B_ar[0:6])
        ts("dve", bsr_[:, 0:n_], bsr_[:, 0:n_], identf[0:2, 1:2], None, ALU.mult, None, [B_bs, B_idf], [B_bs] + B_ar[0:6])
        stt(bs2[G_][:, :], bsg[:, 0:n_], identf[0:2, 0:1], bsr_[:, 0:n_], ALU.mult, ALU.add, [B_bs, B_idf] + B_ar[0:6], [B_bs])
    vnf = k.sb("vnf", [128, 1024], F32); B_vnf = k.buf("vnf")
    vnb = [k.sb(f"vnb{i}", [128, 1024], BF16) for i in range(2)]; B_vnb = k.bufs("vnb", 2)
    raw = [k.sb(f"raw{i}", [128, 520], F32) for i in range(2)]; B_raw = k.bufs("raw", 2)
    cacc = [k.sb(f"cacc{i}", [128, 512], F32) for i in range(2)]; B_cacc = k.bufs("cacc", 2)
    halo = k.sb("halo", [128, 16, 3], F32); B_halo = k.bufs("halo", 16)
    scst = vnf; B_scst = B_vnf
    vaug = k.sb("vaug", [128, 4, 4, 257], BF16); B_vaug = k.bufs("vaug", 4)
    tmpf = [k.sb(f"tmpf{i}", [128, 512], F32) for i in range(2)]; B_tmpf = k.bufs("tmpf", 2)
    tmpg = cacc; B_tmpg = B_cacc
    ktm = k.sb("ktm", [128, 1024], BF16); B_ktm = k.buf("ktm")
    hmb = k.sb("hmb", [128, 1024], BF16); B_hmb = k.buf("hmb")
    slabs = [k.sb(f"slab{i}", [128, 8, 512], BF16) for i in range(3)]; B_slab = k.bufs("slab", 3)
    gi = k.sb("gi", [128, 8], F32); lf = k.sb("lf", [128, 16], F32)
    gb8 = k.sb("gb8", [128, 8], F32); gl8 = k.sb("gl8", [128, 8], F32)
    aa = k.sb("aa", [128, 8], F32)
    Dx = k.sb("Dx", [128, 16], F32)
    mnd = k.sb("mnd", [128, 8], F32)
    B_g = k.buf("gates")
    mprev = {"P": k.sb("mprevP", [128, 4], F32), "S": k.sb("mprevS", [64, 4], F32)}
    B_mprev = {"P": k.buf("mprevP"), "S": k.buf("mprevS")}
    diag = k.sb("diag", [128, 4, 128], F32); B_diag = k.buf("diag")
    DTb = k.sb("DTb", [128, 4, 4, 128], BF16); B_DT = k.bufs("DT", 4)
    decx = k.sb("decx", [128, 16, 4], F32); B_decx = k.buf("decx")
    decb_t = k.sb("decb_t", [128, 4, 64], F32); B_dec = k.bufs("dec", 4)
    wsel_t = k.sb("wsel_t", [128, 4, 4, 16], BF16); B_wsel = k.bufs("wsel", 4)
    Ex_t = k.sb("Ex_t", [128, 4, 16], F32); B_Ex = k.bufs("Ex", 4)
    cm8_t = k.sb("cm8_t", [16, 4, 8], F32); B_cm = k.bufs("cm", 4)
    hs2 = [k.sb(f"hs{i}", [128, 48], F32) for i in range(2)]; B_hs2 = k.bufs("hs", 2)
    nTf_t = k.sb("nTf", [128, 8, 16], F32); B_nTf = k.buf("nTf")
    qTs = k.sb("qTs", [128, 8, 128], BF16); B_qTs = k.buf("qTs")
    qTm = [k.sb(f"qTm{i}", [128, 16, 64], BF16) for i in range(2)]; B_qTm = k.bufs("qTm", 2)
    qTm_diag = []
    for i in range(2):
        a0 = qTm[i][:, :, :]
        qTm_diag.append(bass.AP(a0.tensor, a0.offset, [list(a0.ap[0]), [68, 16], [1, 4]]))
        op("pool", (lambda e, i=i: e.memset(qTm[i][:, :, :], 0.0)), [], [B_qTm[i]])
    STb = [k.sb(f"STb{i}", [128, 128], BF16) for i in range(2)]; B_ST = k.bufs("ST", 2)
    vsb = [k.sb(f"vsb{i}", [128, 257], BF16) for i in range(2)]; B_vs = k.bufs("vs", 2)
    vsm = [k.sb(f"vsm{i}", [64, 256], BF16) for i in range(2)]; B_vsm = k.bufs("vsm", 2)
    hst = k.sb("hst", [128, 16], F32); B_hst = k.buf("hst")
    CTb = [k.sb(f"CTb{i}", [128, 2, 257], BF16) for i in range(2)]; B_CTb = k.bufs("CTb", 2)
    Cst = k.sb("Cst", [128, 5, 2, 256], F32); B_Cst = k.bufs("Cst", 5)

    for h in range(4):
        memset("pool", Cst[:, h, :, :], 0.0, [B_Cst[h]])
    memset("pool", nTf_t[:, :, :], 0.0, [B_nTf])
    memset("pool", mprev["P"][:], 0.0, [B_mprev["P"]])
    memset("pool", vaug[:, :, :, 256:257], 1.0, B_vaug)
    memset("pool", halo[:], 0.0, B_halo)

    class WS:
        def __init__(self):
            self.q = []
            self.issued = 0
            self.occ = {}
            self.slots = []
            self.rcount = 0

        def plan(self, lst):
            self.q.extend(lst)

        def slot_of(self, i):
            while len(self.slots) <= i:
                j = len(self.slots) % 39
                if j in (18, 19):
                    self.slots.append(3 + (j - 18))
                else:
                    self.slots.append(self.rcount % 3)
                    self.rcount += 1
            return self.slots[i]

        def get(self, idx, hold=None):
            hold = idx if hold is None else hold
            while self.issued < len(self.q) and self.issued <= hold + 4:
                i = self.issued
                s = self.slot_of(i)
                prev = self.occ.get(s)
                if prev is not None and prev >= hold:
                    break
                if s >= 3 and not (hold // 39 == i // 39 and hold % 39 >= 16):
                    break
                ap, nk = self.q[i]
                j = i % 39
                tile_, Bs, extra = slot_tiles[s]
                if i < 39:
                    k.dma("pool", tile_[:, 0:nk, :], ap, writes=[Bs] + extra, sem=Bs)
                    k.dma("sp", wscr[j, :, 0:nk * 512], tile_[:, 0:nk, :].rearrange("p k c -> p (k c)"),
                          reads=[Bs] + extra, writes=[B_wscr[j]], sem=B_wscr[j])
                else:
                    k.dma("pool", tile_[:, 0:nk, :], wscr[j, :, 0:nk * 512].rearrange("p (k c) -> p k c", k=nk),
                          reads=[B_wscr[j]], writes=[Bs] + extra, sem=Bs)
                self.occ[s] = i
                self.issued += 1
            s = self.slot_of(idx)
            assert self.occ.get(s) == idx, (idx, s, self.occ)
            return slot_tiles[s][0], slot_tiles[s][1]

    slot_tiles = [(slabs[i_], B_slab[i_], []) for i_ in range(3)]
    slot_tiles.append((arena[:, 0:4096].rearrange("p (k c) -> p k c", k=8), k.buf("slotA"), B_ar[0:8]))
    slot_tiles.append((arena[:, 4096:8192].rearrange("p (k c) -> p k c", k=8), k.buf("slotB"), B_ar[8:16]))
    B_wscr = k.bufs("wscr", 39)
    ws = WS()
    wbase = [0]

    def block_slabs():
        lst = [(D["w_in"][s], 8) for s in range(16)]
        for s in range(2):
            lst += [(D["wpa"][s], 8), (D["wpb"][s], 8)]
        lst += [(D["wo"][s], 8) for s in range(2)]
        lst += [(D["wfi"][s], 8) for s in range(11)]
        for g in range(2):
            for q0 in (0, 8, 16):
                nk = min(8, 22 - q0)
                lst.append((D["wfo"][g, :, q0:q0 + nk, :], nk))
        return lst

    blocks = [("P", b_) for b_ in range(n_prompt_blocks)] + ([("S", 0)] if do_sample else [])
    for _ in blocks:
        ws.plan(block_slabs())
    ws.get(0)

    LG = {"P": 128, "S": 64}
    NBG = {"P": 1, "S": 16}
    MK = {}
    B_mk = k.buf("masks")
    am = k.sb("am", [128, 4, 128], F32); B_am = k.buf("am")
    wsst = am; B_wsst = B_am
    for G in ("P", "S"):
        L = LG[G]
        m = {}
        for nm in ("cmask", "nmT", "nm", "sell"):
            m[nm] = k.sb(f"{nm}{G}", [L, L], F32)
        m["blk"] = k.sb(f"blk{G}", [L, 16], F32)
        m["selc"] = k.sb(f"selc{G}", [L, 16], F32)
        m["wsT"] = k.sb(f"wsT{G}", [L, 4, L], BF16)
        MK[G] = m
        memset("pool", m["cmask"][:], 1.0, [B_mk]); memset("pool", m["nmT"][:], 0.0, [B_mk])
        memset("pool", m["nm"][:], 0.0, [B_mk]); memset("pool", m["sell"][:], 1.0, [B_mk])
        memset("pool", m["blk"][:], 1.0, [B_mk]); memset("pool", m["selc"][:], 1.0, [B_mk])
        k.dma("sp", wsst[0:L, :, 0:L], D["wsT" + G], writes=[B_wsst], sem=B_wsst)
        if G == "P":
            asel(m["cmask"][:], m["cmask"][:], [[1, 128]], ALU.is_ge, 0.0, 0, -1, [B_mk], [B_mk])
            asel(m["nmT"][:], m["nmT"][:], [[1, 128]], ALU.is_ge, NEG, 0, -1, [B_mk], [B_mk])
            asel(m["nm"][:], m["nm"][:], [[-1, 128]], ALU.is_ge, NEG, 0, 1, [B_mk], [B_mk])
            asel(m["sell"][:], m["sell"][:], [[0, 128]], ALU.is_equal, 0.0, -127, 1, [B_mk], [B_mk])
            asel(m["selc"][:], m["selc"][:], [[0, 16]], ALU.is_equal, 0.0, -127, 1, [B_mk], [B_mk])
            for g in range(4):
                asel(wsst[:, g, :], wsst[:, g, :], [[1, 128]], ALU.is_ge, 0.0, 0, -1, [B_wsst, B_mk], [B_wsst])
        else:
            def v3(t):
                return t[:].rearrange("p (b i) -> p b i", b=16)
            for nm, fill in (("cmask", 0.0), ("nmT", NEG)):
                asel(v3(m[nm]), v3(m[nm]), [[-4, 16], [0, 4]], ALU.is_ge, fill, 0, 1, [B_mk], [B_mk])
                asel(v3(m[nm]), v3(m[nm]), [[4, 16], [0, 4]], ALU.is_ge, fill, 3, -1, [B_mk], [B_mk])
                asel(v3(m[nm]), v3(m[nm]), [[4, 16], [1, 4]], ALU.is_ge, fill, 0, -1, [B_mk], [B_mk])
            asel(v3(m["nm"]), v3(m["nm"]), [[-4, 16], [0, 4]], ALU.is_ge, NEG, 0, 1, [B_mk], [B_mk])
            asel(v3(m["nm"]), v3(m["nm"]), [[4, 16], [0, 4]], ALU.is_ge, NEG, 3, -1, [B_mk], [B_mk])
            asel(v3(m["nm"]), v3(m["nm"]), [[-4, 16], [-1, 4]], ALU.is_ge, NEG, 0, 1, [B_mk], [B_mk])
            asel(v3(m["sell"]), v3(m["sell"]), [[-4, 16], [0, 4]], ALU.is_equal, 0.0, -3, 1, [B_mk], [B_mk])
            asel(m["blk"][:], m["blk"][:], [[-4, 16]], ALU.is_ge, 0.0, 0, 1, [B_mk], [B_mk])
            asel(m["blk"][:], m["blk"][:], [[4, 16]], ALU.is_ge, 0.0, 3, -1, [B_mk], [B_mk])
            asel(m["selc"][:], m["selc"][:], [[-4, 16]], ALU.is_equal, 0.0, -3, 1, [B_mk], [B_mk])
            for g in range(4):
                w3 = wsst[0:64, g, 0:64].rearrange("p (b i) -> p b i", b=16)
                asel(w3, w3, [[-4, 16], [0, 4]], ALU.is_ge, 0.0, 0, 1, [B_wsst, B_mk], [B_wsst])
                asel(w3, w3, [[4, 16], [0, 4]], ALU.is_ge, 0.0, 3, -1, [B_wsst], [B_wsst])
                asel(w3, w3, [[4, 16], [1, 4]], ALU.is_ge, 0.0, 0, -1, [B_wsst], [B_wsst])
        cp("dve", m["wsT"][:], wsst[0:L, :, 0:L], [B_wsst], [B_mk])

    rot = {"pj": 0, "raw": 0, "tmpf": 0, "tmpg": 0, "vnb": 0, "ST": 0, "vs": 0, "CTb": 0, "qTm": 0, "cs": 0, "vsm": 0, "pC": 0}

    def nxt(name, n):
        v = rot[name]
        rot[name] = (v + 1) % n
        return v

    pre_done = {}
    carry = {}
    mhalf = k.sb("mhalf", [128, 4], F32)
    memset("pool", mhalf[:], -0.5, [B_ones])

    def rsqrt_eps(out, in_, tmp, n, L, B):
        ts("pool", tmp, in_, EPS, 1.0, ALU.add, ALU.mult, B, B)
        tt("pool", out, tmp, mhalf[0:L, 0:n], ALU.pow, B + [B_ones], B)

    def rms_stats_g(src, Bsrc, L):
        for hh in range(2):
            op("dve", lambda e, hh=hh: e.bn_stats(out=rst[0:L, hh * 6:(hh + 1) * 6], in_=src[:, hh * 512:(hh + 1) * 512]),
               Bsrc, [B_st4])
        op("dve", lambda e: e.bn_aggr(out=st4[0:L, 4:6], in_=rst[0:L, 0:12]), [B_st4], [B_st4])
        stt(st4[0:L, 0:1], st4[0:L, 4:5], st4[0:L, 4:5], st4[0:L, 5:6], ALU.mult, ALU.add, [B_st4], [B_st4])
        rsqrt_eps(st4[0:L, 2:3], st4[0:L, 0:1], st4[0:L, 1:2], 1, L, [B_st4])

    def rmsnorm_to_hT_g(src, Bsrc, L, i, gname):
        rms_stats_g(src, Bsrc, L)
        yield; yield; yield
        stt(xn[0:L, :], src, st4[0:L, 2:3], bc_t[gname][0:L, :], ALU.mult, ALU.mult, Bsrc + [B_st4, B_bc[gname]], [B_xn])
        yield; yield
        for kc in range(8):
            tr(ptb[:, kc, 0:L], xn[0:L, kc * 128:(kc + 1) * 128], identb[0:L, 0:L], [B_xn, B_idb], [B_ptb])
        yield
        cp("act", hT[:, :, i * L:(i + 1) * L], ptb[:, :, 0:L], [B_ptb], [B_hT[i]])

    def prologue_gen(G, blk):
        L = LG[G]
        ntile = 4 if G == "P" else 1
        xin = D["xp"] if G == "P" else D["xs"]
        tok0 = blk * 512 if G == "P" else 0
        for i in range(ntile):
            k.dma("sp", xpre[0:L, i, :], xin[tok0 + i * L: tok0 + (i + 1) * L, :], writes=Bxpre(i), sem=Bxpre(i)[0])
        yield
        for i in range(ntile):
            for _ in rmsnorm_to_hT_g(xpre[0:L, i, :], Bxpre(i), L, i, "g1"):
                yield
            yield
        pre_done["key"] = (G, blk)

    def run_block(G, blk, nxt_blk=None):
        L = LG[G]; nb = NBG[G]; Ls = L // nb
        ntile = 4 if G == "P" else 1
        TB = ntile * L
        M = MK[G]
        xin = D["xp"] if G == "P" else D["xs"]
        yout = O["yp"] if G == "P" else O["ys"]
        tok0 = blk * 512 if G == "P" else 0
        last_prompt_blk = (G == "P" and blk == n_prompt_blocks - 1)
        w0 = wbase[0]
        wbase[0] += 39

        def cols(i):
            return slice(i * L, (i + 1) * L)

        def rms_stats(i):
            rms_stats_g(xt[0:L, i, :], [B_xt[i]], L)

        def rmsnorm_to_hT(i, gname):
            for _ in rmsnorm_to_hT_g(xt[0:L, i, :], [B_xt[i]], L, i, gname):
                pass

        hand = None
        if pre_done.get("key") == (G, blk):
            hand = carry.pop("gen")
        else:
            for i in range(ntile):
                k.dma("sp", xt[0:L, i, :], xin[tok0 + i * L: tok0 + (i + 1) * L, :], writes=[B_xt[i]], sem=B_xt[i])
            for i in range(ntile):
                rmsnorm_to_hT(i, "g1")

        halftick = [False]

        def fm_proj(slab_idx, nchunk, rhs, Rrhs, evac):
            W, Bw = ws.get(slab_idx)
            for cc in range(nchunk):
                b = nxt("pj", 4)
                for kc in range(8):
                    mm(pj[b][:, 0:TB], W[:, kc, cc * 128:(cc + 1) * 128], rhs[:, kc, 0:TB], kc == 0, kc == 7,
                       [Bw] + Rrhs, [B_pj[b]])
                    if kc == 3 and halftick[0]:
                        tick()
                tick()
                evac(cc, pj[b], B_pj[b])

        def tm_proj(slab_idx, lhs, Blhs_of_tile, evac):
            W, Bw = ws.get(slab_idx)
            for i in range(ntile):
                b = nxt("pj", 4)
                for kc in range(8):
                    mm(pj[b][0:L, :], lhs[:, kc, cols(i)], W[:, kc, :], kc == 0, kc == 7,
                       [Bw, Blhs_of_tile(i)], [B_pj[b]])
                    if kc == 3 and halftick[0]:
                        tick()
                tick()
                evac(i, pj[b], B_pj[b])

        if G == "S":
            k.dma("sp", mprev["S"][:], D["smr"], writes=[B_mprev["S"]], sem=B_mprev["S"])
        mp = mprev[G]; Bmp = B_mprev[G]

        def bcast_diag(vec4, R):
            tt("dve", diag[0:L, :, 0:L], identf[0:L, 0:L].unsqueeze(1).to_broadcast([L, 4, L]),
               vec4.unsqueeze(2).to_broadcast([L, 4, L]), ALU.mult, [B_idf] + R, [B_diag])

        def bcast_mm():
            for h in range(4):
                mm(pg[0:L, h * 128:h * 128 + L], onesf[0:L, 0:L], diag[0:L, h, 0:L], True, True, [B_ones, B_diag], [B_pg])
            return pg[:, :].rearrange("p (h t) -> p h t", h=4)[0:L, :, 0:L]

        def gates_gen():
            for _ in range(2 if hand is not None else 0):
                yield
            for i in range(ntile):
                ci = cols(i)
                Ex = Ex_t[:, i, :]; BE = B_Ex[i]
                for kc in range(8):
                    mm(pg[0:L, 0:8], hT[:, kc, ci], wif[:, kc, :], kc == 0, kc == 7, [B_hT[i], B_wif], [B_pg])
                yield
                tt("dve", gi[0:L, :], pg[0:L, 0:8], bif[0:L, :], ALU.add, [B_pg, B_sm], [B_g])
                act(lf[0:L, 0:4], gi[0:L, 4:8], AF.Abs, [B_g], [B_g])
                act(lf[0:L, 4:8], lf[0:L, 0:4], AF.Exp, [B_g], [B_g], scale=-1.0)
                act(lf[0:L, 8:12], lf[0:L, 4:8], AF.Ln, [B_g], [B_g], bias=1.0)
                ts("dve", lf[0:L, 0:4], gi[0:L, 4:8], 0.0, None, ALU.min, None, [B_g], [B_g])
                tt("dve", lf[0:L, 12:16], lf[0:L, 0:4], lf[0:L, 8:12], ALU.subtract, [B_g], [B_g])
                yield
                mm(pg[0:L, 16:20], M["cmask"][:, :], lf[0:L, 12:16], True, True, [B_mk, B_g], [B_pg])
                yield
                cp("dve", gb8[0:L, 4:8], pg[0:L, 16:20], [B_pg], [B_g])
                tt("dve", aa[0:L, 0:4], gi[0:L, 0:4], gb8[0:L, 4:8], ALU.subtract, [B_g], [B_g])
                ts("dve", aa[0:L, 4:8], aa[0:L, 0:4], -LN16, None, ALU.add, None, [B_g], [B_g])
                bcast_diag(aa[0:L, 0:4], [B_g])
                yield
                pr = bcast_mm()
                yield
                tt("dve", am[0:L, :, 0:L], pr, M["nm"][:, :].unsqueeze(1).to_broadcast([L, 4, L]), ALU.add, [B_pg, B_mk], [B_am])
                op("dve", lambda e: e.tensor_reduce(out=gb8[0:L, 0:4], in_=am[0:L, :, 0:L], axis=AX.X, op=ALU.max), [B_am], [B_g])
                tt("dve", gb8[0:L, 0:4], gb8[0:L, 0:4], mp[0:L, :], ALU.max, [B_g, Bmp], [B_g])
                yield
                mm(pg[0:L, 0:8], M["sell"][:, :], gb8[0:L, 0:8], True, True, [B_mk, B_g], [B_pg])
                yield
                cp("dve", gl8[0:L, :], pg[0:L, 0:8], [B_pg], [B_g])
                tt("dve", mnd[0:L, 0:4], gl8[0:L, 0:4], gl8[0:L, 4:8], ALU.add, [B_g], [B_g])
                tt("dve", Dx[0:L, 0:4], mp[0:L, :], gb8[0:L, 0:4], ALU.subtract, [B_g, Bmp], [B_g])
                tt("dve", Dx[0:L, 4:8], gb8[0:L, 0:4], gb8[0:L, 4:8], ALU.add, [B_g], [B_g])
                ts("dve", Dx[0:L, 4:8], Dx[0:L, 4:8], -1.0, None, ALU.mult, None, [B_g], [B_g])
                tt("dve", Dx[0:L, 8:12], mp[0:L, :], gl8[0:L, 0:4], ALU.subtract, [B_g, Bmp], [B_g])
                tt("dve", Dx[0:L, 12:16], aa[0:L, 4:8], gl8[0:L, 0:4], ALU.subtract, [B_g], [B_g])
                act(Ex[0:L, :], Dx[0:L, :], AF.Exp, [B_g], [BE])
                cp("dve", mnd[0:L, 4:8], Ex[0:L, 8:12], [BE], [B_g])
                if G == "P":
                    cp("dve", mp[0:L, :], mnd[0:L, 0:4], [B_g], [Bmp])
                bcast_diag(gb8[0:L, 0:4], [B_g])
                yield
                pr = bcast_mm()
                yield
                stt(am[0:L, :, 0:L], pr, -1.0, M["nmT"][:, :].unsqueeze(1).to_broadcast([L, 4, L]), ALU.mult, ALU.add,
                    [B_pg, B_mk], [B_am])
                for h in range(4):
                    act(DTb[0:L, i, h, 0:L], am[0:L, h, 0:L], AF.Exp, [B_am, B_g], [B_DT[i]], bias=aa[0:L, 4 + h:5 + h])
                mm(pg[0:nb, 0:8], M["selc"][:, 0:nb], mnd[0:L, 0:8], True, True, [B_mk, B_g], [B_pg])
                yield
                cp("dve", cm8_t[0:nb, i, :], pg[0:nb, 0:8], [B_pg], [B_cm[i]])
                tt("dve", decx[0:L, 0:nb, :], mnd[0:L, 4:8].unsqueeze(1).to_broadcast([L, nb, 4]),
                   M["selc"][:, 0:nb].unsqueeze(2).to_broadcast([L, nb, 4]), ALU.mult, [B_g, B_mk], [B_decx])
                yield
                mm(pg[:, 0:nb * 4], onesf[0:L, 0:128], decx[0:L, 0:nb, :].rearrange("p b h -> p (b h)"), True, True,
                   [B_ones, B_decx], [B_pg])
                yield
                cp("dve", decb_t[:, i, 0:nb * 4], pg[:, 0:nb * 4], [B_pg], [B_dec[i]])
                tt("dve", wsel_t[0:L, i, :, 0:nb], Ex[0:L, 12:16].unsqueeze(2).to_broadcast([L, 4, nb]),
                   M["blk"][:, 0:nb].unsqueeze(1).to_broadcast([L, 4, nb]), ALU.mult, [BE, B_mk], [B_wsel[i]])
                yield

        bgq = [gates_gen()]
        bgq2 = []
        bgq3 = [hand] if hand is not None else []

        def _adv(q):
            while q:
                try:
                    next(q[0])
                    return
                except StopIteration:
                    q.pop(0)

        tickn = [0]

        def tick():
            _adv(bgq3)
            _adv(bgq)
            tickn[0] += 1
            if tickn[0] % 3 == 0:
                _adv(bgq2)

        def drain():
            while bgq:
                _adv(bgq)

        def drain2():
            while bgq2:
                _adv(bgq2)

        halftick[0] = True
        for s in range(2):
            def ev_u(cc, p, Bp, s=s):
                c = s * 4 + cc
                act(U8[:, c, 0:TB], p[:, 0:TB], AF.Gelu, [Bp], [B_U8[c]])
            fm_proj(w0 + s, 4, hT, B_hT[0:ntile], ev_u)
        for s in range(2):
            def ev_v(i, p, Bp, s=s):
                act(gv[0:L, i, s * 512:(s + 1) * 512], p[0:L, :], AF.Gelu, [Bp], B_ar[4 * i:4 * i + 4])
            tm_proj(w0 + 2 + s, hT, lambda i: B_hT[i], ev_v)
        halftick[0] = False
        vb_of = {}

        def ln_part(i):
            Bgv = B_ar[4 * i:4 * i + 4]
            for hh in range(2):
                op("dve", lambda e, hh=hh, i=i: e.bn_stats(out=hst[0:L, hh * 6:(hh + 1) * 6], in_=gv[0:L, i, hh * 512:(hh + 1) * 512]),
                   Bgv, [B_hst])
            op("dve", lambda e: e.bn_aggr(out=hst[0:L, 12:14], in_=hst[0:L, 0:12]), [B_hst], [B_hst])
            rsqrt_eps(hst[0:L, 15:16], hst[0:L, 13:14], hst[0:L, 14:15], 1, L, [B_hst])
            ts("dve", vnf[0:L, :], gv[0:L, i, :], hst[0:L, 12:13], hst[0:L, 15:16], ALU.subtract, ALU.mult,
               Bgv + [B_hst], [B_vnf])
            tt("dve", vnf[0:L, :], vnf[0:L, :], bc_t["lng"][0:L, :], ALU.mult, [B_vnf, B_bc["lng"]], [B_vnf])
            vb = nxt("vnb", 2)
            vb_of[i] = vb
            if G == "S":
                tt("dve", vnf[0:L, :], vnf[0:L, :], bc_t["lnb"][0:L, :], ALU.add, [B_vnf, B_bc["lnb"]], [B_vnf])
                k.dma("sp", O["vs"], vnf[0:L, :], reads=[B_vnf], sem=B_vnf)
                cp("pool", vnb[vb][0:L, :], vnf[0:L, :], [B_vnf], [B_vnb[vb]])
            else:
                tt("dve", vnb[vb][0:L, :], vnf[0:L, :], bc_t["lnb"][0:L, :], ALU.add, [B_vnf, B_bc["lnb"]], [B_vnb[vb]])

        def spatial_part(i):
            vb = vb_of[i]
            for half in range(2):
                b = nxt("pj", 4)
                for c4 in range(4):
                    kc = half * 4 + c4
                    g = kc // 2
                    mm(pj[b][:, c4 * 128:c4 * 128 + L], vnb[vb][0:L, kc * 128:(kc + 1) * 128], M["wsT"][:, g, :],
                       True, False, [B_vnb[vb], B_mk], [B_pj[b]])
                    mm(pj[b][:, c4 * 128:c4 * 128 + L], ones2[0:2, 0:128], bs2[G][0:2, g * L:(g + 1) * L],
                       False, True, [B_ones, B_bs], [B_pj[b]])
                tick()
                pv = pj[b][:, :].rearrange("p (c t) -> p c t", c=4)[:, :, 0:L]
                tt("dve", aoT[:, half * 4:half * 4 + 4, cols(i)], pv, U8[:, half * 4:half * 4 + 4, cols(i)], ALU.mult,
                   [B_pj[b]] + B_U8[half * 4:half * 4 + 4], [B_ao[i]])

        special_tm = (G == "S") or last_prompt_blk
        qk_state = {}

        def qk_stageA(c):
            s, cc = divmod(c, 4)
            W, Bw = ws.get(w0 + 4 + s)
            if G == "S" and cc == 0:
                k.dma("sp", scst[0:48, 0:512], D["sconv"][:, s * 512:(s + 1) * 512], writes=[B_scst], sem=B_scst)
            b = nxt("pj", 4)
            for kc in range(8):
                mm(pj[b][:, 0:TB], W[:, kc, cc * 128:(cc + 1) * 128], hT[:, kc, 0:TB], kc == 0, kc == 7,
                   [Bw] + B_hT[0:ntile], [B_pj[b]])
            tick()
            r = nxt("raw", 2)
            if G == "P":
                rv = raw[r][:, 0:515].rearrange("p (b t) -> p b t", b=1)
                cp("pool", raw[r][:, 0:3], halo[:, c, :], [B_halo[c]], [B_raw[r]])
                cp("act", raw[r][:, 3:515], pj[b][:, 0:512], [B_pj[b]], [B_raw[r]])
                cp("pool", halo[:, c, :], raw[r][:, 512:515], [B_raw[r]], [B_halo[c]])
                T = 512
            else:
                rv = raw[r][:, 0:112].rearrange("p (b t) -> p b t", b=16)
                tr(pc[:, 0:48], scst[0:48, cc * 128:(cc + 1) * 128], identf[0:48, 0:48], [B_scst, B_idf], [B_pc])
                cp("dve", rv[:, :, 0:3], pc[:, 0:48].rearrange("p (b j) -> p b j", b=16), [B_pc], [B_raw[r]])
                cp("act", rv[:, :, 3:7], pj[b][:, 0:64].rearrange("p (b t) -> p b t", b=16), [B_pj[b]], [B_raw[r]])
                T = 4
            act(cacc[r][:, 0:TB], pj[b][:, 0:TB], AF.Identity, [B_pj[b], B_sm], [B_cacc[r]],
                bias=convb[:, c:c + 1], scale=convw[:, c, 3:4])
            qk_state[c] = (r, rv, T)

        def qk_stageB(c):
            r, rv, T = qk_state[c]
            ca = cacc[r][:, 0:nb * T].rearrange("p (b t) -> p b t", b=nb)
            for j in range(3):
                stt(ca, rv[:, :, j:j + T], convw[:, c, j:j + 1], ca, ALU.mult, ALU.add,
                    [B_raw[r], B_sm, B_cacc[r]], [B_cacc[r]])
            act(qkT[:, c, 0:TB], cacc[r][:, 0:TB], AF.Silu, [B_cacc[r]], [B_ar[c]])

        def qk_special(s):
            W, Bw = ws.get(w0 + 4 + s)
            it = ntile - 1
            b = nxt("pj", 4)
            for kc in range(8):
                mm(pj[b][0:L, :], hT[:, kc, cols(it)], W[:, kc, :], kc == 0, kc == 7, [Bw, B_hT[it]], [B_pj[b]])
            f = nxt("tmpf", 2)
            cp("act", tmpf[f][0:L, :], pj[b][0:L, :], [B_pj[b]], [B_tmpf[f]])
            if G == "P":
                k.dma("sp", O["convp"][:, s * 512:(s + 1) * 512], tmpf[f][125:128, :], reads=[B_tmpf[f]], sem=B_tmpf[f])
            else:
                k.dma("sp", scr[:, s * 512:(s + 1) * 512], tmpf[f][0:64, :], reads=[B_tmpf[f]], writes=[B_scr],
                      sem=B_tmpf[f])

        ln_part(0)
        qk_stageA(0)
        for c in range(16):
            s, cc = divmod(c, 4)
            if cc == 3 and special_tm:
                qk_special(s)
            if c + 1 < 16:
                if cc == 3 and s + 1 < ntile:
                    ln_part(s + 1)
                qk_stageA(c + 1)
            qk_stageB(c)
            if cc == 3 and s < ntile:
                spatial_part(s)
        if G == "S":
            k.dma("sp", O["convs"], scr.rearrange("(b t) c -> b t c", t=4)[:, 1:4, :], reads=[B_scr], sem=B_scr)
        for s in range(2):
            def ev_vm(i, p, Bp, s=s):
                cp("act", vaug[0:L, i, 2 * s:2 * s + 2, 0:256], p[0:L, :].rearrange("p (h d) -> p h d", h=2), [Bp], [B_vaug[i]])
            tm_proj(w0 + 8 + s, hT, lambda i: B_hT[i], ev_vm)
        for s in range(2):
            def ev_o(cc, p, Bp, s=s):
                c = s * 4 + cc
                act(U8[:, c, 0:TB], p[:, 0:TB], AF.Sigmoid, [Bp], [B_U8[c]])
            fm_proj(w0 + 10 + s, 4, hT, B_hT[0:ntile], ev_o)

        while bgq3:
            _adv(bgq3)
        drain()

        def gagb_gen():
            ctr = 0
            for so in range(4):
                W, Bw = ws.get(w0 + 12 + so)
                dst, Bdst, boff = (sga, B_sga, 0) if so < 2 else (sgb, B_sgb, 8)
                for cc in range(4):
                    c = (so % 2) * 4 + cc
                    b = 2 + ctr % 2
                    ctr += 1
                    for kc in range(8):
                        mm(pj[b][:, 0:TB], W[:, kc, cc * 128:(cc + 1) * 128], hT[:, kc, 0:TB], kc == 0, kc == 7,
                           [Bw] + B_hT[0:ntile], [B_pj[b]])
                        if kc == 3:
                            yield
                    yield
                    act(dst[:, c, 0:TB], pj[b][:, 0:TB], AF.Sigmoid, [B_pj[b], B_sm], [Bdst[c]], bias=bgate[:, boff + c:boff + c + 1])

        bgq2.append(gagb_gen())
        nTf = nTf_t

        def hnorm_gen(i, nbuf, Bnb, hsx, Bhs, Ex, BE, ci):
            tt("dve", hsx[0:L, 0:4], hsx[0:L, 0:4], Ex[0:L, 4:8], ALU.max, [Bhs, BE], [Bhs])
            op("dve", lambda e: e.reciprocal(out=hsx[0:L, 4:8], in_=hsx[0:L, 0:4]), [Bhs], [Bhs])
            yield
            for h in range(4):
                op("dve", lambda e, h=h: e.bn_stats(out=hsx[0:L, 8 + 6 * h:14 + 6 * h], in_=nbuf[0:L, h * 256:(h + 1) * 256]),
                   Bnb, [Bhs])
                op("dve", lambda e, h=h: e.bn_aggr(out=hsx[0:L, 32 + 2 * h:34 + 2 * h], in_=hsx[0:L, 8 + 6 * h:14 + 6 * h]),
                   [Bhs], [Bhs])
                if h % 2 == 1:
                    yield
            mv = hsx[0:L, 32:40].rearrange("p (h t) -> p h t", t=2)
            tt("dve", hsx[0:L, 40:44], hsx[0:L, 4:8], hsx[0:L, 4:8], ALU.mult, [Bhs], [Bhs])
            tt("dve", hsx[0:L, 40:44], hsx[0:L, 40:44], mv[:, :, 1], ALU.mult, [Bhs], [Bhs])
            rsqrt_eps(hsx[0:L, 40:44], hsx[0:L, 40:44], hsx[0:L, 40:44], 4, L, [Bhs])
            yield; yield
            tt("dve", hsx[0:L, 44:48], hsx[0:L, 40:44], hsx[0:L, 4:8], ALU.mult, [Bhs], [Bhs])
            stt(hsx[0:L, 0:4], mv[:, :, 0], -1.0, hsx[0:L, 44:48], ALU.mult, ALU.mult, [Bhs], [Bhs])
            yield
            for h in range(4):
                act(hmb[0:L, h * 256:(h + 1) * 256], nbuf[0:L, h * 256:(h + 1) * 256], AF.Identity, Bnb + [Bhs], [B_hmb],
                    bias=hsx[0:L, h:h + 1], scale=hsx[0:L, 44 + h:45 + h])
            yield; yield; yield; yield; yield
            for kc in range(8):
                tr(ptb[:, kc, 0:L], hmb[0:L, kc * 128:(kc + 1) * 128], identb[0:L, 0:L], [B_hmb, B_idb], [B_ptb])
            yield
            for kc in range(8):
                stt(boT[:, kc, ci], ptb[:, kc, 0:L], hng[:, kc:kc + 1], U8[:, kc, ci], ALU.mult, ALU.mult,
                    [B_ptb, B_sm, B_U8[kc]], [B_bo[i]])
            yield
        if G == "S":
            k.dma("sp", vnf[0:16, :], D["sn"], writes=[B_vnf], sem=B_vnf)
            for j in range(8):
                tr(pc[:, j * 16:j * 16 + nb], vnf[0:nb, j * 128:(j + 1) * 128], identf[0:nb, 0:nb], [B_vnf, B_idf], [B_pc])
            cp("dve", nTf[:, :, 0:nb], pc[:, 0:128].rearrange("p (j b) -> p j b", j=8)[:, :, 0:nb], [B_pc], [B_nTf])
        for i in range(ntile):
            ci = cols(i)
            Ex = Ex_t[:, i, :]
            BE = B_Ex[i]
            if i % 2 == 0:
                nbuf = vnf; Bnb = [B_vnf]
            else:
                nbuf = arena[:, 8192:10240].bitcast(F32); Bnb = B_ar[16:20]
            hsx = hs2[i % 2]; Bhs = B_hs2[i % 2]
            tt("dve", diag[0:L, :, 0:L], identf[0:L, 0:L].unsqueeze(1).to_broadcast([L, 4, L]),
               Ex[0:L, 0:4].unsqueeze(2).to_broadcast([L, 4, L]), ALU.mult, [B_idf, BE], [B_diag])
            for h in range(4):
                mm(pc[:, h * 128:h * 128 + L], onesf[0:L, 0:128], diag[0:L, h, 0:L], True, True, [B_ones, B_diag], [B_pc])
            pr = pc[:, :].rearrange("p (h t) -> p h t", h=4)[:, :, 0:L]
            tt("dve", qTs[:, :, 0:L].rearrange("p (h d) t -> p h d t", h=4),
               qkT[:, 0:8, ci].rearrange("p (h d) t -> p h d t", h=4),
               pr.unsqueeze(2).to_broadcast([128, 4, 2, L]), ALU.mult, [B_pc] + B_ar[0:8], [B_qTs])
            for kc in range(8):
                tr(ptb[0:L, kc, :], qkT[:, 8 + kc, ci], identb[:, :], [B_ar[8 + kc], B_idb], [B_ptb])
            cp("act", ktm[0:L, :], ptb[0:L, :, :].rearrange("p c d -> p (c d)"), [B_ptb], [B_ktm])
            ptbf = ptb[:, :, :].rearrange("p a b -> p (a b)").bitcast(F32)
            pCb = [(pg, B_pg), (ptbf, B_ptb)]
            hstate = {}

            def c_src(h, b):
                if G == "P":
                    return Cst[:, h, :, :], B_Cst[h]
                it = h * nb + b
                return Cst[:, it % 5, :, :], B_Cst[it % 5]

            def c_load(it):
                if G == "S" and it < 4 * nb:
                    h_, b_ = divmod(it, nb)
                    Cn_, BCn_ = c_src(h_, b_)
                    k.dma("sp", Cn_, D["sC"][b_, h_].rearrange("(vh p) d -> p vh d", vh=2), writes=[BCn_], sem=BCn_)

            def c_store(it):
                if G == "S" and 0 <= it < 4 * nb:
                    h_, b_ = divmod(it, nb)
                    Cn_, BCn_ = c_src(h_, b_)
                    k.dma("sp", O["Cs"][b_, h_].rearrange("(vh p) d -> p vh d", vh=2), Cn_, reads=[BCn_], sem=BCn_)

            def c_transposes(h, b):
                Cn, BCn = c_src(h, b)
                pcb, Bpcb = pCb[nxt("pC", 2)]
                pC = pcb[:, :].rearrange("p (dh v) -> p dh v", dh=2)
                for vh in range(2):
                    for dh in range(2):
                        tr(pC[:, dh, vh * 128:(vh + 1) * 128], Cn[:, vh, dh * 128:(dh + 1) * 128], identf[:, :],
                           [BCn, B_idf], [Bpcb])
                cb = nxt("CTb", 2)
                cp("act", CTb[cb][:, :, 0:256], pC, [Bpcb], [B_CTb[cb]])
                cp("pool", CTb[cb][:, :, 256:257], nTf[:, 2 * h:2 * h + 2, b:b + 1], [B_nTf], [B_CTb[cb]])
                return cb

            def stage1(h):
                for dh in range(2):
                    mm(pnum[0:L, 0:L], qkT[:, 8 + 2 * h + dh, ci], qkT[:, 2 * h + dh, ci], dh == 0, dh == 1,
                       [B_ar[8 + 2 * h + dh], B_ar[2 * h + dh]], [B_pnum])
                sb_ = nxt("ST", 2)
                tt("dve", STb[sb_][0:L, 0:L], pnum[0:L, 0:L], DTb[0:L, i, h, 0:L], ALU.mult, [B_pnum, B_DT[i]], [B_ST[sb_]])
                vb_ = nxt("vs", 2)
                ts("pool", vsb[vb_][0:L, :], vaug[0:L, i, h, :], Ex[0:L, 12 + h:13 + h], 0.0, ALU.mult, ALU.add,
                   [B_vaug[i], BE], [B_vs[vb_]])
                tick()
                cb0 = c_transposes(h, 0)
                tick()
                hstate[h] = (sb_, vb_, cb0)

            def stage2(h):
                sb_, vb_, cb0 = hstate[h]
                numh = pj[h % 2]; Bnum = B_pj[h % 2]
                mm(numh[0:L, 0:257], STb[sb_][0:L, 0:L], vaug[0:L, i, h, :], True, False, [B_ST[sb_], B_vaug[i]], [Bnum])
                if nb > 1:
                    for dh in range(2):
                        cp("pool", qTm_diag[dh], qTs[:, 2 * h + dh, 0:L].rearrange("p (b j) -> p b j", b=nb),
                           [B_qTs], [B_qTm[dh]])
                cb = cb0
                for b in range(nb):
                    it = h * nb + b
                    Cn, BCn = c_src(h, b)
                    if nb > 1:
                        c_store(it - 2)
                        c_load(it + 2)
                    if nb > 1:
                        vm_ = nxt("vsm", 2)
                        ts("pool", vsm[vm_][0:L, :], vsb[vb_][0:L, 0:256], M["blk"][:, b:b + 1], 0.0, ALU.mult, ALU.add,
                           [B_vs[vb_], B_mk], [B_vsm[vm_]])
                    cb_next = c_transposes(h, b + 1) if b + 1 < nb else None
                    for dh in range(2):
                        if nb == 1:
                            lhs = qTs[:, 2 * h + dh, 0:L]; Rq = [B_qTs]
                        else:
                            lhs = qTm[dh][:, b, 0:L]; Rq = [B_qTm[dh]]
                        mm(numh[0:L, 0:257], lhs, CTb[cb][:, dh, :], False, (b == nb - 1 and dh == 1), Rq + [B_CTb[cb]], [Bnum])
                    for vh in range(2):
                        lv = vsb[vb_][0:L, vh * 128:(vh + 1) * 128] if nb == 1 else vsm[vm_][0:L, vh * 128:(vh + 1) * 128]
                        mm(pc[:, vh * 256:(vh + 1) * 256], lv, ktm[0:L, h * 256:(h + 1) * 256], True, True,
                           [B_vs[vb_] if nb == 1 else B_vsm[vm_], B_ktm], [B_pc])
                    Cflat = Cn.rearrange("p vh d -> p (vh d)")
                    stt(Cflat, Cflat, decb_t[:, i, b * 4 + h:b * 4 + h + 1], pc[:, :], ALU.mult, ALU.add, [BCn, B_dec[i], B_pc], [BCn])
                    cb = cb_next
                    if nb == 1:
                        tick()
                pn = numh[:, 384:416].rearrange("p (dh b) -> p dh b", dh=2)[:, :, 0:nb]
                for dh in range(2):
                    mm(numh[:, 384 + dh * 16:384 + dh * 16 + nb], ktm[0:L, h * 256 + dh * 128:h * 256 + (dh + 1) * 128],
                       wsel_t[0:L, i, h, 0:nb], True, True, [B_ktm, B_wsel[i]], [Bnum])
                nview = nTf[:, 2 * h:2 * h + 2, 0:nb]
                if nb == 1:
                    stt(nview, nview, decb_t[:, i, h:h + 1], pn, ALU.mult, ALU.add, [B_nTf, B_dec[i], Bnum], [B_nTf])
                else:
                    dview = decb_t[:, i, :].rearrange("p (b h) -> p b h", h=4)[:, 0:nb, h]
                    tt("dve", nview, nview, dview.unsqueeze(1).to_broadcast([128, 2, nb]), ALU.mult, [B_nTf, B_dec[i]], [B_nTf])
                    tt("dve", nview, nview, pn, ALU.add, [B_nTf, Bnum], [B_nTf])
                act(hsx[0:L, h:h + 1], numh[0:L, 256:257], AF.Abs, [Bnum], [Bhs])
                cp("act", nbuf[0:L, h * 256:(h + 1) * 256], numh[0:L, 0:256], [Bnum], Bnb)
                tick()

            if G == "S":
                c_load(0); c_load(1)
            stage1(0)
            for h in range(4):
                if nb == 1 and h + 1 < 4:
                    stage1(h + 1)
                stage2(h)
                if nb > 1 and h + 1 < 4:
                    stage1(h + 1)
            if G == "S":
                c_store(4 * nb - 2); c_store(4 * nb - 1)
            bgq.append(hnorm_gen(i, nbuf, Bnb, hsx, Bhs, Ex, BE, ci))
            if i == ntile - 1:
                drain()
            if G == "S" or (last_prompt_blk and i == ntile - 1):
                k.dma("sp", O["ms"] if G == "S" else O["mp"], cm8_t[0:nb, i, 0:4], reads=[B_cm[i]], sem=B_cm[i])
                for half in range(2):
                    pb_ = pg if half == 0 else pc
                    Bpb = B_pg if half == 0 else B_pc
                    for jj in range(4):
                        j = half * 4 + jj
                        tr(pb_[0:nb, jj * 128:(jj + 1) * 128], nTf[:, j, 0:nb], identf[:, :], [B_nTf, B_idf], [Bpb])
                    cp("dve", vnf[0:nb, half * 512:(half + 1) * 512], pb_[0:nb, :], [Bpb], [B_vnf])
                k.dma("sp", O["ns"] if G == "S" else O["np"], vnf[0:nb, :], reads=[B_vnf], sem=B_vnf)
                if G == "P":
                    for h in range(4):
                        k.dma("sp", O["Cp"][h].rearrange("(vh p) d -> p vh d", vh=2), Cst[:, h, :, :], reads=[B_Cst[h]],
                              sem=B_Cst[h])

        if debug and G == "P" and blk == 0:
            k.dma("sp", O["dbg_ao"], aoT[:, :, :], reads=B_ao, sem=B_ao[0])
            k.dma("sp", O["dbg_bo"], boT[:, :, :], reads=B_bo, sem=B_bo[0])
            k.dma("sp", O["dbg_qk"], qkT, reads=B_ar[0:16], sem=B_ar[0])
        drain2()
        for s in range(2):
            Wa, Ba = ws.get(w0 + 16 + 2 * s)
            Wb, Bb = ws.get(w0 + 17 + 2 * s, hold=w0 + 16 + 2 * s)
            for cc in range(4):
                c = s * 4 + cc
                b1 = nxt("pj", 4)
                for kc in range(8):
                    mm(pj[b1][:, 0:TB], Wa[:, kc, cc * 128:(cc + 1) * 128], aoT[:, kc, 0:TB], kc == 0, kc == 7,
                       [Ba] + B_ao[0:ntile], [B_pj[b1]])
                b2 = nxt("pj", 4)
                for kc in range(8):
                    mm(pj[b2][:, 0:TB], Wb[:, kc, cc * 128:(cc + 1) * 128], boT[:, kc, 0:TB], kc == 0, kc == 7,
                       [Bb] + B_bo[0:ntile], [B_pj[b2]])
                f = nxt("tmpf", 2)
                g_ = nxt("tmpg", 2)
                tt("dve", tmpf[f][:, 0:TB], pj[b1][:, 0:TB], sga[:, c, 0:TB], ALU.mult, [B_pj[b1], B_sga[c]], [B_tmpf[f]])
                tt("dve", tmpg[g_][:, 0:TB], pj[b2][:, 0:TB], sgb[:, c, 0:TB], ALU.mult, [B_pj[b2], B_sgb[c]], [B_tmpg[g_]])
                tt("pool", U8[:, c, 0:TB], tmpf[f][:, 0:TB], tmpg[g_][:, 0:TB], ALU.add, [B_tmpf[f], B_tmpg[g_]], [B_U8[c]])
        if debug and G == "P" and blk == 0:
            k.dma("sp", O["dbg_mg"], U8[:, :, :], reads=B_U8, sem=B_U8[0])
        Wo0, Bwo0 = ws.get(w0 + 20)
        Wo1, Bwo1 = ws.get(w0 + 21, hold=w0 + 20)
        for i in range(ntile):
            for s, (W, Bw) in enumerate(((Wo0, Bwo0), (Wo1, Bwo1))):
                b = nxt("pj", 4)
                for kc in range(8):
                    mm(pj[b][0:L, :], U8[:, kc, cols(i)], W[:, kc, :], kc == 0, kc == 7, [Bw, B_U8[kc]], [B_pj[b]])
                    if kc in (3, 7):
                        tick()
                tt("dve", xt[0:L, i, s * 512:(s + 1) * 512], xt[0:L, i, s * 512:(s + 1) * 512], pj[b][0:L, :], ALU.add,
                   [B_xt[i], B_pj[b]], [B_xt[i]])
            bgq.append(rmsnorm_to_hT_g(xt[0:L, i, :], [B_xt[i]], L, i, "g2"))
        drain()
        for s in range(11):
            W, Bw = ws.get(w0 + 22 + s)
            for jj in range(2):
                j = 2 * s + jj
                b1 = nxt("pj", 4)
                for kc in range(8):
                    mm(pj[b1][:, 0:TB], W[:, kc, jj * 128:(jj + 1) * 128], hT[:, kc, 0:TB], kc == 0, kc == 7,
                       [Bw] + B_hT[0:ntile], [B_pj[b1]])
                b2 = nxt("pj", 4)
                for kc in range(8):
                    mm(pj[b2][:, 0:TB], W[:, kc, 256 + jj * 128:256 + (jj + 1) * 128], hT[:, kc, 0:TB], kc == 0, kc == 7,
                       [Bw] + B_hT[0:ntile], [B_pj[b2]])
                f = nxt("tmpf", 2)
                act(tmpf[f][:, 0:TB], pj[b1][:, 0:TB], AF.Silu, [B_pj[b1]], [B_tmpf[f]])
                tt("dve", gT[:, j, 0:TB], tmpf[f][:, 0:TB], pj[b2][:, 0:TB], ALU.mult, [B_tmpf[f], B_pj[b2]], [B_ar[j]])
        if nxt_blk is not None:
            bgq.append(prologue_gen(*nxt_blk))
        for g in range(2):
            for qi, q0 in enumerate((0, 8, 16)):
                nk = min(8, 22 - q0)
                W, Bw = ws.get(w0 + 33 + g * 3 + qi)
                for i in range(ntile):
                    for kk in range(nk):
                        kc = q0 + kk
                        mm(pj[i][0:L, :], gT[:, kc, cols(i)], W[:, kk, :], kc == 0, kc == 21, [Bw, B_ar[kc]], [B_pj[i]])
                    tick()
            if g == 1:
                drain()
            for i in range(ntile):
                tt("dve", xt[0:L, i, g * 512:(g + 1) * 512], xt[0:L, i, g * 512:(g + 1) * 512], pj[i][0:L, :], ALU.add,
                   [B_xt[i], B_pj[i]], [B_xt[i]])
        def final_gen():
            if nxt_blk is not None:
                Ln = LG[nxt_blk[0]]
                ntn = 4 if nxt_blk[0] == "P" else 1
            else:
                Ln = 0; ntn = 0
            for i in range(ntile):
                rms_stats_g(xt[0:L, i, :], [B_xt[i]], L)
                yield
                stt(xt[0:L, i, :], xt[0:L, i, :], st4[0:L, 2:3], bc_t["gf"][0:L, :], ALU.mult, ALU.mult,
                    [B_xt[i], B_st4, B_bc["gf"]], [B_xt[i]])
                k.dma("sp", yout[tok0 + i * L: tok0 + (i + 1) * L, :], xt[0:L, i, :], reads=[B_xt[i]], sem=B_xt[i])
                if i < ntn:
                    k.dma("sp", xt[0:Ln, i, :], xpre[0:Ln, i, :], reads=Bxpre(i), writes=[B_xt[i]], sem=B_xt[i])
                yield

        if nxt_blk is not None and pre_done.get("key") == nxt_blk:
            carry["gen"] = final_gen()
        else:
            for _ in final_gen():
                pass

    B_scr = k.buf("scr")
    for bi, (G_, b_) in enumerate(blocks):
        run_block(G_, b_, blocks[bi + 1] if bi + 1 < len(blocks) else None)
    allb = B_xt + B_Cst + B_cm + [B_vnf, B_scr] + B_tmpf
    k.final_wait("sp", allb)
    k.emit()
    return nc, k


_CACHE = {}


def _layouts(inp):
    f = np.float32
    A = lambda a: np.ascontiguousarray(a, dtype=f)
    w_in = inp["w_in"][0]
    sh = {}
    w_main = np.concatenate([w_in[:, :6144], w_in[:, 6152:8200]], axis=1)
    sh["w_in"] = A(w_main.reshape(8, 128, 16, 512).transpose(2, 1, 0, 3))
    sh["wif"] = A(w_in[:, 6144:6152].reshape(8, 128, 8).transpose(1, 0, 2))
    sh["bif"] = A(np.concatenate([inp["b_i"][0], inp["b_f"][0]]))
    for nm, key in (("g1", "g_norm1"), ("g2", "g_norm2"), ("lng", "ln_g"), ("lnb", "ln_b")):
        sh[nm] = A(inp[key][0])
    sh["gf"] = A(inp["g_final"])
    ws_ = inp["w_s"][0]
    sh["wsTP"] = A(ws_.transpose(2, 0, 1))
    w4 = ws_[:, :4, :4]
    sh["wsTS"] = A(np.tile(w4.transpose(2, 0, 1), (16, 1, 16)))
    bs_ = inp["b_s"][0]
    sh["bsrP"] = A(bs_.reshape(512))
    sh["bsrS"] = A(np.tile(bs_[:, :4], (1, 16)).reshape(256))
    sh["convw"] = A(inp["conv_w"][0].reshape(4, 16, 128).transpose(2, 1, 0))
    sh["convb"] = A(inp["conv_b"][0].reshape(16, 128).T)
    sh["hng"] = A(inp["hn_g"][0].reshape(8, 128).T)
    sh["bgate"] = A(inp["b_gate"][0].reshape(16, 128).T)
    for nm, key in (("wpa", "w_proj_a"), ("wpb", "w_proj_b"), ("wo", "w_out")):
        sh[nm] = A(inp[key][0].reshape(8, 128, 2, 512).transpose(2, 1, 0, 3))
    wfi = inp["w_ffn_in"][0]
    colidx = []
    for s in range(11):
        for base in (0, 2816):
            for jj in range(2):
                j = 2 * s + jj
                colidx.extend(range(base + j * 128, base + (j + 1) * 128))
    sh["wfi"] = A(wfi[:, colidx].reshape(8, 128, 11, 512).transpose(2, 1, 0, 3))
    sh["wfo"] = A(inp["w_ffn_out"][0].reshape(22, 128, 2, 512).transpose(2, 1, 0, 3))
    return sh


def kernel(**inp):
    inp = {kk: np.asarray(v) for kk, v in inp.items()}
    if "nc" not in _CACHE:
        _CACHE["nc"] = build_program()[0]
    nc = _CACHE["nc"]
    shared = _layouts(inp)
    in_maps = []
    for c in range(NCORES):
        m = dict(shared)
        m["xp"] = np.ascontiguousarray(inp["x_prompt"][c], dtype=np.float32)
        sl = slice(16 * c, 16 * c + 16)
        m["xs"] = np.ascontiguousarray(inp["x_sample"][sl].reshape(64, 1024), dtype=np.float32)
        m["sconv"] = np.ascontiguousarray(inp["state_conv"][0, sl].reshape(48, 2048), dtype=np.float32)
        m["sC"] = np.ascontiguousarray(inp["state_C"][0, sl], dtype=np.float32)
        m["sn"] = np.ascontiguousarray(inp["state_n"][0, sl].reshape(16, 1024), dtype=np.float32)
        m["smr"] = np.ascontiguousarray(np.repeat(inp["state_m"][0, sl], 4, axis=0), dtype=np.float32)
        in_maps.append(m)
    res = run_bass_kernel_spmd(nc, in_maps, core_ids=list(range(NCORES)))
    R = res.results
    cat = lambda nm: [np.asarray(r[nm]) for r in R]
    y_prompt = np.stack(cat("yp"), 0)
    y_sample = np.concatenate(cat("ys"), 0).reshape(128, 4, 1024)
    conv_p = np.stack(cat("convp"), 0)[None]
    C_p = np.stack(cat("Cp"), 0)[None]
    n_p = np.stack([a.reshape(4, 256) for a in cat("np")], 0)[None]
    m_p = np.concatenate(cat("mp"), 0)[None]
    conv_s = np.concatenate(cat("convs"), 0)[None]
    C_s = np.concatenate(cat("Cs"), 0)[None]
    n_s = np.concatenate(cat("ns"), 0).reshape(128, 4, 256)[None]
    m_s = np.concatenate(cat("ms"), 0)[None]
    v_s = np.concatenate(cat("vs"), 0).reshape(128, 4, 1024)[None]
    outs = (y_prompt, y_sample, conv_p, C_p, n_p, m_p, conv_s, C_s, n_s, m_s, v_s)
    return tuple(np.ascontiguousarray(o, dtype=np.float32) for o in outs)
```

```python
import math
import numpy as np
from contextlib import ExitStack
import concourse.bass as bass
import concourse.mybir as mybir
from concourse.bass_utils import run_bass_kernel_spmd

F32 = mybir.dt.float32
BF16 = mybir.dt.bfloat16
AF = mybir.ActivationFunctionType
ALU = mybir.AluOpType
AX = mybir.AxisListType

SAME_ENG_SYNC = True
EPS = 1e-6
LN16 = math.log(16.0)
NEG = -1.0e30
NCORES = 8


class Buf:
    __slots__ = ("name", "w", "r", "excl", "dsem")

    def __init__(self, name, excl=False):
        self.name = name
        self.w = None
        self.r = {}
        self.excl = excl
        self.dsem = None


class KB:
    ENGS = ("pe", "act", "dve", "pool", "sp")

    def __init__(self, nc):
        self.nc = nc
        self.prog = {e: [] for e in self.ENGS}
        self.cnt = {e: 0 for e in self.ENGS}
        self.dcnt = {}
        self.waited = {e: {} for e in self.ENGS}
        self.stack = ExitStack()
        self.nbuf = 0
        self.sb_bytes = 0

    def sb(self, name, shape, dt):
        n = 1
        for s in shape[1:]:
            n *= s
        self.sb_bytes += n * (2 if dt == BF16 else 4)
        return self.stack.enter_context(self.nc.sbuf_tensor("sb_" + name, list(shape), dt))

    def ps(self, name, shape, dt):
        return self.stack.enter_context(self.nc.psum_tensor("ps_" + name, list(shape), dt))

    def buf(self, name, excl=False):
        self.nbuf += 1
        return Buf(f"{name}_{self.nbuf}", excl)

    def bufs(self, name, n):
        return [self.buf(f"{name}{i}") for i in range(n)]

    def _deps(self, eng, reads, writes, dma_sem=None):
        deps = {}
        own = "E:" + eng

        def add(d, excl):
            if d is None:
                return
            kk, v = d
            if kk == own and (excl or eng == "pe" or not SAME_ENG_SYNC):
                return
            if deps.get(kk, 0) < v:
                deps[kk] = v

        for b in reads:
            add(b.w, b.excl)
            if b.excl:
                for kv in b.r.items():
                    add(kv, True)
        for b in writes:
            if not (dma_sem is not None and b.w is not None and b.w[0] == dma_sem):
                add(b.w, b.excl)
            for kv in b.r.items():
                add(kv, b.excl)
        out = []
        for kk, v in deps.items():
            if self.waited[eng].get(kk, 0) >= v:
                continue
            self.waited[eng][kk] = v
            out.append((kk, v))
        return out

    def _mark(self, key, val, reads, writes):
        for b in reads:
            if b.excl:
                b.w = (key, val)
                b.r = {}
            elif b.r.get(key, 0) < val:
                b.r[key] = val
        for b in writes:
            b.w = (key, val)
            b.r = {}

    def op(self, eng, fn, reads=(), writes=()):
        reads = [b for b in reads if b is not None]
        writes = [b for b in writes if b is not None]
        waits = self._deps(eng, reads, writes)
        self.cnt[eng] += 1
        key = "E:" + eng
        self.prog[eng].append((waits, fn, key, 1))
        self._mark(key, self.cnt[eng], reads, writes)

    def dma(self, eng, out, in_, reads=(), writes=(), sem=None):
        reads = [b for b in reads if b is not None]
        writes = [b for b in writes if b is not None]
        if sem.dsem is None:
            sem.dsem = "D:" + sem.name
            self.dcnt[sem.dsem] = 0
        key = sem.dsem
        waits = self._deps(eng, reads, writes, dma_sem=key)
        self.dcnt[key] += 16
        fn = (lambda e, out=out, in_=in_: e.dma_start(out=out, in_=in_))
        self.prog[eng].append((waits, fn, key, 16))
        self._mark(key, self.dcnt[key], reads, writes)

    def final_wait(self, eng, bufs):
        waits = self._deps(eng, [], bufs)
        self.prog[eng].append((waits, None, None, 0))

    def emit(self):
        nc = self.nc
        keys = ["E:" + e for e in self.ENGS if e != "sp"] + list(self.dcnt.keys())
        sems = {}
        for i, kk in enumerate(keys):
            sems[kk] = self.stack.enter_context(nc.semaphore(f"s{i}"))
        prog = self.prog

        def run(e, lst):
            for waits, fn, key, inc in lst:
                for kk, v in waits:
                    e.wait_ge(sems[kk], v)
                if fn is None:
                    continue
                fn(e).then_inc(sems[key], inc)

        with nc.Block() as block:
            @block.tensor
            def _(e):
                run(e, prog["pe"])

            @block.scalar
            def _(e):
                run(e, prog["act"])

            @block.vector
            def _(e):
                run(e, prog["dve"])

            @block.gpsimd
            def _(e):
                run(e, prog["pool"])

            @block.sync
            def _(e):
                run(e, prog["sp"])
        self.stack.close()


def build_program(n_prompt_blocks=4, do_sample=True, debug=False):
    nc = bass.Bass("TRN2", target_bir_lowering=False)

    def din(name, shape, dt=F32):
        return nc.dram_tensor(name, list(shape), dt, kind="ExternalInput").ap()

    def dout(name, shape, dt=F32):
        return nc.dram_tensor(name, list(shape), dt, kind="ExternalOutput").ap()

    D = {}
    D["xp"] = din("xp", [2048, 1024]); D["xs"] = din("xs", [64, 1024])
    D["sconv"] = din("sconv", [48, 2048]); D["sC"] = din("sC", [16, 4, 256, 256])
    D["sn"] = din("sn", [16, 1024]); D["smr"] = din("smr", [64, 4])
    for nm in ("g1", "g2", "gf", "lng", "lnb"):
        D[nm] = din(nm, [1024])
    D["w_in"] = din("w_in", [16, 128, 8, 512]); D["wif"] = din("wif", [128, 8, 8]); D["bif"] = din("bif", [8])
    D["wsTP"] = din("wsTP", [128, 4, 128]); D["wsTS"] = din("wsTS", [64, 4, 64])
    D["bsrP"] = din("bsrP", [512]); D["bsrS"] = din("bsrS", [256])
    D["convw"] = din("convw", [128, 16, 4]); D["convb"] = din("convb", [128, 16])
    D["hng"] = din("hng", [128, 8]); D["bgate"] = din("bgate", [128, 16])
    D["wpa"] = din("wpa", [2, 128, 8, 512]); D["wpb"] = din("wpb", [2, 128, 8, 512]); D["wo"] = din("wo", [2, 128, 8, 512])
    D["wfi"] = din("wfi", [11, 128, 8, 512]); D["wfo"] = din("wfo", [2, 128, 22, 512])
    O = {}
    O["yp"] = dout("yp", [2048, 1024]); O["ys"] = dout("ys", [64, 1024]); O["convp"] = dout("convp", [3, 2048])
    O["Cp"] = dout("Cp", [4, 256, 256]); O["np"] = dout("np", [1, 1024]); O["mp"] = dout("mp", [1, 4])
    O["convs"] = dout("convs", [16, 3, 2048]); O["Cs"] = dout("Cs", [16, 4, 256, 256]); O["ns"] = dout("ns", [16, 1024])
    O["ms"] = dout("ms", [16, 4]); O["vs"] = dout("vs", [64, 1024])
    scr = nc.dram_tensor("qkscr", [64, 2048], F32, kind="Internal").ap()
    wscr = nc.dram_tensor("wscr", [39, 128, 4096], BF16, kind="Internal").ap()
    if debug:
        for nm in ("dbg_ao", "dbg_bo", "dbg_mg"):
            O[nm] = dout(nm, [128, 8, 512], BF16)
        O["dbg_qk"] = dout("dbg_qk", [128, 16, 512], BF16)
        O["dbg_vn"] = dout("dbg_vn", [4, 128, 1024], BF16)
        O["dbg_gv"] = dout("dbg_gv", [4, 128, 1024], F32)
        O["dbg_hst"] = dout("dbg_hst", [4, 128, 16], F32)

    k = KB(nc)
    op = k.op

    def mm(out, lhsT, rhs, start, stop, R, W):
        op("pe", lambda e: e.matmul(out=out, lhsT=lhsT, rhs=rhs, start=start, stop=stop), R, W)

    def tr(out, in_, ident, R, W):
        op("pe", lambda e: e.transpose(out=out, in_=in_, identity=ident), R, W)

    def act(out, in_, func, R, W, bias=0.0, scale=1.0, accum=None, eng="act"):
        if accum is None:
            op(eng, lambda e: e.activation(out=out, in_=in_, func=func, bias=bias, scale=scale), R, W)
        else:
            op(eng, lambda e: e.activation(out=out, in_=in_, func=func, bias=bias, scale=scale, accum_out=accum), R, W)

    def tt(eng, out, in0, in1, o, R, W):
        op(eng, lambda e: e.tensor_tensor(out=out, in0=in0, in1=in1, op=o), R, W)

    def ts(eng, out, in0, s1, s2, op0, op1, R, W):
        if s2 is None:
            op(eng, lambda e: e.tensor_scalar(out=out, in0=in0, scalar1=s1, scalar2=None, op0=op0), R, W)
        else:
            op(eng, lambda e: e.tensor_scalar(out=out, in0=in0, scalar1=s1, scalar2=s2, op0=op0, op1=op1), R, W)

    def stt(out, in0, scalar, in1, op0, op1, R, W):
        op("dve", lambda e: e.scalar_tensor_tensor(out=out, in0=in0, scalar=scalar, in1=in1, op0=op0, op1=op1), R, W)

    def cp(eng, out, in_, R, W):
        if eng == "act":
            op("act", lambda e: e.activation(out=out, in_=in_, func=AF.Copy), R, W)
        else:
            op(eng, lambda e: e.tensor_copy(out=out, in_=in_), R, W)

    def memset(eng, ap, val, W):
        op(eng, lambda e: e.memset(ap, val), [], W)

    def asel(out, in_, pattern, cmp, fill, base, chm, R, W):
        op("pool", lambda e: e.affine_select(out=out, in_=in_, pattern=pattern, compare_op=cmp, fill=fill,
                                             base=base, channel_multiplier=chm), R, W)

    pj = [k.ps(f"pj{i}", [128, 512], F32) for i in range(4)]
    B_pj = [k.buf(f"pj{i}", excl=True) for i in range(4)]
    ptb = k.ps("ptb", [128, 8, 128], BF16); B_ptb = k.buf("ptb", excl=True)
    pg = k.ps("pg", [128, 512], F32); B_pg = k.buf("pg", excl=True)
    pnum = k.ps("pnum", [128, 512], F32); B_pnum = k.buf("pnum", excl=True)
    pc = k.ps("pc", [128, 512], F32); B_pc = k.buf("pc", excl=True)

    identf = k.sb("identf", [128, 128], F32); B_idf = k.buf("identf")
    identb = k.sb("identb", [128, 128], BF16); B_idb = k.buf("identb")
    onesf = k.sb("onesf", [128, 128], F32); B_ones = k.buf("onesf")
    bc_t = {}
    B_bc = {}
    for nm in ("g1", "g2", "gf", "lng", "lnb"):
        bc_t[nm] = k.sb("bc_" + nm, [128, 1024], F32)
        B_bc[nm] = k.buf("bc_" + nm)
        k.dma("sp", bc_t[nm][:], D[nm].partition_broadcast(128), writes=[B_bc[nm]], sem=B_bc[nm])
    B_sm = k.buf("smallconst")
    convw = k.sb("convw", [128, 16, 4], F32); convb = k.sb("convb", [128, 16], F32)
    hng = k.sb("hng", [128, 8], F32); bgate = k.sb("bgate", [128, 16], F32)
    bif = k.sb("bif", [128, 8], F32); wif = k.sb("wif", [128, 8, 8], BF16)
    bsf = {"P": k.sb("bsfP", [2, 512], F32), "S": k.sb("bsfS", [2, 256], F32)}
    bs2 = {"P": k.sb("bs2P", [2, 512], BF16), "S": k.sb("bs2S", [2, 256], BF16)}
    ones2 = k.sb("ones2", [2, 128], BF16)
    k.dma("sp", convw[:], D["convw"], writes=[B_sm], sem=B_sm)
    k.dma("sp", convb[:], D["convb"], writes=[B_sm], sem=B_sm)
    k.dma("sp", hng[:], D["hng"], writes=[B_sm], sem=B_sm)
    k.dma("sp", bgate[:], D["bgate"], writes=[B_sm], sem=B_sm)
    k.dma("sp", bif[:], D["bif"].partition_broadcast(128), writes=[B_sm], sem=B_sm)
    k.dma("sp", bsf["P"][:], D["bsrP"].partition_broadcast(2), writes=[B_sm], sem=B_sm)
    k.dma("sp", bsf["S"][:], D["bsrS"].partition_broadcast(2), writes=[B_sm], sem=B_sm)
    B_wif = k.buf("wif")
    k.dma("pool", wif[:], D["wif"], writes=[B_wif], sem=B_wif)

    memset("pool", identf[:], 1.0, [B_idf])
    asel(identf[:], identf[:], [[-1, 128]], ALU.is_equal, 0.0, 0, 1, [B_idf], [B_idf])
    cp("dve", identb[:], identf[:], [B_idf], [B_idb])
    memset("pool", onesf[:], 1.0, [B_ones])

    xt = k.sb("xt", [128, 4, 1024], F32); B_xt = k.bufs("xt", 4)
    hT = k.sb("hT", [128, 8, 512], BF16); B_hT = k.bufs("hT", 4)
    xn = k.sb("xn", [128, 1024], BF16); B_xn = k.buf("xn")
    junk = xn; B_junk = B_xn
    st4 = k.sb("st4", [128, 8], F32); B_st4 = k.buf("st4")
    rst = k.sb("rst", [128, 12], F32)
    U8 = k.sb("U8", [128, 8, 512], BF16); B_U8 = k.bufs("U8", 8)
    aoT = k.sb("aoT", [128, 8, 512], BF16); B_ao = k.bufs("aoT", 4)
    boT = k.sb("boT", [128, 8, 512], BF16); B_bo = k.bufs("boT", 4)
    sgab = k.sb("sgab", [128, 16, 512], BF16)
    sga = sgab[:, 0:8, :]; B_sga = k.bufs("sga", 8)
    sgb = sgab[:, 8:16, :]; B_sgb = k.bufs("sgb", 8)
    xpre = sgab[:, :, :].rearrange("p a b -> p (a b)").bitcast(F32).rearrange("p (i c) -> p i c", i=4)
    B_sgab = B_sga + B_sgb

    def Bxpre(i):
        return B_sgab[4 * i:4 * i + 4]
    arena = k.sb("arena", [128, 11264], BF16); B_ar = k.bufs("arena", 22)
    gv = arena[:, 0:8192].bitcast(F32).rearrange("p (i c) -> p i c", i=4)
    qkT = arena[:, 0:8192].rearrange("p (c t) -> p c t", c=16)
    gT = arena[:, :].rearrange("p (c t) -> p c t", c=22)
    memset("pool", ones2[:], 1.0, [B_ones])
    B_bs = k.buf("bs2")
    bsh = arena[0:2, 0:512]
    bsg = arena[0:2, 1024:2048].bitcast(F32)
    bsr_ = arena[0:2, 2048:3072].bitcast(F32)
    for G_ in ("P", "S"):
        n_ = 512 if G_ == "P" else 256
        cp("dve", bsh[:, 0:n_], bsf[G_][:, :], [B_sm], [B_bs] + B_ar[0:6])
        cp("dve", bsg[:, 0:n_], bsh[:, 0:n_], [B_bs] + B_ar[0:6], [B_bs] + B_ar[0:6])
        tt("dve", bsr_[:, 0:n_], bsf[G_][:, :], bsg[:, 0:n_], ALU.subtract, [B_sm, B_bs], [B_bs] + B_ar[0:6])
        ts("dve", bsr_[:, 0:n_], bsr_[:, 0:n_], identf[0:2, 1:2], None, ALU.mult, None, [B_bs, B_idf], [B_bs] + B_ar[0:6])
        stt(bs2[G_][:, :], bsg[:, 0:n_], identf[0:2, 0:1], bsr_[:, 0:n_], ALU.mult, ALU.add, [B_bs, B_idf] + B_ar[0:6], [B_bs])
    vnf = k.sb("vnf", [128, 1024], F32); B_vnf = k.buf("vnf")
    vnb = [k.sb(f"vnb{i}", [128, 1024], BF16) for i in range(2)]; B_vnb = k.bufs("vnb", 2)
    raw = [k.sb(f"raw{i}", [128, 520], F32) for i in range(2)]; B_raw = k.bufs("raw", 2)
    cacc = [k.sb(f"cacc{i}", [128, 512], F32) for i in range(2)]; B_cacc = k.bufs("cacc", 2)
    halo = k.sb("halo", [128, 16, 3], F32); B_halo = k.bufs("halo", 16)
    scst = vnf; B_scst = B_vnf
    vaug = k.sb("vaug", [128, 4, 4, 257], BF16); B_vaug = k.bufs("vaug", 4)
    tmpf = [k.sb(f"tmpf{i}", [128, 512], F32) for i in range(2)]; B_tmpf = k.bufs("tmpf", 2)
    tmpg = cacc; B_tmpg = B_cacc
    ktm = k.sb("ktm", [128, 1024], BF16); B_ktm = k.buf("ktm")
    hmb = k.sb("hmb", [128, 1024], BF16); B_hmb = k.buf("hmb")
    slabs = [k.sb(f"slab{i}", [128, 8, 512], BF16) for i in range(3)]; B_slab = k.bufs("slab", 3)
    gi = k.sb("gi", [128, 8], F32); lf = k.sb("lf", [128, 16], F32)
    gb8 = k.sb("gb8", [128, 8], F32); gl8 = k.sb("gl8", [128, 8], F32)
    aa = k.sb("aa", [128, 8], F32)
    Dx = k.sb("Dx", [128, 16], F32)
    mnd = k.sb("mnd", [128, 8], F32)
    B_g = k.buf("gates")
    mprev = {"P": k.sb("mprevP", [128, 4], F32), "S": k.sb("mprevS", [64, 4], F32)}
    B_mprev = {"P": k.buf("mprevP"), "S": k.buf("mprevS")}
    diag = k.sb("diag", [128, 4, 128], F32); B_diag = k.buf("diag")
    DTb = k.sb("DTb", [128, 4, 4, 128], BF16); B_DT = k.bufs("DT", 4)
    decx = k.sb("decx", [128, 16, 4], F32); B_decx = k.buf("decx")
    decb_t = k.sb("decb_t", [128, 4, 64], F32); B_dec = k.bufs("dec", 4)
    wsel_t = k.sb("wsel_t", [128, 4, 4, 16], BF16); B_wsel = k.bufs("wsel", 4)
    Ex_t = k.sb("Ex_t", [128, 4, 16], F32); B_Ex = k.bufs("Ex", 4)
    cm8_t = k.sb("cm8_t", [16, 4, 8], F32); B_cm = k.bufs("cm", 4)
    hs2 = [k.sb(f"hs{i}", [128, 48], F32) for i in range(2)]; B_hs2 = k.bufs("hs", 2)
    nTf_t = k.sb("nTf", [128, 8, 16], F32); B_nTf = k.buf("nTf")
    qTs = k.sb("qTs", [128, 8, 128], BF16); B_qTs = k.buf("qTs")
    qTm = [k.sb(f"qTm{i}", [128, 16, 64], BF16) for i in range(2)]; B_qTm = k.bufs("qTm", 2)
    qTm_diag = []
    for i in range(2):
        a0 = qTm[i][:, :, :]
        qTm_diag.append(bass.AP(a0.tensor, a0.offset, [list(a0.ap[0]), [68, 16], [1, 4]]))
        op("pool", (lambda e, i=i: e.memset(qTm[i][:, :, :], 0.0)), [], [B_qTm[i]])
    STb = [k.sb(f"STb{i}", [128, 128], BF16) for i in range(2)]; B_ST = k.bufs("ST", 2)
    vsb = [k.sb(f"vsb{i}", [128, 257], BF16) for i in range(2)]; B_vs = k.bufs("vs", 2)
    vsm = [k.sb(f"vsm{i}", [64, 256], BF16) for i in range(2)]; B_vsm = k.bufs("vsm", 2)
    hst = k.sb("hst", [128, 16], F32); B_hst = k.buf("hst")
    CTb = [k.sb(f"CTb{i}", [128, 2, 257], BF16) for i in range(2)]; B_CTb = k.bufs("CTb", 2)
    Cst = k.sb("Cst", [128, 5, 2, 256], F32); B_Cst = k.bufs("Cst", 5)

    for h in range(4):
        memset("pool", Cst[:, h, :, :], 0.0, [B_Cst[h]])
    memset("pool", nTf_t[:, :, :], 0.0, [B_nTf])
    memset("pool", mprev["P"][:], 0.0, [B_mprev["P"]])
    memset("pool", vaug[:, :, :, 256:257], 1.0, B_vaug)
    memset("pool", halo[:], 0.0, B_halo)

    class WS:
        def __init__(self):
            self.q = []
            self.issued = 0
            self.occ = {}
            self.slots = []
            self.rcount = 0

        def plan(self, lst):
            self.q.extend(lst)

        def slot_of(self, i):
            while len(self.slots) <= i:
                j = len(self.slots) % 39
                if j in (18, 19):
                    self.slots.append(3 + (j - 18))
                else:
                    self.slots.append(self.rcount % 3)
                    self.rcount += 1
            return self.slots[i]

        def get(self, idx, hold=None):
            hold = idx if hold is None else hold
            while self.issued < len(self.q) and self.issued <= hold + 4:
                i = self.issued
                s = self.slot_of(i)
                prev = self.occ.get(s)
                if prev is not None and prev >= hold:
                    break
                if s >= 3 and not (hold // 39 == i // 39 and hold % 39 >= 16):
                    break
                ap, nk = self.q[i]
                j = i % 39
                tile_, Bs, extra = slot_tiles[s]
                if i < 39:
                    k.dma("pool", tile_[:, 0:nk, :], ap, writes=[Bs] + extra, sem=Bs)
                    k.dma("sp", wscr[j, :, 0:nk * 512], tile_[:, 0:nk, :].rearrange("p k c -> p (k c)"),
                          reads=[Bs] + extra, writes=[B_wscr[j]], sem=B_wscr[j])
                else:
                    k.dma("pool", tile_[:, 0:nk, :], wscr[j, :, 0:nk * 512].rearrange("p (k c) -> p k c", k=nk),
                          reads=[B_wscr[j]], writes=[Bs] + extra, sem=Bs)
                self.occ[s] = i
                self.issued += 1
            s = self.slot_of(idx)
            assert self.occ.get(s) == idx, (idx, s, self.occ)
            return slot_tiles[s][0], slot_tiles[s][1]

    slot_tiles = [(slabs[i_], B_slab[i_], []) for i_ in range(3)]
    slot_tiles.append((arena[:, 0:4096].rearrange("p (k c) -> p k c", k=8), k.buf("slotA"), B_ar[0:8]))
    slot_tiles.append((arena[:, 4096:8192].rearrange("p (k c) -> p k c", k=8), k.buf("slotB"), B_ar[8:16]))
    B_wscr = k.bufs("wscr", 39)
    ws = WS()
    wbase = [0]

    def block_slabs():
        lst = [(D["w_in"][s], 8) for s in range(16)]
        for s in range(2):
            lst += [(D["wpa"][s], 8), (D["wpb"][s], 8)]
        lst += [(D["wo"][s], 8) for s in range(2)]
        lst += [(D["wfi"][s], 8) for s in range(11)]
        for g in range(2):
            for q0 in (0, 8, 16):
                nk = min(8, 22 - q0)
                lst.append((D["wfo"][g, :, q0:q0 + nk, :], nk))
        return lst

    blocks = [("P", b_) for b_ in range(n_prompt_blocks)] + ([("S", 0)] if do_sample else [])
    for _ in blocks:
        ws.plan(block_slabs())
    ws.get(0)

    LG = {"P": 128, "S": 64}
    NBG = {"P": 1, "S": 16}
    MK = {}
    B_mk = k.buf("masks")
    am = k.sb("am", [128, 4, 128], F32); B_am = k.buf("am")
    wsst = am; B_wsst = B_am
    for G in ("P", "S"):
        L = LG[G]
        m = {}
        for nm in ("cmask", "nmT", "nm", "sell"):
            m[nm] = k.sb(f"{nm}{G}", [L, L], F32)
        m["blk"] = k.sb(f"blk{G}", [L, 16], F32)
        m["selc"] = k.sb(f"selc{G}", [L, 16], F32)
        m["wsT"] = k.sb(f"wsT{G}", [L, 4, L], BF16)
        MK[G] = m
        memset("pool", m["cmask"][:], 1.0, [B_mk]); memset("pool", m["nmT"][:], 0.0, [B_mk])
        memset("pool", m["nm"][:], 0.0, [B_mk]); memset("pool", m["sell"][:], 1.0, [B_mk])
        memset("pool", m["blk"][:], 1.0, [B_mk]); memset("pool", m["selc"][:], 1.0, [B_mk])
        k.dma("sp", wsst[0:L, :, 0:L], D["wsT" + G], writes=[B_wsst], sem=B_wsst)
        if G == "P":
            asel(m["cmask"][:], m["cmask"][:], [[1, 128]], ALU.is_ge, 0.0, 0, -1, [B_mk], [B_mk])
            asel(m["nmT"][:], m["nmT"][:], [[1, 128]], ALU.is_ge, NEG, 0, -1, [B_mk], [B_mk])
            asel(m["nm"][:], m["nm"][:], [[-1, 128]], ALU.is_ge, NEG, 0, 1, [B_mk], [B_mk])
            asel(m["sell"][:], m["sell"][:], [[0, 128]], ALU.is_equal, 0.0, -127, 1, [B_mk], [B_mk])
            asel(m["selc"][:], m["selc"][:], [[0, 16]], ALU.is_equal, 0.0, -127, 1, [B_mk], [B_mk])
            for g in range(4):
                asel(wsst[:, g, :], wsst[:, g, :], [[1, 128]], ALU.is_ge, 0.0, 0, -1, [B_wsst, B_mk], [B_wsst])
        else:
            def v3(t):
                return t[:].rearrange("p (b i) -> p b i", b=16)
            for nm, fill in (("cmask", 0.0), ("nmT", NEG)):
                asel(v3(m[nm]), v3(m[nm]), [[-4, 16], [0, 4]], ALU.is_ge, fill, 0, 1, [B_mk], [B_mk])
                asel(v3(m[nm]), v3(m[nm]), [[4, 16], [0, 4]], ALU.is_ge, fill, 3, -1, [B_mk], [B_mk])
                asel(v3(m[nm]), v3(m[nm]), [[4, 16], [1, 4]], ALU.is_ge, fill, 0, -1, [B_mk], [B_mk])
            asel(v3(m["nm"]), v3(m["nm"]), [[-4, 16], [0, 4]], ALU.is_ge, NEG, 0, 1, [B_mk], [B_mk])
            asel(v3(m["nm"]), v3(m["nm"]), [[4, 16], [0, 4]], ALU.is_ge, NEG, 3, -1, [B_mk], [B_mk])
            asel(v3(m["nm"]), v3(m["nm"]), [[-4, 16], [-1, 4]], ALU.is_ge, NEG, 0, 1, [B_mk], [B_mk])
            asel(v3(m["sell"]), v3(m["sell"]), [[-4, 16], [0, 4]], ALU.is_equal, 0.0, -3, 1, [B_mk], [B_mk])
            asel(m["blk"][:], m["blk"][:], [[-4, 16]], ALU.is_ge, 0.0, 0, 1, [B_mk], [B_mk])
            asel(m["blk"][:], m["blk"][:], [[4, 16]], ALU.is_ge, 0.0, 3, -1, [B_mk], [B_mk])
            asel(m["selc"][:], m["selc"][:], [[-4, 16]], ALU.is_equal, 0.0, -3, 1, [B_mk], [B_mk])
            for g in range(4):
                w3 = wsst[0:64, g, 0:64].rearrange("p (b i) -> p b i", b=16)
                asel(w3, w3, [[-4, 16], [0, 4]], ALU.is_ge, 0.0, 0, 1, [B_wsst, B_mk], [B_wsst])
                asel(w3, w3, [[4, 16], [0, 4]], ALU.is_ge, 0.0, 3, -1, [B_wsst], [B_wsst])
                asel(w3, w3, [[4, 16], [1, 4]], ALU.is_ge, 0.0, 0, -1, [B_wsst], [B_wsst])
        cp("dve", m["wsT"][:], wsst[0:L, :, 0:L], [B_wsst], [B_mk])

    rot = {"pj": 0, "raw": 0, "tmpf": 0, "tmpg": 0, "vnb": 0, "ST": 0, "vs": 0, "CTb": 0, "qTm": 0, "cs": 0, "vsm": 0, "pC": 0}

    def nxt(name, n):
        v = rot[name]
        rot[name] = (v + 1) % n
        return v

    pre_done = {}
    carry = {}
    mhalf = k.sb("mhalf", [128, 4], F32)
    memset("pool", mhalf[:], -0.5, [B_ones])

    def rsqrt_eps(out, in_, tmp, n, L, B):
        ts("pool", tmp, in_, EPS, 1.0, ALU.add, ALU.mult, B, B)
        tt("pool", out, tmp, mhalf[0:L, 0:n], ALU.pow, B + [B_ones], B)

    def rms_stats_g(src, Bsrc, L):
        for hh in range(2):
            op("dve", lambda e, hh=hh: e.bn_stats(out=rst[0:L, hh * 6:(hh + 1) * 6], in_=src[:, hh * 512:(hh + 1) * 512]),
               Bsrc, [B_st4])
        op("dve", lambda e: e.bn_aggr(out=st4[0:L, 4:6], in_=rst[0:L, 0:12]), [B_st4], [B_st4])
        stt(st4[0:L, 0:1], st4[0:L, 4:5], st4[0:L, 4:5], st4[0:L, 5:6], ALU.mult, ALU.add, [B_st4], [B_st4])
        rsqrt_eps(st4[0:L, 2:3], st4[0:L, 0:1], st4[0:L, 1:2], 1, L, [B_st4])

    def rmsnorm_to_hT_g(src, Bsrc, L, i, gname):
        rms_stats_g(src, Bsrc, L)
        yield; yield; yield
        stt(xn[0:L, :], src, st4[0:L, 2:3], bc_t[gname][0:L, :], ALU.mult, ALU.mult, Bsrc + [B_st4, B_bc[gname]], [B_xn])
        yield; yield
        for kc in range(8):
            tr(ptb[:, kc, 0:L], xn[0:L, kc * 128:(kc + 1) * 128], identb[0:L, 0:L], [B_xn, B_idb], [B_ptb])
        yield
        cp("act", hT[:, :, i * L:(i + 1) * L], ptb[:, :, 0:L], [B_ptb], [B_hT[i]])

    def prologue_gen(G, blk):
        L = LG[G]
        ntile = 4 if G == "P" else 1
        xin = D["xp"] if G == "P" else D["xs"]
        tok0 = blk * 512 if G == "P" else 0
        for i in range(ntile):
            k.dma("sp", xpre[0:L, i, :], xin[tok0 + i * L: tok0 + (i + 1) * L, :], writes=Bxpre(i), sem=Bxpre(i)[0])
        yield
        for i in range(ntile):
            for _ in rmsnorm_to_hT_g(xpre[0:L, i, :], Bxpre(i), L, i, "g1"):
                yield
            yield
        pre_done["key"] = (G, blk)

    def run_block(G, blk, nxt_blk=None):
        L = LG[G]; nb = NBG[G]; Ls = L // nb
        ntile = 4 if G == "P" else 1
        TB = ntile * L
        M = MK[G]
        xin = D["xp"] if G == "P" else D["xs"]
        yout = O["yp"] if G == "P" else O["ys"]
        tok0 = blk * 512 if G == "P" else 0
        last_prompt_blk = (G == "P" and blk == n_prompt_blocks - 1)
        w0 = wbase[0]
        wbase[0] += 39

        def cols(i):
            return slice(i * L, (i + 1) * L)

        def rms_stats(i):
            rms_stats_g(xt[0:L, i, :], [B_xt[i]], L)

        def rmsnorm_to_hT(i, gname):
            for _ in rmsnorm_to_hT_g(xt[0:L, i, :], [B_xt[i]], L, i, gname):
                pass

        hand = None
        if pre_done.get("key") == (G, blk):
            hand = carry.pop("gen")
        else:
            for i in range(ntile):
                k.dma("sp", xt[0:L, i, :], xin[tok0 + i * L: tok0 + (i + 1) * L, :], writes=[B_xt[i]], sem=B_xt[i])
            for i in range(ntile):
                rmsnorm_to_hT(i, "g1")

        def fm_proj(slab_idx, nchunk, rhs, Rrhs, evac):
            W, Bw = ws.get(slab_idx)
            for cc in range(nchunk):
                b = nxt("pj", 4)
                for kc in range(8):
                    mm(pj[b][:, 0:TB], W[:, kc, cc * 128:(cc + 1) * 128], rhs[:, kc, 0:TB], kc == 0, kc == 7,
                       [Bw] + Rrhs, [B_pj[b]])
                tick()
                evac(cc, pj[b], B_pj[b])

        def tm_proj(slab_idx, lhs, Blhs_of_tile, evac):
            W, Bw = ws.get(slab_idx)
            for i in range(ntile):
                b = nxt("pj", 4)
                for kc in range(8):
                    mm(pj[b][0:L, :], lhs[:, kc, cols(i)], W[:, kc, :], kc == 0, kc == 7,
                       [Bw, Blhs_of_tile(i)], [B_pj[b]])
                tick()
                evac(i, pj[b], B_pj[b])

        if G == "S":
            k.dma("sp", mprev["S"][:], D["smr"], writes=[B_mprev["S"]], sem=B_mprev["S"])
        mp = mprev[G]; Bmp = B_mprev[G]

        def bcast_diag(vec4, R):
            tt("dve", diag[0:L, :, 0:L], identf[0:L, 0:L].unsqueeze(1).to_broadcast([L, 4, L]),
               vec4.unsqueeze(2).to_broadcast([L, 4, L]), ALU.mult, [B_idf] + R, [B_diag])

        def bcast_mm():
            for h in range(4):
                mm(pg[0:L, h * 128:h * 128 + L], onesf[0:L, 0:L], diag[0:L, h, 0:L], True, True, [B_ones, B_diag], [B_pg])
            return pg[:, :].rearrange("p (h t) -> p h t", h=4)[0:L, :, 0:L]

        def gates_gen():
            for _ in range(2 if hand is not None else 0):
                yield
            for i in range(ntile):
                ci = cols(i)
                Ex = Ex_t[:, i, :]; BE = B_Ex[i]
                for kc in range(8):
                    mm(pg[0:L, 0:8], hT[:, kc, ci], wif[:, kc, :], kc == 0, kc == 7, [B_hT[i], B_wif], [B_pg])
                yield
                tt("dve", gi[0:L, :], pg[0:L, 0:8], bif[0:L, :], ALU.add, [B_pg, B_sm], [B_g])
                act(lf[0:L, 0:4], gi[0:L, 4:8], AF.Abs, [B_g], [B_g])
                act(lf[0:L, 4:8], lf[0:L, 0:4], AF.Exp, [B_g], [B_g], scale=-1.0)
                act(lf[0:L, 8:12], lf[0:L, 4:8], AF.Ln, [B_g], [B_g], bias=1.0)
                ts("dve", lf[0:L, 0:4], gi[0:L, 4:8], 0.0, None, ALU.min, None, [B_g], [B_g])
                tt("dve", lf[0:L, 12:16], lf[0:L, 0:4], lf[0:L, 8:12], ALU.subtract, [B_g], [B_g])
                yield
                mm(pg[0:L, 16:20], M["cmask"][:, :], lf[0:L, 12:16], True, True, [B_mk, B_g], [B_pg])
                yield
                cp("dve", gb8[0:L, 4:8], pg[0:L, 16:20], [B_pg], [B_g])
                tt("dve", aa[0:L, 0:4], gi[0:L, 0:4], gb8[0:L, 4:8], ALU.subtract, [B_g], [B_g])
                ts("dve", aa[0:L, 4:8], aa[0:L, 0:4], -LN16, None, ALU.add, None, [B_g], [B_g])
                bcast_diag(aa[0:L, 0:4], [B_g])
                yield
                pr = bcast_mm()
                yield
                tt("dve", am[0:L, :, 0:L], pr, M["nm"][:, :].unsqueeze(1).to_broadcast([L, 4, L]), ALU.add, [B_pg, B_mk], [B_am])
                op("dve", lambda e: e.tensor_reduce(out=gb8[0:L, 0:4], in_=am[0:L, :, 0:L], axis=AX.X, op=ALU.max), [B_am], [B_g])
                tt("dve", gb8[0:L, 0:4], gb8[0:L, 0:4], mp[0:L, :], ALU.max, [B_g, Bmp], [B_g])
                yield
                mm(pg[0:L, 0:8], M["sell"][:, :], gb8[0:L, 0:8], True, True, [B_mk, B_g], [B_pg])
                yield
                cp("dve", gl8[0:L, :], pg[0:L, 0:8], [B_pg], [B_g])
                tt("dve", mnd[0:L, 0:4], gl8[0:L, 0:4], gl8[0:L, 4:8], ALU.add, [B_g], [B_g])
                tt("dve", Dx[0:L, 0:4], mp[0:L, :], gb8[0:L, 0:4], ALU.subtract, [B_g, Bmp], [B_g])
                tt("dve", Dx[0:L, 4:8], gb8[0:L, 0:4], gb8[0:L, 4:8], ALU.add, [B_g], [B_g])
                ts("dve", Dx[0:L, 4:8], Dx[0:L, 4:8], -1.0, None, ALU.mult, None, [B_g], [B_g])
                tt("dve", Dx[0:L, 8:12], mp[0:L, :], gl8[0:L, 0:4], ALU.subtract, [B_g, Bmp], [B_g])
                tt("dve", Dx[0:L, 12:16], aa[0:L, 4:8], gl8[0:L, 0:4], ALU.subtract, [B_g], [B_g])
                act(Ex[0:L, :], Dx[0:L, :], AF.Exp, [B_g], [BE])
                cp("dve", mnd[0:L, 4:8], Ex[0:L, 8:12], [BE], [B_g])
                if G == "P":
                    cp("dve", mp[0:L, :], mnd[0:L, 0:4], [B_g], [Bmp])
                bcast_diag(gb8[0:L, 0:4], [B_g])
                yield
                pr = bcast_mm()
                yield
                stt(am[0:L, :, 0:L], pr, -1.0, M["nmT"][:, :].unsqueeze(1).to_broadcast([L, 4, L]), ALU.mult, ALU.add,
                    [B_pg, B_mk], [B_am])
                for h in range(4):
                    act(DTb[0:L, i, h, 0:L], am[0:L, h, 0:L], AF.Exp, [B_am, B_g], [B_DT[i]], bias=aa[0:L, 4 + h:5 + h])
                mm(pg[0:nb, 0:8], M["selc"][:, 0:nb], mnd[0:L, 0:8], True, True, [B_mk, B_g], [B_pg])
                yield
                cp("dve", cm8_t[0:nb, i, :], pg[0:nb, 0:8], [B_pg], [B_cm[i]])
                tt("dve", decx[0:L, 0:nb, :], mnd[0:L, 4:8].unsqueeze(1).to_broadcast([L, nb, 4]),
                   M["selc"][:, 0:nb].unsqueeze(2).to_broadcast([L, nb, 4]), ALU.mult, [B_g, B_mk], [B_decx])
                yield
                mm(pg[:, 0:nb * 4], onesf[0:L, 0:128], decx[0:L, 0:nb, :].rearrange("p b h -> p (b h)"), True, True,
                   [B_ones, B_decx], [B_pg])
                yield
                cp("dve", decb_t[:, i, 0:nb * 4], pg[:, 0:nb * 4], [B_pg], [B_dec[i]])
                tt("dve", wsel_t[0:L, i, :, 0:nb], Ex[0:L, 12:16].unsqueeze(2).to_broadcast([L, 4, nb]),
                   M["blk"][:, 0:nb].unsqueeze(1).to_broadcast([L, 4, nb]), ALU.mult, [BE, B_mk], [B_wsel[i]])
                yield

        bgq = [gates_gen()]
        bgq2 = []
        bgq3 = [hand] if hand is not None else []

        def _adv(q):
            while q:
                try:
                    next(q[0])
                    return
                except StopIteration:
                    q.pop(0)

        tickn = [0]

        def tick():
            _adv(bgq3)
            _adv(bgq)
            tickn[0] += 1
            if tickn[0] % 3 == 0:
                _adv(bgq2)

        def drain():
            while bgq:
                _adv(bgq)

        def drain2():
            while bgq2:
                _adv(bgq2)

        for s in range(2):
            def ev_u(cc, p, Bp, s=s):
                c = s * 4 + cc
                act(U8[:, c, 0:TB], p[:, 0:TB], AF.Gelu, [Bp], [B_U8[c]])
            fm_proj(w0 + s, 4, hT, B_hT[0:ntile], ev_u)
        for s in range(2):
            def ev_v(i, p, Bp, s=s):
                act(gv[0:L, i, s * 512:(s + 1) * 512], p[0:L, :], AF.Gelu, [Bp], B_ar[4 * i:4 * i + 4])
            tm_proj(w0 + 2 + s, hT, lambda i: B_hT[i], ev_v)
        vb_of = {}

        def ln_part(i):
            Bgv = B_ar[4 * i:4 * i + 4]
            for hh in range(2):
                op("dve", lambda e, hh=hh, i=i: e.bn_stats(out=hst[0:L, hh * 6:(hh + 1) * 6], in_=gv[0:L, i, hh * 512:(hh + 1) * 512]),
                   Bgv, [B_hst])
            op("dve", lambda e: e.bn_aggr(out=hst[0:L, 12:14], in_=hst[0:L, 0:12]), [B_hst], [B_hst])
            rsqrt_eps(hst[0:L, 15:16], hst[0:L, 13:14], hst[0:L, 14:15], 1, L, [B_hst])
            stt(hst[0:L, 14:15], hst[0:L, 12:13], -1.0, hst[0:L, 15:16], ALU.mult, ALU.mult, [B_hst], [B_hst])
            act(vnf[0:L, :], gv[0:L, i, :], AF.Identity, Bgv + [B_hst], [B_vnf],
                bias=hst[0:L, 14:15], scale=hst[0:L, 15:16])
            tt("dve", vnf[0:L, :], vnf[0:L, :], bc_t["lng"][0:L, :], ALU.mult, [B_vnf, B_bc["lng"]], [B_vnf])
            vb = nxt("vnb", 2)
            vb_of[i] = vb
            if G == "S":
                tt("dve", vnf[0:L, :], vnf[0:L, :], bc_t["lnb"][0:L, :], ALU.add, [B_vnf, B_bc["lnb"]], [B_vnf])
                k.dma("sp", O["vs"], vnf[0:L, :], reads=[B_vnf], sem=B_vnf)
                cp("pool", vnb[vb][0:L, :], vnf[0:L, :], [B_vnf], [B_vnb[vb]])
            else:
                tt("dve", vnb[vb][0:L, :], vnf[0:L, :], bc_t["lnb"][0:L, :], ALU.add, [B_vnf, B_bc["lnb"]], [B_vnb[vb]])

        def spatial_part(i):
            vb = vb_of[i]
            for half in range(2):
                b = nxt("pj", 4)
                for c4 in range(4):
                    kc = half * 4 + c4
                    g = kc // 2
                    mm(pj[b][:, c4 * 128:c4 * 128 + L], vnb[vb][0:L, kc * 128:(kc + 1) * 128], M["wsT"][:, g, :],
                       True, False, [B_vnb[vb], B_mk], [B_pj[b]])
                    mm(pj[b][:, c4 * 128:c4 * 128 + L], ones2[0:2, 0:128], bs2[G][0:2, g * L:(g + 1) * L],
                       False, True, [B_ones, B_bs], [B_pj[b]])
                tick()
                pv = pj[b][:, :].rearrange("p (c t) -> p c t", c=4)[:, :, 0:L]
                tt("dve", aoT[:, half * 4:half * 4 + 4, cols(i)], pv, U8[:, half * 4:half * 4 + 4, cols(i)], ALU.mult,
                   [B_pj[b]] + B_U8[half * 4:half * 4 + 4], [B_ao[i]])

        special_tm = (G == "S") or last_prompt_blk
        qk_state = {}

        def qk_stageA(c):
            s, cc = divmod(c, 4)
            W, Bw = ws.get(w0 + 4 + s)
            if G == "S" and cc == 0:
                k.dma("sp", scst[0:48, 0:512], D["sconv"][:, s * 512:(s + 1) * 512], writes=[B_scst], sem=B_scst)
            b = nxt("pj", 4)
            for kc in range(8):
                mm(pj[b][:, 0:TB], W[:, kc, cc * 128:(cc + 1) * 128], hT[:, kc, 0:TB], kc == 0, kc == 7,
                   [Bw] + B_hT[0:ntile], [B_pj[b]])
            tick()
            r = nxt("raw", 2)
            if G == "P":
                rv = raw[r][:, 0:515].rearrange("p (b t) -> p b t", b=1)
                cp("pool", raw[r][:, 0:3], halo[:, c, :], [B_halo[c]], [B_raw[r]])
                cp("act", raw[r][:, 3:515], pj[b][:, 0:512], [B_pj[b]], [B_raw[r]])
                cp("pool", halo[:, c, :], raw[r][:, 512:515], [B_raw[r]], [B_halo[c]])
                T = 512
            else:
                rv = raw[r][:, 0:112].rearrange("p (b t) -> p b t", b=16)
                tr(pc[:, 0:48], scst[0:48, cc * 128:(cc + 1) * 128], identf[0:48, 0:48], [B_scst, B_idf], [B_pc])
                cp("dve", rv[:, :, 0:3], pc[:, 0:48].rearrange("p (b j) -> p b j", b=16), [B_pc], [B_raw[r]])
                cp("act", rv[:, :, 3:7], pj[b][:, 0:64].rearrange("p (b t) -> p b t", b=16), [B_pj[b]], [B_raw[r]])
                T = 4
            act(cacc[r][:, 0:TB], pj[b][:, 0:TB], AF.Identity, [B_pj[b], B_sm], [B_cacc[r]],
                bias=convb[:, c:c + 1], scale=convw[:, c, 3:4])
            qk_state[c] = (r, rv, T)

        def qk_stageB(c):
            r, rv, T = qk_state[c]
            ca = cacc[r][:, 0:nb * T].rearrange("p (b t) -> p b t", b=nb)
            for j in range(3):
                stt(ca, rv[:, :, j:j + T], convw[:, c, j:j + 1], ca, ALU.mult, ALU.add,
                    [B_raw[r], B_sm, B_cacc[r]], [B_cacc[r]])
            act(qkT[:, c, 0:TB], cacc[r][:, 0:TB], AF.Silu, [B_cacc[r]], [B_ar[c]])

        def qk_special(s):
            W, Bw = ws.get(w0 + 4 + s)
            it = ntile - 1
            b = nxt("pj", 4)
            for kc in range(8):
                mm(pj[b][0:L, :], hT[:, kc, cols(it)], W[:, kc, :], kc == 0, kc == 7, [Bw, B_hT[it]], [B_pj[b]])
            f = nxt("tmpf", 2)
            cp("act", tmpf[f][0:L, :], pj[b][0:L, :], [B_pj[b]], [B_tmpf[f]])
            if G == "P":
                k.dma("sp", O["convp"][:, s * 512:(s + 1) * 512], tmpf[f][125:128, :], reads=[B_tmpf[f]], sem=B_tmpf[f])
            else:
                k.dma("sp", scr[:, s * 512:(s + 1) * 512], tmpf[f][0:64, :], reads=[B_tmpf[f]], writes=[B_scr],
                      sem=B_tmpf[f])

        ln_part(0)
        qk_stageA(0)
        for c in range(16):
            s, cc = divmod(c, 4)
            if cc == 3 and special_tm:
                qk_special(s)
            if c + 1 < 16:
                if cc == 3 and s + 1 < ntile:
                    ln_part(s + 1)
                qk_stageA(c + 1)
            qk_stageB(c)
            if cc == 3 and s < ntile:
                spatial_part(s)
        if G == "S":
            k.dma("sp", O["convs"], scr.rearrange("(b t) c -> b t c", t=4)[:, 1:4, :], reads=[B_scr], sem=B_scr)
        for s in range(2):
            def ev_vm(i, p, Bp, s=s):
                cp("act", vaug[0:L, i, 2 * s:2 * s + 2, 0:256], p[0:L, :].rearrange("p (h d) -> p h d", h=2), [Bp], [B_vaug[i]])
            tm_proj(w0 + 8 + s, hT, lambda i: B_hT[i], ev_vm)
        for s in range(2):
            def ev_o(cc, p, Bp, s=s):
                c = s * 4 + cc
                act(U8[:, c, 0:TB], p[:, 0:TB], AF.Sigmoid, [Bp], [B_U8[c]])
            fm_proj(w0 + 10 + s, 4, hT, B_hT[0:ntile], ev_o)

        while bgq3:
            _adv(bgq3)
        drain()

        def gagb_gen():
            ctr = 0
            for so in range(4):
                W, Bw = ws.get(w0 + 12 + so)
                dst, Bdst, boff = (sga, B_sga, 0) if so < 2 else (sgb, B_sgb, 8)
                for cc in range(4):
                    c = (so % 2) * 4 + cc
                    b = 2 + ctr % 2
                    ctr += 1
                    for kc in range(8):
                        mm(pj[b][:, 0:TB], W[:, kc, cc * 128:(cc + 1) * 128], hT[:, kc, 0:TB], kc == 0, kc == 7,
                           [Bw] + B_hT[0:ntile], [B_pj[b]])
                        if kc == 3:
                            yield
                    yield
                    act(dst[:, c, 0:TB], pj[b][:, 0:TB], AF.Sigmoid, [B_pj[b], B_sm], [Bdst[c]], bias=bgate[:, boff + c:boff + c + 1])

        bgq2.append(gagb_gen())
        nTf = nTf_t

        def hnorm_gen(i, nbuf, Bnb, hsx, Bhs, Ex, BE, ci):
            tt("dve", hsx[0:L, 0:4], hsx[0:L, 0:4], Ex[0:L, 4:8], ALU.max, [Bhs, BE], [Bhs])
            op("dve", lambda e: e.reciprocal(out=hsx[0:L, 4:8], in_=hsx[0:L, 0:4]), [Bhs], [Bhs])
            yield
            for h in range(4):
                op("dve", lambda e, h=h: e.bn_stats(out=hsx[0:L, 8 + 6 * h:14 + 6 * h], in_=nbuf[0:L, h * 256:(h + 1) * 256]),
                   Bnb, [Bhs])
                op("dve", lambda e, h=h: e.bn_aggr(out=hsx[0:L, 32 + 2 * h:34 + 2 * h], in_=hsx[0:L, 8 + 6 * h:14 + 6 * h]),
                   [Bhs], [Bhs])
                if h % 2 == 1:
                    yield
            mv = hsx[0:L, 32:40].rearrange("p (h t) -> p h t", t=2)
            tt("dve", hsx[0:L, 40:44], hsx[0:L, 4:8], hsx[0:L, 4:8], ALU.mult, [Bhs], [Bhs])
            tt("dve", hsx[0:L, 40:44], hsx[0:L, 40:44], mv[:, :, 1], ALU.mult, [Bhs], [Bhs])
            rsqrt_eps(hsx[0:L, 40:44], hsx[0:L, 40:44], hsx[0:L, 40:44], 4, L, [Bhs])
            yield; yield
            tt("dve", hsx[0:L, 44:48], hsx[0:L, 40:44], hsx[0:L, 4:8], ALU.mult, [Bhs], [Bhs])
            stt(hsx[0:L, 0:4], mv[:, :, 0], -1.0, hsx[0:L, 44:48], ALU.mult, ALU.mult, [Bhs], [Bhs])
            yield
            for h in range(4):
                act(hmb[0:L, h * 256:(h + 1) * 256], nbuf[0:L, h * 256:(h + 1) * 256], AF.Identity, Bnb + [Bhs], [B_hmb],
                    bias=hsx[0:L, h:h + 1], scale=hsx[0:L, 44 + h:45 + h])
            yield; yield; yield; yield; yield
            for kc in range(8):
                tr(ptb[:, kc, 0:L], hmb[0:L, kc * 128:(kc + 1) * 128], identb[0:L, 0:L], [B_hmb, B_idb], [B_ptb])
            yield
            for kc in range(8):
                stt(boT[:, kc, ci], ptb[:, kc, 0:L], hng[:, kc:kc + 1], U8[:, kc, ci], ALU.mult, ALU.mult,
                    [B_ptb, B_sm, B_U8[kc]], [B_bo[i]])
            yield
        if G == "S":
            k.dma("sp", vnf[0:16, :], D["sn"], writes=[B_vnf], sem=B_vnf)
            for j in range(8):
                tr(pc[:, j * 16:j * 16 + nb], vnf[0:nb, j * 128:(j + 1) * 128], identf[0:nb, 0:nb], [B_vnf, B_idf], [B_pc])
            cp("dve", nTf[:, :, 0:nb], pc[:, 0:128].rearrange("p (j b) -> p j b", j=8)[:, :, 0:nb], [B_pc], [B_nTf])
        for i in range(ntile):
            ci = cols(i)
            Ex = Ex_t[:, i, :]
            BE = B_Ex[i]
            if i % 2 == 0:
                nbuf = vnf; Bnb = [B_vnf]
            else:
                nbuf = arena[:, 8192:10240].bitcast(F32); Bnb = B_ar[16:20]
            hsx = hs2[i % 2]; Bhs = B_hs2[i % 2]
            tt("dve", diag[0:L, :, 0:L], identf[0:L, 0:L].unsqueeze(1).to_broadcast([L, 4, L]),
               Ex[0:L, 0:4].unsqueeze(2).to_broadcast([L, 4, L]), ALU.mult, [B_idf, BE], [B_diag])
            for h in range(4):
                mm(pc[:, h * 128:h * 128 + L], onesf[0:L, 0:128], diag[0:L, h, 0:L], True, True, [B_ones, B_diag], [B_pc])
            pr = pc[:, :].rearrange("p (h t) -> p h t", h=4)[:, :, 0:L]
            tt("dve", qTs[:, :, 0:L].rearrange("p (h d) t -> p h d t", h=4),
               qkT[:, 0:8, ci].rearrange("p (h d) t -> p h d t", h=4),
               pr.unsqueeze(2).to_broadcast([128, 4, 2, L]), ALU.mult, [B_pc] + B_ar[0:8], [B_qTs])
            for kc in range(8):
                tr(ptb[0:L, kc, :], qkT[:, 8 + kc, ci], identb[:, :], [B_ar[8 + kc], B_idb], [B_ptb])
            cp("act", ktm[0:L, :], ptb[0:L, :, :].rearrange("p c d -> p (c d)"), [B_ptb], [B_ktm])
            ptbf = ptb[:, :, :].rearrange("p a b -> p (a b)").bitcast(F32)
            pCb = [(pg, B_pg), (ptbf, B_ptb)]
            hstate = {}

            def c_src(h, b):
                if G == "P":
                    return Cst[:, h, :, :], B_Cst[h]
                it = h * nb + b
                return Cst[:, it % 5, :, :], B_Cst[it % 5]

            def c_load(it):
                if G == "S" and it < 4 * nb:
                    h_, b_ = divmod(it, nb)
                    Cn_, BCn_ = c_src(h_, b_)
                    k.dma("sp", Cn_, D["sC"][b_, h_].rearrange("(vh p) d -> p vh d", vh=2), writes=[BCn_], sem=BCn_)

            def c_store(it):
                if G == "S" and 0 <= it < 4 * nb:
                    h_, b_ = divmod(it, nb)
                    Cn_, BCn_ = c_src(h_, b_)
                    k.dma("sp", O["Cs"][b_, h_].rearrange("(vh p) d -> p vh d", vh=2), Cn_, reads=[BCn_], sem=BCn_)

            def c_transposes(h, b):
                Cn, BCn = c_src(h, b)
                pcb, Bpcb = pCb[nxt("pC", 2)]
                pC = pcb[:, :].rearrange("p (dh v) -> p dh v", dh=2)
                for vh in range(2):
                    for dh in range(2):
                        tr(pC[:, dh, vh * 128:(vh + 1) * 128], Cn[:, vh, dh * 128:(dh + 1) * 128], identf[:, :],
                           [BCn, B_idf], [Bpcb])
                cb = nxt("CTb", 2)
                cp("act", CTb[cb][:, :, 0:256], pC, [Bpcb], [B_CTb[cb]])
                cp("pool", CTb[cb][:, :, 256:257], nTf[:, 2 * h:2 * h + 2, b:b + 1], [B_nTf], [B_CTb[cb]])
                return cb

            def stage1(h):
                for dh in range(2):
                    mm(pnum[0:L, 0:L], qkT[:, 8 + 2 * h + dh, ci], qkT[:, 2 * h + dh, ci], dh == 0, dh == 1,
                       [B_ar[8 + 2 * h + dh], B_ar[2 * h + dh]], [B_pnum])
                sb_ = nxt("ST", 2)
                tt("dve", STb[sb_][0:L, 0:L], pnum[0:L, 0:L], DTb[0:L, i, h, 0:L], ALU.mult, [B_pnum, B_DT[i]], [B_ST[sb_]])
                vb_ = nxt("vs", 2)
                ts("pool", vsb[vb_][0:L, :], vaug[0:L, i, h, :], Ex[0:L, 12 + h:13 + h], 0.0, ALU.mult, ALU.add,
                   [B_vaug[i], BE], [B_vs[vb_]])
                tick()
                cb0 = c_transposes(h, 0)
                tick()
                hstate[h] = (sb_, vb_, cb0)

            def stage2(h):
                sb_, vb_, cb0 = hstate[h]
                numh = pj[h % 2]; Bnum = B_pj[h % 2]
                mm(numh[0:L, 0:257], STb[sb_][0:L, 0:L], vaug[0:L, i, h, :], True, False, [B_ST[sb_], B_vaug[i]], [Bnum])
                if nb > 1:
                    for dh in range(2):
                        cp("pool", qTm_diag[dh], qTs[:, 2 * h + dh, 0:L].rearrange("p (b j) -> p b j", b=nb),
                           [B_qTs], [B_qTm[dh]])
                cb = cb0
                for b in range(nb):
                    it = h * nb + b
                    Cn, BCn = c_src(h, b)
                    if nb > 1:
                        c_store(it - 2)
                        c_load(it + 2)
                    if nb > 1:
                        vm_ = nxt("vsm", 2)
                        ts("pool", vsm[vm_][0:L, :], vsb[vb_][0:L, 0:256], M["blk"][:, b:b + 1], 0.0, ALU.mult, ALU.add,
                           [B_vs[vb_], B_mk], [B_vsm[vm_]])
                    cb_next = c_transposes(h, b + 1) if b + 1 < nb else None
                    for dh in range(2):
                        if nb == 1:
                            lhs = qTs[:, 2 * h + dh, 0:L]; Rq = [B_qTs]
                        else:
                            lhs = qTm[dh][:, b, 0:L]; Rq = [B_qTm[dh]]
                        mm(numh[0:L, 0:257], lhs, CTb[cb][:, dh, :], False, (b == nb - 1 and dh == 1), Rq + [B_CTb[cb]], [Bnum])
                    for vh in range(2):
                        lv = vsb[vb_][0:L, vh * 128:(vh + 1) * 128] if nb == 1 else vsm[vm_][0:L, vh * 128:(vh + 1) * 128]
                        mm(pc[:, vh * 256:(vh + 1) * 256], lv, ktm[0:L, h * 256:(h + 1) * 256], True, True,
                           [B_vs[vb_] if nb == 1 else B_vsm[vm_], B_ktm], [B_pc])
                    Cflat = Cn.rearrange("p vh d -> p (vh d)")
                    stt(Cflat, Cflat, decb_t[:, i, b * 4 + h:b * 4 + h + 1], pc[:, :], ALU.mult, ALU.add, [BCn, B_dec[i], B_pc], [BCn])
                    cb = cb_next
                    if nb == 1:
                        tick()
                pn = numh[:, 384:416].rearrange("p (dh b) -> p dh b", dh=2)[:, :, 0:nb]
                for dh in range(2):
                    mm(numh[:, 384 + dh * 16:384 + dh * 16 + nb], ktm[0:L, h * 256 + dh * 128:h * 256 + (dh + 1) * 128],
                       wsel_t[0:L, i, h, 0:nb], True, True, [B_ktm, B_wsel[i]], [Bnum])
                nview = nTf[:, 2 * h:2 * h + 2, 0:nb]
                if nb == 1:
                    stt(nview, nview, decb_t[:, i, h:h + 1], pn, ALU.mult, ALU.add, [B_nTf, B_dec[i], Bnum], [B_nTf])
                else:
                    dview = decb_t[:, i, :].rearrange("p (b h) -> p b h", h=4)[:, 0:nb, h]
                    tt("dve", nview, nview, dview.unsqueeze(1).to_broadcast([128, 2, nb]), ALU.mult, [B_nTf, B_dec[i]], [B_nTf])
                    tt("dve", nview, nview, pn, ALU.add, [B_nTf, Bnum], [B_nTf])
                act(hsx[0:L, h:h + 1], numh[0:L, 256:257], AF.Abs, [Bnum], [Bhs])
                cp("act", nbuf[0:L, h * 256:(h + 1) * 256], numh[0:L, 0:256], [Bnum], Bnb)
                tick()

            if G == "S":
                c_load(0); c_load(1)
            stage1(0)
            for h in range(4):
                if nb == 1 and h + 1 < 4:
                    stage1(h + 1)
                stage2(h)
                if nb > 1 and h + 1 < 4:
                    stage1(h + 1)
            if G == "S":
                c_store(4 * nb - 2); c_store(4 * nb - 1)
            bgq.append(hnorm_gen(i, nbuf, Bnb, hsx, Bhs, Ex, BE, ci))
            if i == ntile - 1:
                drain()
            if G == "S" or (last_prompt_blk and i == ntile - 1):
                k.dma("sp", O["ms"] if G == "S" else O["mp"], cm8_t[0:nb, i, 0:4], reads=[B_cm[i]], sem=B_cm[i])
                for half in range(2):
                    pb_ = pg if half == 0 else pc
                    Bpb = B_pg if half == 0 else B_pc
                    for jj in range(4):
                        j = half * 4 + jj
                        tr(pb_[0:nb, jj * 128:(jj + 1) * 128], nTf[:, j, 0:nb], identf[:, :], [B_nTf, B_idf], [Bpb])
                    cp("dve", vnf[0:nb, half * 512:(half + 1) * 512], pb_[0:nb, :], [Bpb], [B_vnf])
                k.dma("sp", O["ns"] if G == "S" else O["np"], vnf[0:nb, :], reads=[B_vnf], sem=B_vnf)
                if G == "P":
                    for h in range(4):
                        k.dma("sp", O["Cp"][h].rearrange("(vh p) d -> p vh d", vh=2), Cst[:, h, :, :], reads=[B_Cst[h]],
                              sem=B_Cst[h])

        if debug and G == "P" and blk == 0:
            k.dma("sp", O["dbg_ao"], aoT[:, :, :], reads=B_ao, sem=B_ao[0])
            k.dma("sp", O["dbg_bo"], boT[:, :, :], reads=B_bo, sem=B_bo[0])
            k.dma("sp", O["dbg_qk"], qkT, reads=B_ar[0:16], sem=B_ar[0])
        drain2()
        for s in range(2):
            Wa, Ba = ws.get(w0 + 16 + 2 * s)
            Wb, Bb = ws.get(w0 + 17 + 2 * s, hold=w0 + 16 + 2 * s)
            for cc in range(4):
                c = s * 4 + cc
                b1 = nxt("pj", 4)
                for kc in range(8):
                    mm(pj[b1][:, 0:TB], Wa[:, kc, cc * 128:(cc + 1) * 128], aoT[:, kc, 0:TB], kc == 0, kc == 7,
                       [Ba] + B_ao[0:ntile], [B_pj[b1]])
                b2 = nxt("pj", 4)
                for kc in range(8):
                    mm(pj[b2][:, 0:TB], Wb[:, kc, cc * 128:(cc + 1) * 128], boT[:, kc, 0:TB], kc == 0, kc == 7,
                       [Bb] + B_bo[0:ntile], [B_pj[b2]])
                f = nxt("tmpf", 2)
                g_ = nxt("tmpg", 2)
                tt("dve", tmpf[f][:, 0:TB], pj[b1][:, 0:TB], sga[:, c, 0:TB], ALU.mult, [B_pj[b1], B_sga[c]], [B_tmpf[f]])
                tt("dve", tmpg[g_][:, 0:TB], pj[b2][:, 0:TB], sgb[:, c, 0:TB], ALU.mult, [B_pj[b2], B_sgb[c]], [B_tmpg[g_]])
                tt("pool", U8[:, c, 0:TB], tmpf[f][:, 0:TB], tmpg[g_][:, 0:TB], ALU.add, [B_tmpf[f], B_tmpg[g_]], [B_U8[c]])
        if debug and G == "P" and blk == 0:
            k.dma("sp", O["dbg_mg"], U8[:, :, :], reads=B_U8, sem=B_U8[0])
        Wo0, Bwo0 = ws.get(w0 + 20)
        Wo1, Bwo1 = ws.get(w0 + 21, hold=w0 + 20)
        for i in range(ntile):
            for s, (W, Bw) in enumerate(((Wo0, Bwo0), (Wo1, Bwo1))):
                b = nxt("pj", 4)
                for kc in range(8):
                    mm(pj[b][0:L, :], U8[:, kc, cols(i)], W[:, kc, :], kc == 0, kc == 7, [Bw, B_U8[kc]], [B_pj[b]])
                    if kc in (3, 7):
                        tick()
                tt("dve", xt[0:L, i, s * 512:(s + 1) * 512], xt[0:L, i, s * 512:(s + 1) * 512], pj[b][0:L, :], ALU.add,
                   [B_xt[i], B_pj[b]], [B_xt[i]])
            bgq.append(rmsnorm_to_hT_g(xt[0:L, i, :], [B_xt[i]], L, i, "g2"))
        drain()
        for s in range(11):
            W, Bw = ws.get(w0 + 22 + s)
            for jj in range(2):
                j = 2 * s + jj
                b1 = nxt("pj", 4)
                for kc in range(8):
                    mm(pj[b1][:, 0:TB], W[:, kc, jj * 128:(jj + 1) * 128], hT[:, kc, 0:TB], kc == 0, kc == 7,
                       [Bw] + B_hT[0:ntile], [B_pj[b1]])
                b2 = nxt("pj", 4)
                for kc in range(8):
                    mm(pj[b2][:, 0:TB], W[:, kc, 256 + jj * 128:256 + (jj + 1) * 128], hT[:, kc, 0:TB], kc == 0, kc == 7,
                       [Bw] + B_hT[0:ntile], [B_pj[b2]])
                f = nxt("tmpf", 2)
                act(tmpf[f][:, 0:TB], pj[b1][:, 0:TB], AF.Silu, [B_pj[b1]], [B_tmpf[f]])
                tt("dve", gT[:, j, 0:TB], tmpf[f][:, 0:TB], pj[b2][:, 0:TB], ALU.mult, [B_tmpf[f], B_pj[b2]], [B_ar[j]])
        if nxt_blk is not None:
            bgq.append(prologue_gen(*nxt_blk))
        for g in range(2):
            for qi, q0 in enumerate((0, 8, 16)):
                nk = min(8, 22 - q0)
                W, Bw = ws.get(w0 + 33 + g * 3 + qi)
                for i in range(ntile):
                    for kk in range(nk):
                        kc = q0 + kk
                        mm(pj[i][0:L, :], gT[:, kc, cols(i)], W[:, kk, :], kc == 0, kc == 21, [Bw, B_ar[kc]], [B_pj[i]])
                    tick()
            if g == 1:
                drain()
            for i in range(ntile):
                tt("dve", xt[0:L, i, g * 512:(g + 1) * 512], xt[0:L, i, g * 512:(g + 1) * 512], pj[i][0:L, :], ALU.add,
                   [B_xt[i], B_pj[i]], [B_xt[i]])
        def final_gen():
            if nxt_blk is not None:
                Ln = LG[nxt_blk[0]]
                ntn = 4 if nxt_blk[0] == "P" else 1
            else:
                Ln = 0; ntn = 0
            for i in range(ntile):
                rms_stats_g(xt[0:L, i, :], [B_xt[i]], L)
                yield
                stt(xt[0:L, i, :], xt[0:L, i, :], st4[0:L, 2:3], bc_t["gf"][0:L, :], ALU.mult, ALU.mult,
                    [B_xt[i], B_st4, B_bc["gf"]], [B_xt[i]])
                k.dma("sp", yout[tok0 + i * L: tok0 + (i + 1) * L, :], xt[0:L, i, :], reads=[B_xt[i]], sem=B_xt[i])
                if i < ntn:
                    k.dma("sp", xt[0:Ln, i, :], xpre[0:Ln, i, :], reads=Bxpre(i), writes=[B_xt[i]], sem=B_xt[i])
                yield

        if nxt_blk is not None and pre_done.get("key") == nxt_blk:
            carry["gen"] = final_gen()
        else:
            for _ in final_gen():
                pass

    B_scr = k.buf("scr")
    for bi, (G_, b_) in enumerate(blocks):
        run_block(G_, b_, blocks[bi + 1] if bi + 1 < len(blocks) else None)
    allb = B_xt + B_Cst + B_cm + [B_vnf, B_scr] + B_tmpf
    k.final_wait("sp", allb)
    k.emit()
    return nc, k


_CACHE = {}


def _layouts(inp):
    f = np.float32
    A = lambda a: np.ascontiguousarray(a, dtype=f)
    w_in = inp["w_in"][0]
    sh = {}
    w_main = np.concatenate([w_in[:, :6144], w_in[:, 6152:8200]], axis=1)
    sh["w_in"] = A(w_main.reshape(8, 128, 16, 512).transpose(2, 1, 0, 3))
    sh["wif"] = A(w_in[:, 6144:6152].reshape(8, 128, 8).transpose(1, 0, 2))
    sh["bif"] = A(np.concatenate([inp["b_i"][0], inp["b_f"][0]]))
    for nm, key in (("g1", "g_norm1"), ("g2", "g_norm2"), ("lng", "ln_g"), ("lnb", "ln_b")):
        sh[nm] = A(inp[key][0])
    sh["gf"] = A(inp["g_final"])
    ws_ = inp["w_s"][0]
    sh["wsTP"] = A(ws_.transpose(2, 0, 1))
    w4 = ws_[:, :4, :4]
    sh["wsTS"] = A(np.tile(w4.transpose(2, 0, 1), (16, 1, 16)))
    bs_ = inp["b_s"][0]
    sh["bsrP"] = A(bs_.reshape(512))
    sh["bsrS"] = A(np.tile(bs_[:, :4], (1, 16)).reshape(256))
    sh["convw"] = A(inp["conv_w"][0].reshape(4, 16, 128).transpose(2, 1, 0))
    sh["convb"] = A(inp["conv_b"][0].reshape(16, 128).T)
    sh["hng"] = A(inp["hn_g"][0].reshape(8, 128).T)
    sh["bgate"] = A(inp["b_gate"][0].reshape(16, 128).T)
    for nm, key in (("wpa", "w_proj_a"), ("wpb", "w_proj_b"), ("wo", "w_out")):
        sh[nm] = A(inp[key][0].reshape(8, 128, 2, 512).transpose(2, 1, 0, 3))
    wfi = inp["w_ffn_in"][0]
    colidx = []
    for s in range(11):
        for base in (0, 2816):
            for jj in range(2):
                j = 2 * s + jj
                colidx.extend(range(base + j * 128, base + (j + 1) * 128))
    sh["wfi"] = A(wfi[:, colidx].reshape(8, 128, 11, 512).transpose(2, 1, 0, 3))
    sh["wfo"] = A(inp["w_ffn_out"][0].reshape(22, 128, 2, 512).transpose(2, 1, 0, 3))
    return sh


def kernel(**inp):
    inp = {kk: np.asarray(v) for kk, v in inp.items()}
    if "nc" not in _CACHE:
        _CACHE["nc"] = build_program()[0]
    nc = _CACHE["nc"]
    shared = _layouts(inp)
    in_maps = []
    for c in range(NCORES):
        m = dict(shared)
        m["xp"] = np.ascontiguousarray(inp["x_prompt"][c], dtype=np.float32)
        sl = slice(16 * c, 16 * c + 16)
        m["xs"] = np.ascontiguousarray(inp["x_sample"][sl].reshape(64, 1024), dtype=np.float32)
        m["sconv"] = np.ascontiguousarray(inp["state_conv"][0, sl].reshape(48, 2048), dtype=np.float32)
        m["sC"] = np.ascontiguousarray(inp["state_C"][0, sl], dtype=np.float32)
        m["sn"] = np.ascontiguousarray(inp["state_n"][0, sl].reshape(16, 1024), dtype=np.float32)
        m["smr"] = np.ascontiguousarray(np.repeat(inp["state_m"][0, sl], 4, axis=0), dtype=np.float32)
        in_maps.append(m)
    res = run_bass_kernel_spmd(nc, in_maps, core_ids=list(range(NCORES)))
    R = res.results
    cat = lambda nm: [np.asarray(r[nm]) for r in R]
    y_prompt = np.stack(cat("yp"), 0)
    y_sample = np.concatenate(cat("ys"), 0).reshape(128, 4, 1024)
    conv_p = np.stack(cat("convp"), 0)[None]
    C_p = np.stack(cat("Cp"), 0)[None]
    n_p = np.stack([a.reshape(4, 256) for a in cat("np")], 0)[None]
    m_p = np.concatenate(cat("mp"), 0)[None]
    conv_s = np.concatenate(cat("convs"), 0)[None]
    C_s = np.concatenate(cat("Cs"), 0)[None]
    n_s = np.concatenate(cat("ns"), 0).reshape(128, 4, 256)[None]
    m_s = np.concatenate(cat("ms"), 0)[None]
    v_s = np.concatenate(cat("vs"), 0).reshape(128, 4, 1024)[None]
    outs = (y_prompt, y_sample, conv_p, C_p, n_p, m_p, conv_s, C_s, n_s, m_s, v_s)
    return tuple(np.ascontiguousarray(o, dtype=np.float32) for o in outs)
```

```python
import math
import numpy as np
from contextlib import ExitStack
import concourse.bass as bass
import concourse.mybir as mybir
from concourse.bass_utils import run_bass_kernel_spmd

F32 = mybir.dt.float32
BF16 = mybir.dt.bfloat16
AF = mybir.ActivationFunctionType
ALU = mybir.AluOpType
AX = mybir.AxisListType

SAME_ENG_SYNC = True
EPS = 1e-6
LN16 = math.log(16.0)
NEG = -1.0e30
NCORES = 8


class Buf:
    __slots__ = ("name", "w", "r", "excl", "dsem")

    def __init__(self, name, excl=False):
        self.name = name
        self.w = None
        self.r = {}
        self.excl = excl
        self.dsem = None


class KB:
    ENGS = ("pe", "act", "dve", "pool", "sp")

    def __init__(self, nc):
        self.nc = nc
        self.prog = {e: [] for e in self.ENGS}
        self.cnt = {e: 0 for e in self.ENGS}
        self.dcnt = {}
        self.waited = {e: {} for e in self.ENGS}
        self.stack = ExitStack()
        self.nbuf = 0
        self.sb_bytes = 0

    def sb(self, name, shape, dt):
        n = 1
        for s in shape[1:]:
            n *= s
        self.sb_bytes += n * (2 if dt == BF16 else 4)
        return self.stack.enter_context(self.nc.sbuf_tensor("sb_" + name, list(shape), dt))

    def ps(self, name, shape, dt):
        return self.stack.enter_context(self.nc.psum_tensor("ps_" + name, list(shape), dt))

    def buf(self, name, excl=False):
        self.nbuf += 1
        return Buf(f"{name}_{self.nbuf}", excl)

    def bufs(self, name, n):
        return [self.buf(f"{name}{i}") for i in range(n)]

    def _deps(self, eng, reads, writes, dma_sem=None):
        deps = {}
        own = "E:" + eng

        def add(d, excl):
            if d is None:
                return
            kk, v = d
            if kk == own and (excl or eng == "pe" or not SAME_ENG_SYNC):
                return
            if deps.get(kk, 0) < v:
                deps[kk] = v

        for b in reads:
            add(b.w, b.excl)
            if b.excl:
                for kv in b.r.items():
                    add(kv, True)
        for b in writes:
            if not (dma_sem is not None and b.w is not None and b.w[0] == dma_sem):
                add(b.w, b.excl)
            for kv in b.r.items():
                add(kv, b.excl)
        out = []
        for kk, v in deps.items():
            if self.waited[eng].get(kk, 0) >= v:
                continue
            self.waited[eng][kk] = v
            out.append((kk, v))
        return out

    def _mark(self, key, val, reads, writes):
        for b in reads:
            if b.excl:
                b.w = (key, val)
                b.r = {}
            elif b.r.get(key, 0) < val:
                b.r[key] = val
        for b in writes:
            b.w = (key, val)
            b.r = {}

    def op(self, eng, fn, reads=(), writes=()):
        reads = [b for b in reads if b is not None]
        writes = [b for b in writes if b is not None]
        waits = self._deps(eng, reads, writes)
        self.cnt[eng] += 1
        key = "E:" + eng
        self.prog[eng].append((waits, fn, key, 1))
        self._mark(key, self.cnt[eng], reads, writes)

    def dma(self, eng, out, in_, reads=(), writes=(), sem=None):
        reads = [b for b in reads if b is not None]
        writes = [b for b in writes if b is not None]
        if sem.dsem is None:
            sem.dsem = "D:" + sem.name
            self.dcnt[sem.dsem] = 0
        key = sem.dsem
        waits = self._deps(eng, reads, writes, dma_sem=key)
        self.dcnt[key] += 16
        fn = (lambda e, out=out, in_=in_: e.dma_start(out=out, in_=in_))
        self.prog[eng].append((waits, fn, key, 16))
        self._mark(key, self.dcnt[key], reads, writes)

    def final_wait(self, eng, bufs):
        waits = self._deps(eng, [], bufs)
        self.prog[eng].append((waits, None, None, 0))

    def emit(self):
        nc = self.nc
        keys = ["E:" + e for e in self.ENGS if e != "sp"] + list(self.dcnt.keys())
        sems = {}
        for i, kk in enumerate(keys):
            sems[kk] = self.stack.enter_context(nc.semaphore(f"s{i}"))
        prog = self.prog

        def run(e, lst):
            for waits, fn, key, inc in lst:
                for kk, v in waits:
                    e.wait_ge(sems[kk], v)
                if fn is None:
                    continue
                fn(e).then_inc(sems[key], inc)

        with nc.Block() as block:
            @block.tensor
            def _(e):
                run(e, prog["pe"])

            @block.scalar
            def _(e):
                run(e, prog["act"])

            @block.vector
            def _(e):
                run(e, prog["dve"])

            @block.gpsimd
            def _(e):
                run(e, prog["pool"])

            @block.sync
            def _(e):
                run(e, prog["sp"])
        self.stack.close()


def build_program(n_prompt_blocks=4, do_sample=True, debug=False):
    nc = bass.Bass("TRN2", target_bir_lowering=False)

    def din(name, shape, dt=F32):
        return nc.dram_tensor(name, list(shape), dt, kind="ExternalInput").ap()

    def dout(name, shape, dt=F32):
        return nc.dram_tensor(name, list(shape), dt, kind="ExternalOutput").ap()

    D = {}
    D["xp"] = din("xp", [2048, 1024]); D["xs"] = din("xs", [64, 1024])
    D["sconv"] = din("sconv", [48, 2048]); D["sC"] = din("sC", [16, 4, 256, 256])
    D["sn"] = din("sn", [16, 1024]); D["smr"] = din("smr", [64, 4])
    for nm in ("g1", "g2", "gf", "lng", "lnb"):
        D[nm] = din(nm, [1024])
    D["w_in"] = din("w_in", [16, 128, 8, 512]); D["wif"] = din("wif", [128, 8, 8]); D["bif"] = din("bif", [8])
    D["wsTP"] = din("wsTP", [128, 4, 128]); D["wsTS"] = din("wsTS", [64, 4, 64])
    D["bsrP"] = din("bsrP", [512]); D["bsrS"] = din("bsrS", [256])
    D["convw"] = din("convw", [128, 16, 4]); D["convb"] = din("convb", [128, 16])
    D["hng"] = din("hng", [128, 8]); D["bgate"] = din("bgate", [128, 16])
    D["wpa"] = din("wpa", [2, 128, 8, 512]); D["wpb"] = din("wpb", [2, 128, 8, 512]); D["wo"] = din("wo", [2, 128, 8, 512])
    D["wfi"] = din("wfi", [11, 128, 8, 512]); D["wfo"] = din("wfo", [2, 128, 22, 512])
    O = {}
    O["yp"] = dout("yp", [2048, 1024]); O["ys"] = dout("ys", [64, 1024]); O["convp"] = dout("convp", [3, 2048])
    O["Cp"] = dout("Cp", [4, 256, 256]); O["np"] = dout("np", [1, 1024]); O["mp"] = dout("mp", [1, 4])
    O["convs"] = dout("convs", [16, 3, 2048]); O["Cs"] = dout("Cs", [16, 4, 256, 256]); O["ns"] = dout("ns", [16, 1024])
    O["ms"] = dout("ms", [16, 4]); O["vs"] = dout("vs", [64, 1024])
    scr = nc.dram_tensor("qkscr", [64, 2048], F32, kind="Internal").ap()
    wscr = nc.dram_tensor("wscr", [39, 128, 4096], BF16, kind="Internal").ap()
    if debug:
        for nm in ("dbg_ao", "dbg_bo", "dbg_mg"):
            O[nm] = dout(nm, [128, 8, 512], BF16)
        O["dbg_qk"] = dout("dbg_qk", [128, 16, 512], BF16)
        O["dbg_vn"] = dout("dbg_vn", [4, 128, 1024], BF16)
        O["dbg_gv"] = dout("dbg_gv", [4, 128, 1024], F32)
        O["dbg_hst"] = dout("dbg_hst", [4, 128, 16], F32)

    k = KB(nc)
    op = k.op

    def mm(out, lhsT, rhs, start, stop, R, W):
        op("pe", lambda e: e.matmul(out=out, lhsT=lhsT, rhs=rhs, start=start, stop=stop), R, W)

    def tr(out, in_, ident, R, W):
        op("pe", lambda e: e.transpose(out=out, in_=in_, identity=ident), R, W)

    def act(out, in_, func, R, W, bias=0.0, scale=1.0, accum=None, eng="act"):
        if accum is None:
            op(eng, lambda e: e.activation(out=out, in_=in_, func=func, bias=bias, scale=scale), R, W)
        else:
            op(eng, lambda e: e.activation(out=out, in_=in_, func=func, bias=bias, scale=scale, accum_out=accum), R, W)

    def tt(eng, out, in0, in1, o, R, W):
        op(eng, lambda e: e.tensor_tensor(out=out, in0=in0, in1=in1, op=o), R, W)

    def ts(eng, out, in0, s1, s2, op0, op1, R, W):
        if s2 is None:
            op(eng, lambda e: e.tensor_scalar(out=out, in0=in0, scalar1=s1, scalar2=None, op0=op0), R, W)
        else:
            op(eng, lambda e: e.tensor_scalar(out=out, in0=in0, scalar1=s1, scalar2=s2, op0=op0, op1=op1), R, W)

    def stt(out, in0, scalar, in1, op0, op1, R, W):
        op("dve", lambda e: e.scalar_tensor_tensor(out=out, in0=in0, scalar=scalar, in1=in1, op0=op0, op1=op1), R, W)

    def cp(eng, out, in_, R, W):
        if eng == "act":
            op("act", lambda e: e.activation(out=out, in_=in_, func=AF.Copy), R, W)
        else:
            op(eng, lambda e: e.tensor_copy(out=out, in_=in_), R, W)

    def memset(eng, ap, val, W):
        op(eng, lambda e: e.memset(ap, val), [], W)

    def asel(out, in_, pattern, cmp, fill, base, chm, R, W):
        op("pool", lambda e: e.affine_select(out=out, in_=in_, pattern=pattern, compare_op=cmp, fill=fill,
                                             base=base, channel_multiplier=chm), R, W)

    pj = [k.ps(f"pj{i}", [128, 512], F32) for i in range(4)]
    B_pj = [k.buf(f"pj{i}", excl=True) for i in range(4)]
    ptb = k.ps("ptb", [128, 8, 128], BF16); B_ptb = k.buf("ptb", excl=True)
    pg = k.ps("pg", [128, 512], F32); B_pg = k.buf("pg", excl=True)
    pnum = k.ps("pnum", [128, 512], F32); B_pnum = k.buf("pnum", excl=True)
    pc = k.ps("pc", [128, 512], F32); B_pc = k.buf("pc", excl=True)

    identf = k.sb("identf", [128, 128], F32); B_idf = k.buf("identf")
    identb = k.sb("identb", [128, 128], BF16); B_idb = k.buf("identb")
    onesf = k.sb("onesf", [128, 128], F32); B_ones = k.buf("onesf")
    bc_t = {}
    B_bc = {}
    for nm in ("g1", "g2", "gf", "lng", "lnb"):
        bc_t[nm] = k.sb("bc_" + nm, [128, 1024], F32)
        B_bc[nm] = k.buf("bc_" + nm)
        k.dma("sp", bc_t[nm][:], D[nm].partition_broadcast(128), writes=[B_bc[nm]], sem=B_bc[nm])
    B_sm = k.buf("smallconst")
    convw = k.sb("convw", [128, 16, 4], F32); convb = k.sb("convb", [128, 16], F32)
    hng = k.sb("hng", [128, 8], F32); bgate = k.sb("bgate", [128, 16], F32)
    bif = k.sb("bif", [128, 8], F32); wif = k.sb("wif", [128, 8, 8], BF16)
    bsf = {"P": k.sb("bsfP", [2, 512], F32), "S": k.sb("bsfS", [2, 256], F32)}
    bs2 = {"P": k.sb("bs2P", [2, 512], BF16), "S": k.sb("bs2S", [2, 256], BF16)}
    ones2 = k.sb("ones2", [2, 128], BF16)
    k.dma("sp", convw[:], D["convw"], writes=[B_sm], sem=B_sm)
    k.dma("sp", convb[:], D["convb"], writes=[B_sm], sem=B_sm)
    k.dma("sp", hng[:], D["hng"], writes=[B_sm], sem=B_sm)
    k.dma("sp", bgate[:], D["bgate"], writes=[B_sm], sem=B_sm)
    k.dma("sp", bif[:], D["bif"].partition_broadcast(128), writes=[B_sm], sem=B_sm)
    k.dma("sp", bsf["P"][:], D["bsrP"].partition_broadcast(2), writes=[B_sm], sem=B_sm)
    k.dma("sp", bsf["S"][:], D["bsrS"].partition_broadcast(2), writes=[B_sm], sem=B_sm)
    B_wif = k.buf("wif")
    k.dma("pool", wif[:], D["wif"], writes=[B_wif], sem=B_wif)

    memset("pool", identf[:], 1.0, [B_idf])
    asel(identf[:], identf[:], [[-1, 128]], ALU.is_equal, 0.0, 0, 1, [B_idf], [B_idf])
    cp("dve", identb[:], identf[:], [B_idf], [B_idb])
    memset("pool", onesf[:], 1.0, [B_ones])

    xt = k.sb("xt", [128, 4, 1024], F32); B_xt = k.bufs("xt", 4)
    hT = k.sb("hT", [128, 8, 512], BF16); B_hT = k.bufs("hT", 4)
    xn = k.sb("xn", [128, 1024], BF16); B_xn = k.buf("xn")
    junk = xn; B_junk = B_xn
    st4 = k.sb("st4", [128, 8], F32); B_st4 = k.buf("st4")
    rst = k.sb("rst", [128, 12], F32)
    U8 = k.sb("U8", [128, 8, 512], BF16); B_U8 = k.bufs("U8", 8)
    aoT = k.sb("aoT", [128, 8, 512], BF16); B_ao = k.bufs("aoT", 4)
    boT = k.sb("boT", [128, 8, 512], BF16); B_bo = k.bufs("boT", 4)
    sgab = k.sb("sgab", [128, 16, 512], BF16)
    sga = sgab[:, 0:8, :]; B_sga = k.bufs("sga", 8)
    sgb = sgab[:, 8:16, :]; B_sgb = k.bufs("sgb", 8)
    xpre = sgab[:, :, :].rearrange("p a b -> p (a b)").bitcast(F32).rearrange("p (i c) -> p i c", i=4)
    B_sgab = B_sga + B_sgb

    def Bxpre(i):
        return B_sgab[4 * i:4 * i + 4]
    arena = k.sb("arena", [128, 11264], BF16); B_ar = k.bufs("arena", 22)
    gv = arena[:, 0:8192].bitcast(F32).rearrange("p (i c) -> p i c", i=4)
    qkT = arena[:, 0:8192].rearrange("p (c t) -> p c t", c=16)
    gT = arena[:, :].rearrange("p (c t) -> p c t", c=22)
    memset("pool", ones2[:], 1.0, [B_ones])
    B_bs = k.buf("bs2")
    bsh = arena[0:2, 0:512]
    bsg = arena[0:2, 1024:2048].bitcast(F32)
    bsr_ = arena[0:2, 2048:3072].bitcast(F32)
    for G_ in ("P", "S"):
        n_ = 512 if G_ == "P" else 256
        cp("dve", bsh[:, 0:n_], bsf[G_][:, :], [B_sm], [B_bs] + B_ar[0:6])
        cp("dve", bsg[:, 0:n_], bsh[:, 0:n_], [B_bs] + B_ar[0:6], [B_bs] + B_ar[0:6])
        tt("dve", bsr_[:, 0:n_], bsf[G_][:, :], bsg[:, 0:n_], ALU.subtract, [B_sm, B_bs], [B_bs] + B_ar[0:6])
        ts("dve", bsr_[:, 0:n_], bsr_[:, 0:n_], identf[0:2, 1:2], None, ALU.mult, None, [B_bs, B_idf], [B_bs] + B_ar[0:6])
        stt(bs2[G_][:, :], bsg[:, 0:n_], identf[0:2, 0:1], bsr_[:, 0:n_], ALU.mult, ALU.add, [B_bs, B_idf] + B_ar[0:6], [B_bs])
    vnf = k.sb("vnf", [128, 1024], F32); B_vnf = k.buf("vnf")
    vnb = [k.sb(f"vnb{i}", [128, 1024], BF16) for i in range(2)]; B_vnb = k.bufs("vnb", 2)
    raw = [k.sb(f"raw{i}", [128, 520], F32) for i in range(2)]; B_raw = k.bufs("raw", 2)
    cacc = [k.sb(f"cacc{i}", [128, 512], F32) for i in range(2)]; B_cacc = k.bufs("cacc", 2)
    halo = k.sb("halo", [128, 16, 3], F32); B_halo = k.bufs("halo", 16)
    scst = vnf; B_scst = B_vnf
    vaug = k.sb("vaug", [128, 4, 4, 257], BF16); B_vaug = k.bufs("vaug", 4)
    tmpf = [k.sb(f"tmpf{i}", [128, 512], F32) for i in range(2)]; B_tmpf = k.bufs("tmpf", 2)
    tmpg = cacc; B_tmpg = B_cacc
    ktm = k.sb("ktm", [128, 1024], BF16); B_ktm = k.buf("ktm")
    hmb = k.sb("hmb", [128, 1024], BF16); B_hmb = k.buf("hmb")
    slabs = [k.sb(f"slab{i}", [128, 8, 512], BF16) for i in range(3)]; B_slab = k.bufs("slab", 3)
    gi = k.sb("gi", [128, 8], F32); lf = k.sb("lf", [128, 16], F32)
    gb8 = k.sb("gb8", [128, 8], F32); gl8 = k.sb("gl8", [128, 8], F32)
    aa = k.sb("aa", [128, 8], F32)
    Dx = k.sb("Dx", [128, 16], F32)
    mnd = k.sb("mnd", [128, 8], F32)
    B_g = k.buf("gates")
    mprev = {"P": k.sb("mprevP", [128, 4], F32), "S": k.sb("mprevS", [64, 4], F32)}
    B_mprev = {"P": k.buf("mprevP"), "S": k.buf("mprevS")}
    diag = k.sb("diag", [128, 4, 128], F32); B_diag = k.buf("diag")
    DTb = k.sb("DTb", [128, 4, 4, 128], BF16); B_DT = k.bufs("DT", 4)
    decx = k.sb("decx", [128, 16, 4], F32); B_decx = k.buf("decx")
    decb_t = k.sb("decb_t", [128, 4, 64], F32); B_dec = k.bufs("dec", 4)
    wsel_t = k.sb("wsel_t", [128, 4, 4, 16], BF16); B_wsel = k.bufs("wsel", 4)
    Ex_t = k.sb("Ex_t", [128, 4, 16], F32); B_Ex = k.bufs("Ex", 4)
    cm8_t = k.sb("cm8_t", [16, 4, 8], F32); B_cm = k.bufs("cm", 4)
    hs2 = [k.sb(f"hs{i}", [128, 48], F32) for i in range(2)]; B_hs2 = k.bufs("hs", 2)
    nTf_t = k.sb("nTf", [128, 8, 16], F32); B_nTf = k.buf("nTf")
    qTs = k.sb("qTs", [128, 8, 128], BF16); B_qTs = k.buf("qTs")
    qTm = [k.sb(f"qTm{i}", [128, 16, 64], BF16) for i in range(2)]; B_qTm = k.bufs("qTm", 2)
    qTm_diag = []
    for i in range(2):
        a0 = qTm[i][:, :, :]
        qTm_diag.append(bass.AP(a0.tensor, a0.offset, [list(a0.ap[0]), [68, 16], [1, 4]]))
        op("pool", (lambda e, i=i: e.memset(qTm[i][:, :, :], 0.0)), [], [B_qTm[i]])
    STb = [k.sb(f"STb{i}", [128, 128], BF16) for i in range(2)]; B_ST = k.bufs("ST", 2)
    vsb = [k.sb(f"vsb{i}", [128, 257], BF16) for i in range(2)]; B_vs = k.bufs("vs", 2)
    vsm = [k.sb(f"vsm{i}", [64, 256], BF16) for i in range(2)]; B_vsm = k.bufs("vsm", 2)
    hst = k.sb("hst", [128, 16], F32); B_hst = k.buf("hst")
    CTb = [k.sb(f"CTb{i}", [128, 2, 257], BF16) for i in range(2)]; B_CTb = k.bufs("CTb", 2)
    Cst = k.sb("Cst", [128, 5, 2, 256], F32); B_Cst = k.bufs("Cst", 5)

    for h in range(4):
        memset("pool", Cst[:, h, :, :], 0.0, [B_Cst[h]])
    memset("pool", nTf_t[:, :, :], 0.0, [B_nTf])
    memset("pool", mprev["P"][:], 0.0, [B_mprev["P"]])
    memset("pool", vaug[:, :, :, 256:257], 1.0, B_vaug)
    memset("pool", halo[:], 0.0, B_halo)

    class WS:
        def __init__(self):
            self.q = []
            self.issued = 0
            self.occ = {}
            self.slots = []
            self.rcount = 0

        def plan(self, lst):
            self.q.extend(lst)

        def slot_of(self, i):
            while len(self.slots) <= i:
                j = len(self.slots) % 39
                if j in (18, 19):
                    self.slots.append(3 + (j - 18))
                else:
                    self.slots.append(self.rcount % 3)
                    self.rcount += 1
            return self.slots[i]

        def get(self, idx, hold=None):
            hold = idx if hold is None else hold
            while self.issued < len(self.q) and self.issued <= hold + 4:
                i = self.issued
                s = self.slot_of(i)
                prev = self.occ.get(s)
                if prev is not None and prev >= hold:
                    break
                if s >= 3 and not (hold // 39 == i // 39 and hold % 39 >= 16):
                    break
                ap, nk = self.q[i]
                j = i % 39
                tile_, Bs, extra = slot_tiles[s]
                if i < 39:
                    k.dma("pool", tile_[:, 0:nk, :], ap, writes=[Bs] + extra, sem=Bs)
                    k.dma("sp", wscr[j, :, 0:nk * 512], tile_[:, 0:nk, :].rearrange("p k c -> p (k c)"),
                          reads=[Bs] + extra, writes=[B_wscr[j]], sem=B_wscr[j])
                else:
                    k.dma("pool", tile_[:, 0:nk, :], wscr[j, :, 0:nk * 512].rearrange("p (k c) -> p k c", k=nk),
                          reads=[B_wscr[j]], writes=[Bs] + extra, sem=Bs)
                self.occ[s] = i
                self.issued += 1
            s = self.slot_of(idx)
            assert self.occ.get(s) == idx, (idx, s, self.occ)
            return slot_tiles[s][0], slot_tiles[s][1]

    slot_tiles = [(slabs[i_], B_slab[i_], []) for i_ in range(3)]
    slot_tiles.append((arena[:, 0:4096].rearrange("p (k c) -> p k c", k=8), k.buf("slotA"), B_ar[0:8]))
    slot_tiles.append((arena[:, 4096:8192].rearrange("p (k c) -> p k c", k=8), k.buf("slotB"), B_ar[8:16]))
    B_wscr = k.bufs("wscr", 39)
    ws = WS()
    wbase = [0]

    def block_slabs():
        lst = [(D["w_in"][s], 8) for s in range(16)]
        for s in range(2):
            lst += [(D["wpa"][s], 8), (D["wpb"][s], 8)]
        lst += [(D["wo"][s], 8) for s in range(2)]
        lst += [(D["wfi"][s], 8) for s in range(11)]
        for g in range(2):
            for q0 in (0, 8, 16):
                nk = min(8, 22 - q0)
                lst.append((D["wfo"][g, :, q0:q0 + nk, :], nk))
        return lst

    blocks = [("P", b_) for b_ in range(n_prompt_blocks)] + ([("S", 0)] if do_sample else [])
    for _ in blocks:
        ws.plan(block_slabs())
    ws.get(0)

    LG = {"P": 128, "S": 64}
    NBG = {"P": 1, "S": 16}
    MK = {}
    B_mk = k.buf("masks")
    am = k.sb("am", [128, 4, 128], F32); B_am = k.buf("am")
    wsst = am; B_wsst = B_am
    for G in ("P", "S"):
        L = LG[G]
        m = {}
        for nm in ("cmask", "nmT", "nm", "sell"):
            m[nm] = k.sb(f"{nm}{G}", [L, L], F32)
        m["blk"] = k.sb(f"blk{G}", [L, 16], F32)
        m["selc"] = k.sb(f"selc{G}", [L, 16], F32)
        m["wsT"] = k.sb(f"wsT{G}", [L, 4, L], BF16)
        MK[G] = m
        memset("pool", m["cmask"][:], 1.0, [B_mk]); memset("pool", m["nmT"][:], 0.0, [B_mk])
        memset("pool", m["nm"][:], 0.0, [B_mk]); memset("pool", m["sell"][:], 1.0, [B_mk])
        memset("pool", m["blk"][:], 1.0, [B_mk]); memset("pool", m["selc"][:], 1.0, [B_mk])
        k.dma("sp", wsst[0:L, :, 0:L], D["wsT" + G], writes=[B_wsst], sem=B_wsst)
        if G == "P":
            asel(m["cmask"][:], m["cmask"][:], [[1, 128]], ALU.is_ge, 0.0, 0, -1, [B_mk], [B_mk])
            asel(m["nmT"][:], m["nmT"][:], [[1, 128]], ALU.is_ge, NEG, 0, -1, [B_mk], [B_mk])
            asel(m["nm"][:], m["nm"][:], [[-1, 128]], ALU.is_ge, NEG, 0, 1, [B_mk], [B_mk])
            asel(m["sell"][:], m["sell"][:], [[0, 128]], ALU.is_equal, 0.0, -127, 1, [B_mk], [B_mk])
            asel(m["selc"][:], m["selc"][:], [[0, 16]], ALU.is_equal, 0.0, -127, 1, [B_mk], [B_mk])
            for g in range(4):
                asel(wsst[:, g, :], wsst[:, g, :], [[1, 128]], ALU.is_ge, 0.0, 0, -1, [B_wsst, B_mk], [B_wsst])
        else:
            def v3(t):
                return t[:].rearrange("p (b i) -> p b i", b=16)
            for nm, fill in (("cmask", 0.0), ("nmT", NEG)):
                asel(v3(m[nm]), v3(m[nm]), [[-4, 16], [0, 4]], ALU.is_ge, fill, 0, 1, [B_mk], [B_mk])
                asel(v3(m[nm]), v3(m[nm]), [[4, 16], [0, 4]], ALU.is_ge, fill, 3, -1, [B_mk], [B_mk])
                asel(v3(m[nm]), v3(m[nm]), [[4, 16], [1, 4]], ALU.is_ge, fill, 0, -1, [B_mk], [B_mk])
            asel(v3(m["nm"]), v3(m["nm"]), [[-4, 16], [0, 4]], ALU.is_ge, NEG, 0, 1, [B_mk], [B_mk])
            asel(v3(m["nm"]), v3(m["nm"]), [[4, 16], [0, 4]], ALU.is_ge, NEG, 3, -1, [B_mk], [B_mk])
            asel(v3(m["nm"]), v3(m["nm"]), [[-4, 16], [-1, 4]], ALU.is_ge, NEG, 0, 1, [B_mk], [B_mk])
            asel(v3(m["sell"]), v3(m["sell"]), [[-4, 16], [0, 4]], ALU.is_equal, 0.0, -3, 1, [B_mk], [B_mk])
            asel(m["blk"][:], m["blk"][:], [[-4, 16]], ALU.is_ge, 0.0, 0, 1, [B_mk], [B_mk])
            asel(m["blk"][:], m["blk"][:], [[4, 16]], ALU.is_ge, 0.0, 3, -1, [B_mk], [B_mk])
            asel(m["selc"][:], m["selc"][:], [[-4, 16]], ALU.is_equal, 0.0, -3, 1, [B_mk], [B_mk])
            for g in range(4):
                w3 = wsst[0:64, g, 0:64].rearrange("p (b i) -> p b i", b=16)
                asel(w3, w3, [[-4, 16], [0, 4]], ALU.is_ge, 0.0, 0, 1, [B_wsst, B_mk], [B_wsst])
                asel(w3, w3, [[4, 16], [0, 4]], ALU.is_ge, 0.0, 3, -1, [B_wsst], [B_wsst])
                asel(w3, w3, [[4, 16], [1, 4]], ALU.is_ge, 0.0, 0, -1, [B_wsst], [B_wsst])
        cp("dve", m["wsT"][:], wsst[0:L, :, 0:L], [B_wsst], [B_mk])

    rot = {"pj": 0, "raw": 0, "tmpf": 0, "tmpg": 0, "vnb": 0, "ST": 0, "vs": 0, "CTb": 0, "qTm": 0, "cs": 0, "vsm": 0, "pC": 0}

    def nxt(name, n):
        v = rot[name]
        rot[name] = (v + 1) % n
        return v

    pre_done = {}
    carry = {}
    mhalf = k.sb("mhalf", [128, 4], F32)
    memset("pool", mhalf[:], -0.5, [B_ones])

    def rsqrt_eps(out, in_, tmp, n, L, B):
        ts("pool", tmp, in_, EPS, 1.0, ALU.add, ALU.mult, B, B)
        tt("pool", out, tmp, mhalf[0:L, 0:n], ALU.pow, B + [B_ones], B)

    def rms_stats_g(src, Bsrc, L):
        for hh in range(2):
            op("dve", lambda e, hh=hh: e.bn_stats(out=rst[0:L, hh * 6:(hh + 1) * 6], in_=src[:, hh * 512:(hh + 1) * 512]),
               Bsrc, [B_st4])
        op("dve", lambda e: e.bn_aggr(out=st4[0:L, 4:6], in_=rst[0:L, 0:12]), [B_st4], [B_st4])
        stt(st4[0:L, 0:1], st4[0:L, 4:5], st4[0:L, 4:5], st4[0:L, 5:6], ALU.mult, ALU.add, [B_st4], [B_st4])
        rsqrt_eps(st4[0:L, 2:3], st4[0:L, 0:1], st4[0:L, 1:2], 1, L, [B_st4])

    def rmsnorm_to_hT_g(src, Bsrc, L, i, gname):
        rms_stats_g(src, Bsrc, L)
        yield; yield; yield
        stt(xn[0:L, :], src, st4[0:L, 2:3], bc_t[gname][0:L, :], ALU.mult, ALU.mult, Bsrc + [B_st4, B_bc[gname]], [B_xn])
        yield; yield
        for kc in range(8):
            tr(ptb[:, kc, 0:L], xn[0:L, kc * 128:(kc + 1) * 128], identb[0:L, 0:L], [B_xn, B_idb], [B_ptb])
        yield
        cp("act", hT[:, :, i * L:(i + 1) * L], ptb[:, :, 0:L], [B_ptb], [B_hT[i]])

    def prologue_gen(G, blk):
        L = LG[G]
        ntile = 4 if G == "P" else 1
        xin = D["xp"] if G == "P" else D["xs"]
        tok0 = blk * 512 if G == "P" else 0
        for i in range(ntile):
            k.dma("sp", xpre[0:L, i, :], xin[tok0 + i * L: tok0 + (i + 1) * L, :], writes=Bxpre(i), sem=Bxpre(i)[0])
        yield
        for i in range(ntile):
            for _ in rmsnorm_to_hT_g(xpre[0:L, i, :], Bxpre(i), L, i, "g1"):
                yield
            yield
        pre_done["key"] = (G, blk)

    def run_block(G, blk, nxt_blk=None):
        L = LG[G]; nb = NBG[G]; Ls = L // nb
        ntile = 4 if G == "P" else 1
        TB = ntile * L
        M = MK[G]
        xin = D["xp"] if G == "P" else D["xs"]
        yout = O["yp"] if G == "P" else O["ys"]
        tok0 = blk * 512 if G == "P" else 0
        last_prompt_blk = (G == "P" and blk == n_prompt_blocks - 1)
        w0 = wbase[0]
        wbase[0] += 39

        def cols(i):
            return slice(i * L, (i + 1) * L)

        def rms_stats(i):
            rms_stats_g(xt[0:L, i, :], [B_xt[i]], L)

        def rmsnorm_to_hT(i, gname):
            for _ in rmsnorm_to_hT_g(xt[0:L, i, :], [B_xt[i]], L, i, gname):
                pass

        hand = None
        if pre_done.get("key") == (G, blk):
            hand = carry.pop("gen")
        else:
            for i in range(ntile):
                k.dma("sp", xt[0:L, i, :], xin[tok0 + i * L: tok0 + (i + 1) * L, :], writes=[B_xt[i]], sem=B_xt[i])
            for i in range(ntile):
                rmsnorm_to_hT(i, "g1")

        def fm_proj(slab_idx, nchunk, rhs, Rrhs, evac):
            W, Bw = ws.get(slab_idx)
            for cc in range(nchunk):
                b = nxt("pj", 4)
                for kc in range(8):
                    mm(pj[b][:, 0:TB], W[:, kc, cc * 128:(cc + 1) * 128], rhs[:, kc, 0:TB], kc == 0, kc == 7,
                       [Bw] + Rrhs, [B_pj[b]])
                tick()
                evac(cc, pj[b], B_pj[b])

        def tm_proj(slab_idx, lhs, Blhs_of_tile, evac):
            W, Bw = ws.get(slab_idx)
            for i in range(ntile):
                b = nxt("pj", 4)
                for kc in range(8):
                    mm(pj[b][0:L, :], lhs[:, kc, cols(i)], W[:, kc, :], kc == 0, kc == 7,
                       [Bw, Blhs_of_tile(i)], [B_pj[b]])
                tick()
                evac(i, pj[b], B_pj[b])

        if G == "S":
            k.dma("sp", mprev["S"][:], D["smr"], writes=[B_mprev["S"]], sem=B_mprev["S"])
        mp = mprev[G]; Bmp = B_mprev[G]

        def bcast_diag(vec4, R):
            tt("dve", diag[0:L, :, 0:L], identf[0:L, 0:L].unsqueeze(1).to_broadcast([L, 4, L]),
               vec4.unsqueeze(2).to_broadcast([L, 4, L]), ALU.mult, [B_idf] + R, [B_diag])

        def bcast_mm():
            for h in range(4):
                mm(pg[0:L, h * 128:h * 128 + L], onesf[0:L, 0:L], diag[0:L, h, 0:L], True, True, [B_ones, B_diag], [B_pg])
            return pg[:, :].rearrange("p (h t) -> p h t", h=4)[0:L, :, 0:L]

        def gates_gen():
            for _ in range(2 if hand is not None else 0):
                yield
            for i in range(ntile):
                ci = cols(i)
                Ex = Ex_t[:, i, :]; BE = B_Ex[i]
                for kc in range(8):
                    mm(pg[0:L, 0:8], hT[:, kc, ci], wif[:, kc, :], kc == 0, kc == 7, [B_hT[i], B_wif], [B_pg])
                yield
                tt("dve", gi[0:L, :], pg[0:L, 0:8], bif[0:L, :], ALU.add, [B_pg, B_sm], [B_g])
                act(lf[0:L, 0:4], gi[0:L, 4:8], AF.Abs, [B_g], [B_g])
                act(lf[0:L, 4:8], lf[0:L, 0:4], AF.Exp, [B_g], [B_g], scale=-1.0)
                act(lf[0:L, 8:12], lf[0:L, 4:8], AF.Ln, [B_g], [B_g], bias=1.0)
                ts("dve", lf[0:L, 0:4], gi[0:L, 4:8], 0.0, None, ALU.min, None, [B_g], [B_g])
                tt("dve", lf[0:L, 12:16], lf[0:L, 0:4], lf[0:L, 8:12], ALU.subtract, [B_g], [B_g])
                yield
                mm(pg[0:L, 16:20], M["cmask"][:, :], lf[0:L, 12:16], True, True, [B_mk, B_g], [B_pg])
                yield
                cp("dve", gb8[0:L, 4:8], pg[0:L, 16:20], [B_pg], [B_g])
                tt("dve", aa[0:L, 0:4], gi[0:L, 0:4], gb8[0:L, 4:8], ALU.subtract, [B_g], [B_g])
                ts("dve", aa[0:L, 4:8], aa[0:L, 0:4], -LN16, None, ALU.add, None, [B_g], [B_g])
                bcast_diag(aa[0:L, 0:4], [B_g])
                yield
                pr = bcast_mm()
                yield
                tt("dve", am[0:L, :, 0:L], pr, M["nm"][:, :].unsqueeze(1).to_broadcast([L, 4, L]), ALU.add, [B_pg, B_mk], [B_am])
                op("dve", lambda e: e.tensor_reduce(out=gb8[0:L, 0:4], in_=am[0:L, :, 0:L], axis=AX.X, op=ALU.max), [B_am], [B_g])
                tt("dve", gb8[0:L, 0:4], gb8[0:L, 0:4], mp[0:L, :], ALU.max, [B_g, Bmp], [B_g])
                yield
                mm(pg[0:L, 0:8], M["sell"][:, :], gb8[0:L, 0:8], True, True, [B_mk, B_g], [B_pg])
                yield
                cp("dve", gl8[0:L, :], pg[0:L, 0:8], [B_pg], [B_g])
                tt("dve", mnd[0:L, 0:4], gl8[0:L, 0:4], gl8[0:L, 4:8], ALU.add, [B_g], [B_g])
                tt("dve", Dx[0:L, 0:4], mp[0:L, :], gb8[0:L, 0:4], ALU.subtract, [B_g, Bmp], [B_g])
                tt("dve", Dx[0:L, 4:8], gb8[0:L, 0:4], gb8[0:L, 4:8], ALU.add, [B_g], [B_g])
                ts("dve", Dx[0:L, 4:8], Dx[0:L, 4:8], -1.0, None, ALU.mult, None, [B_g], [B_g])
                tt("dve", Dx[0:L, 8:12], mp[0:L, :], gl8[0:L, 0:4], ALU.subtract, [B_g, Bmp], [B_g])
                tt("dve", Dx[0:L, 12:16], aa[0:L, 4:8], gl8[0:L, 0:4], ALU.subtract, [B_g], [B_g])
                act(Ex[0:L, :], Dx[0:L, :], AF.Exp, [B_g], [BE])
                cp("dve", mnd[0:L, 4:8], Ex[0:L, 8:12], [BE], [B_g])
                if G == "P":
                    cp("dve", mp[0:L, :], mnd[0:L, 0:4], [B_g], [Bmp])
                bcast_diag(gb8[0:L, 0:4], [B_g])
                yield
                pr = bcast_mm()
                yield
                stt(am[0:L, :, 0:L], pr, -1.0, M["nmT"][:, :].unsqueeze(1).to_broadcast([L, 4, L]), ALU.mult, ALU.add,
                    [B_pg, B_mk], [B_am])
                for h in range(4):
                    act(DTb[0:L, i, h, 0:L], am[0:L, h, 0:L], AF.Exp, [B_am, B_g], [B_DT[i]], bias=aa[0:L, 4 + h:5 + h])
                mm(pg[0:nb, 0:8], M["selc"][:, 0:nb], mnd[0:L, 0:8], True, True, [B_mk, B_g], [B_pg])
                yield
                cp("dve", cm8_t[0:nb, i, :], pg[0:nb, 0:8], [B_pg], [B_cm[i]])
                tt("dve", decx[0:L, 0:nb, :], mnd[0:L, 4:8].unsqueeze(1).to_broadcast([L, nb, 4]),
                   M["selc"][:, 0:nb].unsqueeze(2).to_broadcast([L, nb, 4]), ALU.mult, [B_g, B_mk], [B_decx])
                yield
                mm(pg[:, 0:nb * 4], onesf[0:L, 0:128], decx[0:L, 0:nb, :].rearrange("p b h -> p (b h)"), True, True,
                   [B_ones, B_decx], [B_pg])
                yield
                cp("dve", decb_t[:, i, 0:nb * 4], pg[:, 0:nb * 4], [B_pg], [B_dec[i]])
                tt("dve", wsel_t[0:L, i, :, 0:nb], Ex[0:L, 12:16].unsqueeze(2).to_broadcast([L, 4, nb]),
                   M["blk"][:, 0:nb].unsqueeze(1).to_broadcast([L, 4, nb]), ALU.mult, [BE, B_mk], [B_wsel[i]])
                yield

        bgq = [gates_gen()]
        bgq2 = []
        bgq3 = [hand] if hand is not None else []

        def _adv(q):
            while q:
                try:
                    next(q[0])
                    return
                except StopIteration:
                    q.pop(0)

        tickn = [0]

        def tick():
            _adv(bgq3)
            _adv(bgq)
            tickn[0] += 1
            if tickn[0] % 3 == 0:
                _adv(bgq2)

        def drain():
            while bgq:
                _adv(bgq)

        def drain2():
            while bgq2:
                _adv(bgq2)

        for s in range(2):
            def ev_u(cc, p, Bp, s=s):
                c = s * 4 + cc
                act(U8[:, c, 0:TB], p[:, 0:TB], AF.Gelu, [Bp], [B_U8[c]])
            fm_proj(w0 + s, 4, hT, B_hT[0:ntile], ev_u)
        for s in range(2):
            def ev_v(i, p, Bp, s=s):
                act(gv[0:L, i, s * 512:(s + 1) * 512], p[0:L, :], AF.Gelu, [Bp], B_ar[4 * i:4 * i + 4])
            tm_proj(w0 + 2 + s, hT, lambda i: B_hT[i], ev_v)
        vb_of = {}

        def ln_part(i):
            Bgv = B_ar[4 * i:4 * i + 4]
            for hh in range(2):
                op("dve", lambda e, hh=hh, i=i: e.bn_stats(out=hst[0:L, hh * 6:(hh + 1) * 6], in_=gv[0:L, i, hh * 512:(hh + 1) * 512]),
                   Bgv, [B_hst])
            op("dve", lambda e: e.bn_aggr(out=hst[0:L, 12:14], in_=hst[0:L, 0:12]), [B_hst], [B_hst])
            rsqrt_eps(hst[0:L, 15:16], hst[0:L, 13:14], hst[0:L, 14:15], 1, L, [B_hst])
            ts("dve", vnf[0:L, :], gv[0:L, i, :], hst[0:L, 12:13], hst[0:L, 15:16], ALU.subtract, ALU.mult,
               Bgv + [B_hst], [B_vnf])
            tt("dve", vnf[0:L, :], vnf[0:L, :], bc_t["lng"][0:L, :], ALU.mult, [B_vnf, B_bc["lng"]], [B_vnf])
            vb = nxt("vnb", 2)
            vb_of[i] = vb
            if G == "S":
                tt("dve", vnf[0:L, :], vnf[0:L, :], bc_t["lnb"][0:L, :], ALU.add, [B_vnf, B_bc["lnb"]], [B_vnf])
                k.dma("sp", O["vs"], vnf[0:L, :], reads=[B_vnf], sem=B_vnf)
                cp("pool", vnb[vb][0:L, :], vnf[0:L, :], [B_vnf], [B_vnb[vb]])
            else:
                tt("dve", vnb[vb][0:L, :], vnf[0:L, :], bc_t["lnb"][0:L, :], ALU.add, [B_vnf, B_bc["lnb"]], [B_vnb[vb]])

        def spatial_part(i):
            vb = vb_of[i]
            for half in range(2):
                b = nxt("pj", 4)
                for c4 in range(4):
                    kc = half * 4 + c4
                    g = kc // 2
                    mm(pj[b][:, c4 * 128:c4 * 128 + L], vnb[vb][0:L, kc * 128:(kc + 1) * 128], M["wsT"][:, g, :],
                       True, False, [B_vnb[vb], B_mk], [B_pj[b]])
                    mm(pj[b][:, c4 * 128:c4 * 128 + L], ones2[0:2, 0:128], bs2[G][0:2, g * L:(g + 1) * L],
                       False, True, [B_ones, B_bs], [B_pj[b]])
                tick()
                pv = pj[b][:, :].rearrange("p (c t) -> p c t", c=4)[:, :, 0:L]
                tt("dve", aoT[:, half * 4:half * 4 + 4, cols(i)], pv, U8[:, half * 4:half * 4 + 4, cols(i)], ALU.mult,
                   [B_pj[b]] + B_U8[half * 4:half * 4 + 4], [B_ao[i]])

        special_tm = (G == "S") or last_prompt_blk
        qk_state = {}

        def qk_stageA(c):
            s, cc = divmod(c, 4)
            W, Bw = ws.get(w0 + 4 + s)
            if G == "S" and cc == 0:
                k.dma("sp", scst[0:48, 0:512], D["sconv"][:, s * 512:(s + 1) * 512], writes=[B_scst], sem=B_scst)
            b = nxt("pj", 4)
            for kc in range(8):
                mm(pj[b][:, 0:TB], W[:, kc, cc * 128:(cc + 1) * 128], hT[:, kc, 0:TB], kc == 0, kc == 7,
                   [Bw] + B_hT[0:ntile], [B_pj[b]])
            tick()
            r = nxt("raw", 2)
            if G == "P":
                rv = raw[r][:, 0:515].rearrange("p (b t) -> p b t", b=1)
                cp("pool", raw[r][:, 0:3], halo[:, c, :], [B_halo[c]], [B_raw[r]])
                cp("act", raw[r][:, 3:515], pj[b][:, 0:512], [B_pj[b]], [B_raw[r]])
                cp("pool", halo[:, c, :], raw[r][:, 512:515], [B_raw[r]], [B_halo[c]])
                T = 512
            else:
                rv = raw[r][:, 0:112].rearrange("p (b t) -> p b t", b=16)
                tr(pc[:, 0:48], scst[0:48, cc * 128:(cc + 1) * 128], identf[0:48, 0:48], [B_scst, B_idf], [B_pc])
                cp("dve", rv[:, :, 0:3], pc[:, 0:48].rearrange("p (b j) -> p b j", b=16), [B_pc], [B_raw[r]])
                cp("act", rv[:, :, 3:7], pj[b][:, 0:64].rearrange("p (b t) -> p b t", b=16), [B_pj[b]], [B_raw[r]])
                T = 4
            act(cacc[r][:, 0:TB], pj[b][:, 0:TB], AF.Identity, [B_pj[b], B_sm], [B_cacc[r]],
                bias=convb[:, c:c + 1], scale=convw[:, c, 3:4])
            qk_state[c] = (r, rv, T)

        def qk_stageB(c):
            r, rv, T = qk_state[c]
            ca = cacc[r][:, 0:nb * T].rearrange("p (b t) -> p b t", b=nb)
            for j in range(3):
                stt(ca, rv[:, :, j:j + T], convw[:, c, j:j + 1], ca, ALU.mult, ALU.add,
                    [B_raw[r], B_sm, B_cacc[r]], [B_cacc[r]])
            if c % 2 == 1:
                tick()
            act(qkT[:, c, 0:TB], cacc[r][:, 0:TB], AF.Silu, [B_cacc[r]], [B_ar[c]])

        def qk_special(s):
            W, Bw = ws.get(w0 + 4 + s)
            it = ntile - 1
            b = nxt("pj", 4)
            for kc in range(8):
                mm(pj[b][0:L, :], hT[:, kc, cols(it)], W[:, kc, :], kc == 0, kc == 7, [Bw, B_hT[it]], [B_pj[b]])
            f = nxt("tmpf", 2)
            cp("act", tmpf[f][0:L, :], pj[b][0:L, :], [B_pj[b]], [B_tmpf[f]])
            if G == "P":
                k.dma("sp", O["convp"][:, s * 512:(s + 1) * 512], tmpf[f][125:128, :], reads=[B_tmpf[f]], sem=B_tmpf[f])
            else:
                k.dma("sp", scr[:, s * 512:(s + 1) * 512], tmpf[f][0:64, :], reads=[B_tmpf[f]], writes=[B_scr],
                      sem=B_tmpf[f])

        ln_part(0)
        qk_stageA(0)
        for c in range(16):
            s, cc = divmod(c, 4)
            if cc == 3 and special_tm:
                qk_special(s)
            if c + 1 < 16:
                if cc == 3 and s + 1 < ntile:
                    ln_part(s + 1)
                qk_stageA(c + 1)
            qk_stageB(c)
            if cc == 3 and s < ntile:
                spatial_part(s)
        if G == "S":
            k.dma("sp", O["convs"], scr.rearrange("(b t) c -> b t c", t=4)[:, 1:4, :], reads=[B_scr], sem=B_scr)
        for s in range(2):
            def ev_vm(i, p, Bp, s=s):
                cp("act", vaug[0:L, i, 2 * s:2 * s + 2, 0:256], p[0:L, :].rearrange("p (h d) -> p h d", h=2), [Bp], [B_vaug[i]])
            tm_proj(w0 + 8 + s, hT, lambda i: B_hT[i], ev_vm)
        for s in range(2):
            def ev_o(cc, p, Bp, s=s):
                c = s * 4 + cc
                act(U8[:, c, 0:TB], p[:, 0:TB], AF.Sigmoid, [Bp], [B_U8[c]])
            fm_proj(w0 + 10 + s, 4, hT, B_hT[0:ntile], ev_o)

        while bgq3:
            _adv(bgq3)
        drain()

        def gagb_gen():
            ctr = 0
            for so in range(4):
                W, Bw = ws.get(w0 + 12 + so)
                dst, Bdst, boff = (sga, B_sga, 0) if so < 2 else (sgb, B_sgb, 8)
                for cc in range(4):
                    c = (so % 2) * 4 + cc
                    b = 2 + ctr % 2
                    ctr += 1
                    for kc in range(8):
                        mm(pj[b][:, 0:TB], W[:, kc, cc * 128:(cc + 1) * 128], hT[:, kc, 0:TB], kc == 0, kc == 7,
                           [Bw] + B_hT[0:ntile], [B_pj[b]])
                        if kc == 3:
                            yield
                    yield
                    act(dst[:, c, 0:TB], pj[b][:, 0:TB], AF.Sigmoid, [B_pj[b], B_sm], [Bdst[c]], bias=bgate[:, boff + c:boff + c + 1])

        bgq2.append(gagb_gen())
        nTf = nTf_t

        def hnorm_gen(i, nbuf, Bnb, hsx, Bhs, Ex, BE, ci):
            tt("dve", hsx[0:L, 0:4], hsx[0:L, 0:4], Ex[0:L, 4:8], ALU.max, [Bhs, BE], [Bhs])
            op("dve", lambda e: e.reciprocal(out=hsx[0:L, 4:8], in_=hsx[0:L, 0:4]), [Bhs], [Bhs])
            yield
            for h in range(4):
                op("dve", lambda e, h=h: e.bn_stats(out=hsx[0:L, 8 + 6 * h:14 + 6 * h], in_=nbuf[0:L, h * 256:(h + 1) * 256]),
                   Bnb, [Bhs])
                op("dve", lambda e, h=h: e.bn_aggr(out=hsx[0:L, 32 + 2 * h:34 + 2 * h], in_=hsx[0:L, 8 + 6 * h:14 + 6 * h]),
                   [Bhs], [Bhs])
                if h % 2 == 1:
                    yield
            mv = hsx[0:L, 32:40].rearrange("p (h t) -> p h t", t=2)
            tt("dve", hsx[0:L, 40:44], hsx[0:L, 4:8], hsx[0:L, 4:8], ALU.mult, [Bhs], [Bhs])
            tt("dve", hsx[0:L, 40:44], hsx[0:L, 40:44], mv[:, :, 1], ALU.mult, [Bhs], [Bhs])
            rsqrt_eps(hsx[0:L, 40:44], hsx[0:L, 40:44], hsx[0:L, 40:44], 4, L, [Bhs])
            yield; yield
            tt("dve", hsx[0:L, 44:48], hsx[0:L, 40:44], hsx[0:L, 4:8], ALU.mult, [Bhs], [Bhs])
            stt(hsx[0:L, 0:4], mv[:, :, 0], -1.0, hsx[0:L, 44:48], ALU.mult, ALU.mult, [Bhs], [Bhs])
            yield
            for h in range(4):
                act(hmb[0:L, h * 256:(h + 1) * 256], nbuf[0:L, h * 256:(h + 1) * 256], AF.Identity, Bnb + [Bhs], [B_hmb],
                    bias=hsx[0:L, h:h + 1], scale=hsx[0:L, 44 + h:45 + h])
            yield; yield; yield; yield; yield
            for kc in range(8):
                tr(ptb[:, kc, 0:L], hmb[0:L, kc * 128:(kc + 1) * 128], identb[0:L, 0:L], [B_hmb, B_idb], [B_ptb])
            yield
            for kc in range(8):
                stt(boT[:, kc, ci], ptb[:, kc, 0:L], hng[:, kc:kc + 1], U8[:, kc, ci], ALU.mult, ALU.mult,
                    [B_ptb, B_sm, B_U8[kc]], [B_bo[i]])
            yield
        if G == "S":
            k.dma("sp", vnf[0:16, :], D["sn"], writes=[B_vnf], sem=B_vnf)
            for j in range(8):
                tr(pc[:, j * 16:j * 16 + nb], vnf[0:nb, j * 128:(j + 1) * 128], identf[0:nb, 0:nb], [B_vnf, B_idf], [B_pc])
            cp("dve", nTf[:, :, 0:nb], pc[:, 0:128].rearrange("p (j b) -> p j b", j=8)[:, :, 0:nb], [B_pc], [B_nTf])
        for i in range(ntile):
            ci = cols(i)
            Ex = Ex_t[:, i, :]
            BE = B_Ex[i]
            if i % 2 == 0:
                nbuf = vnf; Bnb = [B_vnf]
            else:
                nbuf = arena[:, 8192:10240].bitcast(F32); Bnb = B_ar[16:20]
            hsx = hs2[i % 2]; Bhs = B_hs2[i % 2]
            tt("dve", diag[0:L, :, 0:L], identf[0:L, 0:L].unsqueeze(1).to_broadcast([L, 4, L]),
               Ex[0:L, 0:4].unsqueeze(2).to_broadcast([L, 4, L]), ALU.mult, [B_idf, BE], [B_diag])
            for h in range(4):
                mm(pc[:, h * 128:h * 128 + L], onesf[0:L, 0:128], diag[0:L, h, 0:L], True, True, [B_ones, B_diag], [B_pc])
            pr = pc[:, :].rearrange("p (h t) -> p h t", h=4)[:, :, 0:L]
            tt("dve", qTs[:, :, 0:L].rearrange("p (h d) t -> p h d t", h=4),
               qkT[:, 0:8, ci].rearrange("p (h d) t -> p h d t", h=4),
               pr.unsqueeze(2).to_broadcast([128, 4, 2, L]), ALU.mult, [B_pc] + B_ar[0:8], [B_qTs])
            for kc in range(8):
                tr(ptb[0:L, kc, :], qkT[:, 8 + kc, ci], identb[:, :], [B_ar[8 + kc], B_idb], [B_ptb])
            cp("act", ktm[0:L, :], ptb[0:L, :, :].rearrange("p c d -> p (c d)"), [B_ptb], [B_ktm])
            ptbf = ptb[:, :, :].rearrange("p a b -> p (a b)").bitcast(F32)
            pCb = [(pg, B_pg), (ptbf, B_ptb)]
            hstate = {}

            def c_src(h, b):
                if G == "P":
                    return Cst[:, h, :, :], B_Cst[h]
                it = h * nb + b
                return Cst[:, it % 5, :, :], B_Cst[it % 5]

            def c_load(it):
                if G == "S" and it < 4 * nb:
                    h_, b_ = divmod(it, nb)
                    Cn_, BCn_ = c_src(h_, b_)
                    k.dma("sp", Cn_, D["sC"][b_, h_].rearrange("(vh p) d -> p vh d", vh=2), writes=[BCn_], sem=BCn_)

            def c_store(it):
                if G == "S" and 0 <= it < 4 * nb:
                    h_, b_ = divmod(it, nb)
                    Cn_, BCn_ = c_src(h_, b_)
                    k.dma("sp", O["Cs"][b_, h_].rearrange("(vh p) d -> p vh d", vh=2), Cn_, reads=[BCn_], sem=BCn_)

            def c_transposes(h, b):
                Cn, BCn = c_src(h, b)
                pcb, Bpcb = pCb[nxt("pC", 2)]
                pC = pcb[:, :].rearrange("p (dh v) -> p dh v", dh=2)
                for vh in range(2):
                    for dh in range(2):
                        tr(pC[:, dh, vh * 128:(vh + 1) * 128], Cn[:, vh, dh * 128:(dh + 1) * 128], identf[:, :],
                           [BCn, B_idf], [Bpcb])
                cb = nxt("CTb", 2)
                cp("act", CTb[cb][:, :, 0:256], pC, [Bpcb], [B_CTb[cb]])
                cp("pool", CTb[cb][:, :, 256:257], nTf[:, 2 * h:2 * h + 2, b:b + 1], [B_nTf], [B_CTb[cb]])
                return cb

            def stage1(h):
                for dh in range(2):
                    mm(pnum[0:L, 0:L], qkT[:, 8 + 2 * h + dh, ci], qkT[:, 2 * h + dh, ci], dh == 0, dh == 1,
                       [B_ar[8 + 2 * h + dh], B_ar[2 * h + dh]], [B_pnum])
                sb_ = nxt("ST", 2)
                tt("dve", STb[sb_][0:L, 0:L], pnum[0:L, 0:L], DTb[0:L, i, h, 0:L], ALU.mult, [B_pnum, B_DT[i]], [B_ST[sb_]])
                vb_ = nxt("vs", 2)
                ts("pool", vsb[vb_][0:L, :], vaug[0:L, i, h, :], Ex[0:L, 12 + h:13 + h], 0.0, ALU.mult, ALU.add,
                   [B_vaug[i], BE], [B_vs[vb_]])
                tick()
                cb0 = c_transposes(h, 0)
                tick()
                hstate[h] = (sb_, vb_, cb0)

            def stage2(h):
                sb_, vb_, cb0 = hstate[h]
                numh = pj[h % 2]; Bnum = B_pj[h % 2]
                mm(numh[0:L, 0:257], STb[sb_][0:L, 0:L], vaug[0:L, i, h, :], True, False, [B_ST[sb_], B_vaug[i]], [Bnum])
                if nb > 1:
                    for dh in range(2):
                        cp("pool", qTm_diag[dh], qTs[:, 2 * h + dh, 0:L].rearrange("p (b j) -> p b j", b=nb),
                           [B_qTs], [B_qTm[dh]])
                cb = cb0
                for b in range(nb):
                    it = h * nb + b
                    Cn, BCn = c_src(h, b)
                    if nb > 1:
                        c_store(it - 2)
                        c_load(it + 2)
                    if nb > 1:
                        vm_ = nxt("vsm", 2)
                        ts("pool", vsm[vm_][0:L, :], vsb[vb_][0:L, 0:256], M["blk"][:, b:b + 1], 0.0, ALU.mult, ALU.add,
                           [B_vs[vb_], B_mk], [B_vsm[vm_]])
                    cb_next = c_transposes(h, b + 1) if b + 1 < nb else None
                    for dh in range(2):
                        if nb == 1:
                            lhs = qTs[:, 2 * h + dh, 0:L]; Rq = [B_qTs]
                        else:
                            lhs = qTm[dh][:, b, 0:L]; Rq = [B_qTm[dh]]
                        mm(numh[0:L, 0:257], lhs, CTb[cb][:, dh, :], False, (b == nb - 1 and dh == 1), Rq + [B_CTb[cb]], [Bnum])
                    for vh in range(2):
                        lv = vsb[vb_][0:L, vh * 128:(vh + 1) * 128] if nb == 1 else vsm[vm_][0:L, vh * 128:(vh + 1) * 128]
                        mm(pc[:, vh * 256:(vh + 1) * 256], lv, ktm[0:L, h * 256:(h + 1) * 256], True, True,
                           [B_vs[vb_] if nb == 1 else B_vsm[vm_], B_ktm], [B_pc])
                    Cflat = Cn.rearrange("p vh d -> p (vh d)")
                    stt(Cflat, Cflat, decb_t[:, i, b * 4 + h:b * 4 + h + 1], pc[:, :], ALU.mult, ALU.add, [BCn, B_dec[i], B_pc], [BCn])
                    cb = cb_next
                    if nb == 1:
                        tick()
                pn = numh[:, 384:416].rearrange("p (dh b) -> p dh b", dh=2)[:, :, 0:nb]
                for dh in range(2):
                    mm(numh[:, 384 + dh * 16:384 + dh * 16 + nb], ktm[0:L, h * 256 + dh * 128:h * 256 + (dh + 1) * 128],
                       wsel_t[0:L, i, h, 0:nb], True, True, [B_ktm, B_wsel[i]], [Bnum])
                nview = nTf[:, 2 * h:2 * h + 2, 0:nb]
                if nb == 1:
                    stt(nview, nview, decb_t[:, i, h:h + 1], pn, ALU.mult, ALU.add, [B_nTf, B_dec[i], Bnum], [B_nTf])
                else:
                    dview = decb_t[:, i, :].rearrange("p (b h) -> p b h", h=4)[:, 0:nb, h]
                    tt("dve", nview, nview, dview.unsqueeze(1).to_broadcast([128, 2, nb]), ALU.mult, [B_nTf, B_dec[i]], [B_nTf])
                    tt("dve", nview, nview, pn, ALU.add, [B_nTf, Bnum], [B_nTf])
                act(hsx[0:L, h:h + 1], numh[0:L, 256:257], AF.Abs, [Bnum], [Bhs])
                cp("act", nbuf[0:L, h * 256:(h + 1) * 256], numh[0:L, 0:256], [Bnum], Bnb)
                tick()

            if G == "S":
                c_load(0); c_load(1)
            stage1(0)
            for h in range(4):
                if nb == 1 and h + 1 < 4:
                    stage1(h + 1)
                stage2(h)
                if nb > 1 and h + 1 < 4:
                    stage1(h + 1)
            if G == "S":
                c_store(4 * nb - 2); c_store(4 * nb - 1)
            bgq.append(hnorm_gen(i, nbuf, Bnb, hsx, Bhs, Ex, BE, ci))
            if i == ntile - 1:
                drain()
            if G == "S" or (last_prompt_blk and i == ntile - 1):
                k.dma("sp", O["ms"] if G == "S" else O["mp"], cm8_t[0:nb, i, 0:4], reads=[B_cm[i]], sem=B_cm[i])
                for half in range(2):
                    pb_ = pg if half == 0 else pc
                    Bpb = B_pg if half == 0 else B_pc
                    for jj in range(4):
                        j = half * 4 + jj
                        tr(pb_[0:nb, jj * 128:(jj + 1) * 128], nTf[:, j, 0:nb], identf[:, :], [B_nTf, B_idf], [Bpb])
                    cp("dve", vnf[0:nb, half * 512:(half + 1) * 512], pb_[0:nb, :], [Bpb], [B_vnf])
                k.dma("sp", O["ns"] if G == "S" else O["np"], vnf[0:nb, :], reads=[B_vnf], sem=B_vnf)
                if G == "P":
                    for h in range(4):
                        k.dma("sp", O["Cp"][h].rearrange("(vh p) d -> p vh d", vh=2), Cst[:, h, :, :], reads=[B_Cst[h]],
                              sem=B_Cst[h])

        if debug and G == "P" and blk == 0:
            k.dma("sp", O["dbg_ao"], aoT[:, :, :], reads=B_ao, sem=B_ao[0])
            k.dma("sp", O["dbg_bo"], boT[:, :, :], reads=B_bo, sem=B_bo[0])
            k.dma("sp", O["dbg_qk"], qkT, reads=B_ar[0:16], sem=B_ar[0])
        drain2()
        for s in range(2):
            Wa, Ba = ws.get(w0 + 16 + 2 * s)
            Wb, Bb = ws.get(w0 + 17 + 2 * s, hold=w0 + 16 + 2 * s)
            for cc in range(4):
                c = s * 4 + cc
                b1 = nxt("pj", 4)
                for kc in range(8):
                    mm(pj[b1][:, 0:TB], Wa[:, kc, cc * 128:(cc + 1) * 128], aoT[:, kc, 0:TB], kc == 0, kc == 7,
                       [Ba] + B_ao[0:ntile], [B_pj[b1]])
                b2 = nxt("pj", 4)
                for kc in range(8):
                    mm(pj[b2][:, 0:TB], Wb[:, kc, cc * 128:(cc + 1) * 128], boT[:, kc, 0:TB], kc == 0, kc == 7,
                       [Bb] + B_bo[0:ntile], [B_pj[b2]])
                f = nxt("tmpf", 2)
                g_ = nxt("tmpg", 2)
                tt("dve", tmpf[f][:, 0:TB], pj[b1][:, 0:TB], sga[:, c, 0:TB], ALU.mult, [B_pj[b1], B_sga[c]], [B_tmpf[f]])
                tt("dve", tmpg[g_][:, 0:TB], pj[b2][:, 0:TB], sgb[:, c, 0:TB], ALU.mult, [B_pj[b2], B_sgb[c]], [B_tmpg[g_]])
                tt("pool", U8[:, c, 0:TB], tmpf[f][:, 0:TB], tmpg[g_][:, 0:TB], ALU.add, [B_tmpf[f], B_tmpg[g_]], [B_U8[c]])
        if debug and G == "P" and blk == 0:
            k.dma("sp", O["dbg_mg"], U8[:, :, :], reads=B_U8, sem=B_U8[0])
        Wo0, Bwo0 = ws.get(w0 + 20)
        Wo1, Bwo1 = ws.get(w0 + 21, hold=w0 + 20)
        for i in range(ntile):
            for s, (W, Bw) in enumerate(((Wo0, Bwo0), (Wo1, Bwo1))):
                b = nxt("pj", 4)
                for kc in range(8):
                    mm(pj[b][0:L, :], U8[:, kc, cols(i)], W[:, kc, :], kc == 0, kc == 7, [Bw, B_U8[kc]], [B_pj[b]])
                    if kc in (3, 7):
                        tick()
                tt("dve", xt[0:L, i, s * 512:(s + 1) * 512], xt[0:L, i, s * 512:(s + 1) * 512], pj[b][0:L, :], ALU.add,
                   [B_xt[i], B_pj[b]], [B_xt[i]])
            bgq.append(rmsnorm_to_hT_g(xt[0:L, i, :], [B_xt[i]], L, i, "g2"))
        drain()
        for s in range(11):
            W, Bw = ws.get(w0 + 22 + s)
            for jj in range(2):
                j = 2 * s + jj
                b1 = nxt("pj", 4)
                for kc in range(8):
                    mm(pj[b1][:, 0:TB], W[:, kc, jj * 128:(jj + 1) * 128], hT[:, kc, 0:TB], kc == 0, kc == 7,
                       [Bw] + B_hT[0:ntile], [B_pj[b1]])
                b2 = nxt("pj", 4)
                for kc in range(8):
                    mm(pj[b2][:, 0:TB], W[:, kc, 256 + jj * 128:256 + (jj + 1) * 128], hT[:, kc, 0:TB], kc == 0, kc == 7,
                       [Bw] + B_hT[0:ntile], [B_pj[b2]])
                f = nxt("tmpf", 2)
                act(tmpf[f][:, 0:TB], pj[b1][:, 0:TB], AF.Silu, [B_pj[b1]], [B_tmpf[f]])
                tt("dve", gT[:, j, 0:TB], tmpf[f][:, 0:TB], pj[b2][:, 0:TB], ALU.mult, [B_tmpf[f], B_pj[b2]], [B_ar[j]])
        if nxt_blk is not None:
            bgq.append(prologue_gen(*nxt_blk))
        for g in range(2):
            for qi, q0 in enumerate((0, 8, 16)):
                nk = min(8, 22 - q0)
                W, Bw = ws.get(w0 + 33 + g * 3 + qi)
                for i in range(ntile):
                    for kk in range(nk):
                        kc = q0 + kk
                        mm(pj[i][0:L, :], gT[:, kc, cols(i)], W[:, kk, :], kc == 0, kc == 21, [Bw, B_ar[kc]], [B_pj[i]])
                    tick()
            if g == 1:
                drain()
            for i in range(ntile):
                tt("dve", xt[0:L, i, g * 512:(g + 1) * 512], xt[0:L, i, g * 512:(g + 1) * 512], pj[i][0:L, :], ALU.add,
                   [B_xt[i], B_pj[i]], [B_xt[i]])
        def final_gen():
            if nxt_blk is not None:
                Ln = LG[nxt_blk[0]]
                ntn = 4 if nxt_blk[0] == "P" else 1
            else:
                Ln = 0; ntn = 0
            for i in range(ntile):
                rms_stats_g(xt[0:L, i, :], [B_xt[i]], L)
                yield
                stt(xt[0:L, i, :], xt[0:L, i, :], st4[0:L, 2:3], bc_t["gf"][0:L, :], ALU.mult, ALU.mult,
                    [B_xt[i], B_st4, B_bc["gf"]], [B_xt[i]])
                k.dma("sp", yout[tok0 + i * L: tok0 + (i + 1) * L, :], xt[0:L, i, :], reads=[B_xt[i]], sem=B_xt[i])
                if i < ntn:
                    k.dma("sp", xt[0:Ln, i, :], xpre[0:Ln, i, :], reads=Bxpre(i), writes=[B_xt[i]], sem=B_xt[i])
                yield

        if nxt_blk is not None and pre_done.get("key") == nxt_blk:
            carry["gen"] = final_gen()
        else:
            for _ in final_gen():
                pass

    B_scr = k.buf("scr")
    for bi, (G_, b_) in enumerate(blocks):
        run_block(G_, b_, blocks[bi + 1] if bi + 1 < len(blocks) else None)
    allb = B_xt + B_Cst + B_cm + [B_vnf, B_scr] + B_tmpf
    k.final_wait("sp", allb)
    k.emit()
    return nc, k


_CACHE = {}


def _layouts(inp):
    f = np.float32
    A = lambda a: np.ascontiguousarray(a, dtype=f)
    w_in = inp["w_in"][0]
    sh = {}
    w_main = np.concatenate([w_in[:, :6144], w_in[:, 6152:8200]], axis=1)
    sh["w_in"] = A(w_main.reshape(8, 128, 16, 512).transpose(2, 1, 0, 3))
    sh["wif"] = A(w_in[:, 6144:6152].reshape(8, 128, 8).transpose(1, 0, 2))
    sh["bif"] = A(np.concatenate([inp["b_i"][0], inp["b_f"][0]]))
    for nm, key in (("g1", "g_norm1"), ("g2", "g_norm2"), ("lng", "ln_g"), ("lnb", "ln_b")):
        sh[nm] = A(inp[key][0])
    sh["gf"] = A(inp["g_final"])
    ws_ = inp["w_s"][0]
    sh["wsTP"] = A(ws_.transpose(2, 0, 1))
    w4 = ws_[:, :4, :4]
    sh["wsTS"] = A(np.tile(w4.transpose(2, 0, 1), (16, 1, 16)))
    bs_ = inp["b_s"][0]
    sh["bsrP"] = A(bs_.reshape(512))
    sh["bsrS"] = A(np.tile(bs_[:, :4], (1, 16)).reshape(256))
    sh["convw"] = A(inp["conv_w"][0].reshape(4, 16, 128).transpose(2, 1, 0))
    sh["convb"] = A(inp["conv_b"][0].reshape(16, 128).T)
    sh["hng"] = A(inp["hn_g"][0].reshape(8, 128).T)
    sh["bgate"] = A(inp["b_gate"][0].reshape(16, 128).T)
    for nm, key in (("wpa", "w_proj_a"), ("wpb", "w_proj_b"), ("wo", "w_out")):
        sh[nm] = A(inp[key][0].reshape(8, 128, 2, 512).transpose(2, 1, 0, 3))
    wfi = inp["w_ffn_in"][0]
    colidx = []
    for s in range(11):
        for base in (0, 2816):
            for jj in range(2):
                j = 2 * s + jj
                colidx.extend(range(base + j * 128, base + (j + 1) * 128))
    sh["wfi"] = A(wfi[:, colidx].reshape(8, 128, 11, 512).transpose(2, 1, 0, 3))
    sh["wfo"] = A(inp["w_ffn_out"][0].reshape(22, 128, 2, 512).transpose(2, 1, 0, 3))
    return sh


def kernel(**inp):
    inp = {kk: np.asarray(v) for kk, v in inp.items()}
    if "nc" not in _CACHE:
        _CACHE["nc"] = build_program()[0]
    nc = _CACHE["nc"]
    shared = _layouts(inp)
    in_maps = []
    for c in range(NCORES):
        m = dict(shared)
        m["xp"] = np.ascontiguousarray(inp["x_prompt"][c], dtype=np.float32)
        sl = slice(16 * c, 16 * c + 16)
        m["xs"] = np.ascontiguousarray(inp["x_sample"][sl].reshape(64, 1024), dtype=np.float32)
        m["sconv"] = np.ascontiguousarray(inp["state_conv"][0, sl].reshape(48, 2048), dtype=np.float32)
        m["sC"] = np.ascontiguousarray(inp["state_C"][0, sl], dtype=np.float32)
        m["sn"] = np.ascontiguousarray(inp["state_n"][0, sl].reshape(16, 1024), dtype=np.float32)
        m["smr"] = np.ascontiguousarray(np.repeat(inp["state_m"][0, sl], 4, axis=0), dtype=np.float32)
        in_maps.append(m)
    res = run_bass_kernel_spmd(nc, in_maps, core_ids=list(range(NCORES)))
    R = res.results
    cat = lambda nm: [np.asarray(r[nm]) for r in R]
    y_prompt = np.stack(cat("yp"), 0)
    y_sample = np.concatenate(cat("ys"), 0).reshape(128, 4, 1024)
    conv_p = np.stack(cat("convp"), 0)[None]
    C_p = np.stack(cat("Cp"), 0)[None]
    n_p = np.stack([a.reshape(4, 256) for a in cat("np")], 0)[None]
    m_p = np.concatenate(cat("mp"), 0)[None]
    conv_s = np.concatenate(cat("convs"), 0)[None]
    C_s = np.concatenate(cat("Cs"), 0)[None]
    n_s = np.concatenate(cat("ns"), 0).reshape(128, 4, 256)[None]
    m_s = np.concatenate(cat("ms"), 0)[None]
    v_s = np.concatenate(cat("vs"), 0).reshape(128, 4, 1024)[None]
    outs = (y_prompt, y_sample, conv_p, C_p, n_p, m_p, conv_s, C_s, n_s, m_s, v_s)
    return tuple(np.ascontiguousarray(o, dtype=np.float32) for o in outs)
```
